# Optimizing a Trainium2 kernel written in Bass

```python
import math
import jax, jax.numpy as jnp
from jax import lax
import numpy as np

D_MODEL = 1024
BATCH = 8
SEQ = 4096
DEPTH = 1

N_MEM = 256
RET_HEADS = 4
RET_DV = D_MODEL // 2 // RET_HEADS
RET_DK = RET_DV // 2
RET_QK_W = RET_HEADS * RET_DK
RET_V_W = RET_HEADS * RET_DV
DIFF_HEADS = 4
DIFF_DH = D_MODEL // 2 // DIFF_HEADS // 2
DIFF_DV = 2 * DIFF_DH
DIFF_QK_W = DIFF_HEADS * 2 * DIFF_DH
DIFF_V_W = DIFF_HEADS * DIFF_DV
MIX_W = RET_V_W + DIFF_V_W
IN_COLS = 2 * RET_QK_W + 2 * RET_V_W + 2 * DIFF_QK_W + DIFF_V_W
XATTN_HEADS = 4
XATTN_DH = D_MODEL // XATTN_HEADS
D_FF = ((8 * D_MODEL // 3 + 127) // 128) * 128
RET_CHUNK = 128
Q_BLOCK = 128
ROPE_BASE = 10000.0
EPS = 1e-6

kernel_name = "hymba_retnet_diffattn_macaron"


def rmsnorm(x, g):
    xf = x.astype(jnp.float32)
    y = xf * lax.rsqrt(jnp.mean(xf * xf, axis=-1, keepdims=True) + EPS)
    return (y * g.astype(jnp.float32)).astype(x.dtype)


def swiglu(h, w_gate, w_up, w_down):
    return (jax.nn.silu(h @ w_gate) * (h @ w_up)) @ w_down


def rotary(t, pos):
    d = t.shape[-1]
    inv = 1.0 / (ROPE_BASE ** (jnp.arange(0, d, 2, dtype=jnp.float32) / d))
    ang = pos.astype(jnp.float32)[:, None] * inv[None, :]
    cos = jnp.cos(ang)[None, :, None, :].astype(t.dtype)
    sin = jnp.sin(ang)[None, :, None, :].astype(t.dtype)
    t1, t2 = t[..., : d // 2], t[..., d // 2:]
    return jnp.concatenate([t1 * cos - t2 * sin, t1 * sin + t2 * cos], axis=-1)


def retention_chunkwise(q, k, v):
    B, S, H, dk = q.shape
    dv = v.shape[-1]
    C = RET_CHUNK
    NC = S // C
    log_g = jnp.log(1.0 - 2.0 ** (-5.0 - jnp.arange(H, dtype=jnp.float32)))
    n = jnp.arange(C, dtype=jnp.float32)
    rel = n[:, None] - n[None, :]
    decay = jnp.where(rel[None] >= 0, jnp.exp(jnp.maximum(rel, 0.0)[None] * log_g[:, None, None]), 0.0)
    xi = jnp.exp((n[None, :] + 1.0) * log_g[:, None])
    zeta = jnp.exp((C - 1.0 - n[None, :]) * log_g[:, None])
    g_chunk = jnp.exp(C * log_g)

    def to_chunks(t):
        return t.astype(jnp.float32).reshape(B, NC, C, H, t.shape[-1]).transpose(1, 0, 3, 2, 4)

    qc, kc, vc = to_chunks(q), to_chunks(k), to_chunks(v)

    def step(R, inp):
        qi, ki, vi = inp
        inner = jnp.einsum('bhnd,bhmd->bhnm', qi, ki) * decay[None]
        o = jnp.einsum('bhnm,bhme->bhne', inner, vi)
        o = o + jnp.einsum('bhnd,bhde->bhne', qi, R) * xi[None, :, :, None]
        R = g_chunk[None, :, None, None] * R + jnp.einsum('bhmd,bhme->bhde', ki * zeta[None, :, :, None], vi)
        return R, o

    R0 = jnp.zeros((B, H, dk, dv), jnp.float32)
    _, out = lax.scan(step, R0, (qc, kc, vc))
    return out.transpose(1, 0, 3, 2, 4).reshape(B, S, H, dv)


def diff_attention(q, k, v, lam):
    B, S, H, _, dh = q.shape
    NB = S // Q_BLOCK
    scale = 1.0 / math.sqrt(dh)
    qb = q.reshape(B, NB, Q_BLOCK, H, 2, dh).transpose(1, 0, 2, 3, 4, 5)
    kpos = jnp.arange(S)

    def block(args):
        qi, i = args
        s = jnp.einsum('bqhcd,bkhcd->bhcqk', qi, k).astype(jnp.float32) * scale
        qpos = i * Q_BLOCK + jnp.arange(Q_BLOCK)
        mask = kpos[None, :] <= qpos[:, None]
        s = jnp.where(mask[None, None, None], s, -jnp.inf)
        p = jax.nn.softmax(s, axis=-1)
        a = p[:, :, 0] - lam * p[:, :, 1]
        return jnp.einsum('bhqk,bkhe->bqhe', a.astype(v.dtype), v)

    out = lax.map(block, (qb, jnp.arange(NB)))
    return out.transpose(1, 0, 2, 3, 4).reshape(B, S, H, v.shape[-1])


def memory_cross_attention(h, m, wq, wkv, wo):
    B, S, _ = h.shape
    M = m.shape[1]
    q = (h @ wq).reshape(B, S, XATTN_HEADS, XATTN_DH)
    kv = m @ wkv
    k = kv[..., :D_MODEL].reshape(B, M, XATTN_HEADS, XATTN_DH)
    v = kv[..., D_MODEL:].reshape(B, M, XATTN_HEADS, XATTN_DH)
    s = jnp.einsum('bshd,bmhd->bhsm', q, k).astype(jnp.float32) / math.sqrt(XATTN_DH)
    p = jax.nn.softmax(s, axis=-1).astype(v.dtype)
    o = jnp.einsum('bhsm,bmhd->bshd', p, v).reshape(B, S, D_MODEL)
    return o @ wo


def setup_inputs(seed: int = 0) -> dict:
    key = jax.random.key(seed)
    ks = jax.random.split(key, 32)
    f32 = jnp.float32

    def w(k, shape, fan_in):
        return jax.random.normal(k, shape, f32) * (fan_in ** -0.5)

    def gain(k, shape):
        return 1.0 + 0.02 * jax.random.normal(k, shape, f32)

    L = DEPTH
    return {
        "x": jax.random.normal(ks[0], (BATCH, SEQ, D_MODEL), f32),
        "mem": jax.random.normal(ks[1], (BATCH, N_MEM, D_MODEL), f32),
        "ffn1_norm": gain(ks[2], (L, D_MODEL)),
        "ffn1_w_gate": w(ks[3], (L, D_MODEL, D_FF), D_MODEL),
        "ffn1_w_up": w(ks[4], (L, D_MODEL, D_FF), D_MODEL),
        "ffn1_w_down": w(ks[5], (L, D_FF, D_MODEL), D_FF),
        "mix_norm": gain(ks[6], (L, D_MODEL)),
        "w_in": w(ks[7], (L, D_MODEL, IN_COLS), D_MODEL),
        "ret_out_norm": gain(ks[8], (L, RET_HEADS, RET_DV)),
        "diff_lq1": 0.1 * jax.random.normal(ks[9], (L, DIFF_DH), f32),
        "diff_lk1": 0.1 * jax.random.normal(ks[10], (L, DIFF_DH), f32),
        "diff_lq2": 0.1 * jax.random.normal(ks[11], (L, DIFF_DH), f32),
        "diff_lk2": 0.1 * jax.random.normal(ks[12], (L, DIFF_DH), f32),
        "diff_out_norm": gain(ks[13], (L, DIFF_HEADS, DIFF_DV)),
        "w_out": w(ks[14], (L, MIX_W, D_MODEL), MIX_W),
        "xattn_norm": gain(ks[15], (L, D_MODEL)),
        "mem_norm": gain(ks[16], (L, D_MODEL)),
        "xattn_wq": w(ks[17], (L, D_MODEL, D_MODEL), D_MODEL),
        "xattn_wkv": w(ks[18], (L, D_MODEL, 2 * D_MODEL), D_MODEL),
        "xattn_wo": w(ks[19], (L, D_MODEL, D_MODEL), D_MODEL),
        "ffn2_norm": gain(ks[20], (L, D_MODEL)),
        "ffn2_w_gate": w(ks[21], (L, D_MODEL, D_FF), D_MODEL),
        "ffn2_w_up": w(ks[22], (L, D_MODEL, D_FF), D_MODEL),
        "ffn2_w_down": w(ks[23], (L, D_FF, D_MODEL), D_FF),
        "final_norm": gain(ks[24], (D_MODEL,)),
    }


def reference(x, mem, ffn1_norm, ffn1_w_gate, ffn1_w_up, ffn1_w_down, mix_norm, w_in,
              ret_out_norm, diff_lq1, diff_lk1, diff_lq2, diff_lk2, diff_out_norm, w_out,
              xattn_norm, mem_norm, xattn_wq, xattn_wkv, xattn_wo,
              ffn2_norm, ffn2_w_gate, ffn2_w_up, ffn2_w_down, final_norm):
    B, S, _ = x.shape
    pos = jnp.arange(S)
    split_at = np.cumsum([RET_QK_W, RET_QK_W, RET_V_W, RET_V_W, DIFF_QK_W, DIFF_QK_W]).tolist()
    for l in range(DEPTH):
        x = x + 0.5 * swiglu(rmsnorm(x, ffn1_norm[l]), ffn1_w_gate[l], ffn1_w_up[l], ffn1_w_down[l])

        h = rmsnorm(x, mix_norm[l])
        p = h @ w_in[l]
        rq, rk, rv, rg, dq, dk, dv = jnp.split(p, split_at, axis=-1)

        rq = rotary(rq.reshape(B, S, RET_HEADS, RET_DK), pos)
        rk = rotary(rk.reshape(B, S, RET_HEADS, RET_DK), pos) * (RET_DK ** -0.5)
        rv = rv.reshape(B, S, RET_HEADS, RET_DV)
        y_ret = retention_chunkwise(rq, rk, rv).astype(x.dtype)
        y_ret = rmsnorm(y_ret, ret_out_norm[l]).reshape(B, S, RET_V_W) * jax.nn.silu(rg)

        lambda_init = 0.8 - 0.6 * math.exp(-0.3 * l)
        lam = (jnp.exp(jnp.sum(diff_lq1[l].astype(jnp.float32) * diff_lk1[l].astype(jnp.float32)))
               - jnp.exp(jnp.sum(diff_lq2[l].astype(jnp.float32) * diff_lk2[l].astype(jnp.float32)))
               + lambda_init)
        dq = dq.reshape(B, S, DIFF_HEADS, 2, DIFF_DH)
        dk = dk.reshape(B, S, DIFF_HEADS, 2, DIFF_DH)
        dv = dv.reshape(B, S, DIFF_HEADS, DIFF_DV)
        y_diff = diff_attention(dq, dk, dv, lam)
        y_diff = (rmsnorm(y_diff, diff_out_norm[l]) * (1.0 - lambda_init)).reshape(B, S, DIFF_V_W)

        x = x + jnp.concatenate([y_ret, y_diff], axis=-1) @ w_out[l]

        x = x + memory_cross_attention(rmsnorm(x, xattn_norm[l]), rmsnorm(mem, mem_norm[l]),
                                       xattn_wq[l], xattn_wkv[l], xattn_wo[l])

        x = x + 0.5 * swiglu(rmsnorm(x, ffn2_norm[l]), ffn2_w_gate[l], ffn2_w_up[l], ffn2_w_down[l])
    return rmsnorm(x, final_norm)
```

```python
import contextlib
import os
import math
import numpy as np
import ml_dtypes
import concourse.bass as bass
import concourse.mybir as mybir
from concourse.bass_utils import run_bass_kernel_spmd

F32 = mybir.dt.float32
BF16 = mybir.dt.bfloat16
ALU = mybir.AluOpType
AF = mybir.ActivationFunctionType

D = 1024
SEQ = 4096
NMEM = 256
DFF = 2816
NF = 22
EPS = 1e-6
TT = 512
NSUB = 4
RING = 4
POOL_ELT = os.environ.get("POOL_ELT", "dve")
NPRE = int(os.environ.get("NPRE", "4"))
SLOT = 4096


class _Op:
    __slots__ = ("eng", "fn", "waits", "flag", "idx", "dma_grp", "dma_cnt", "epoch", "count", "vc")


class Sched:
    ENGS = ("pe", "act", "dve", "pool", "sp")

    def __init__(self):
        self.ops = {e: [] for e in self.ENGS}
        self.state = {}
        self.vc = {e: {} for e in self.ENGS}
        self.dma_cnt = {}
        self.dma_vc = {}
        self.epoch = 0
        self.final_waits = []
        self.alias = {}

    @staticmethod
    def _join(a, b):
        for k, v in b.items():
            if a.get(k, -1) < v:
                a[k] = v

    def add(self, eng, fn, reads=(), writes=(), dma=None):
        op = _Op()
        op.eng = eng
        op.fn = fn
        op.waits = []
        op.flag = False
        op.idx = len(self.ops[eng])
        op.dma_grp = dma
        op.dma_cnt = 0
        op.epoch = self.epoch
        op.count = 0
        if dma is not None:
            self.dma_cnt[dma] = self.dma_cnt.get(dma, 0) + 16
            op.dma_cnt = self.dma_cnt[dma]
            ref = ("dma", dma, op.dma_cnt)
            rkey = "dma:" + dma
        else:
            ref = ("eng", eng, op.idx, op)
            rkey = eng
        if self.alias:
            w2 = list(writes)
            for k in writes:
                if k in self.alias:
                    w2.extend(self.alias[k])
            writes = w2
        for k in reads:
            st = self.state.get(k)
            if st is not None and st[0] is not None:
                self._need(op, st[0])
        for k in writes:
            st = self.state.get(k)
            if st is not None:
                if st[0] is not None:
                    self._need(op, st[0])
                for r in st[1].values():
                    self._need(op, r)
        for k in reads:
            st = self.state.get(k)
            if st is None:
                st = [None, {}]
                self.state[k] = st
            st[1][rkey] = ref
        for k in writes:
            self.state[k] = [ref, {}]
        vc = dict(self.vc[eng])
        if dma is not None:
            prev = self.dma_vc.get((dma, op.dma_cnt - 16))
            if prev is not None:
                self._join(vc, prev)
            vc["dma:" + dma] = op.dma_cnt
            self.dma_vc[(dma, op.dma_cnt)] = vc
        else:
            vc[eng] = op.idx
        op.vc = vc
        self.ops[eng].append(op)
        return op

    def _need(self, op, d):
        evc = self.vc[op.eng]
        if d[0] == "eng":
            X, idx, dop = d[1], d[2], d[3]
            if X == "pe" and op.eng == "pe" and op.dma_grp is None:
                return
            if evc.get(X, -1) >= idx:
                return
            dop.flag = True
            op.waits.append(("eng", X, dop))
            self._join(evc, dop.vc)
        else:
            G, cnt = d[1], d[2]
            key = "dma:" + G
            if evc.get(key, -1) >= cnt:
                return
            op.waits.append(("dma", G, cnt))
            self._join(evc, self.dma_vc[(G, cnt)])

    def seal(self, grp):
        tot = self.dma_cnt[grp]
        for k, stt_ in self.state.items():
            w = stt_[0]
            if w is not None and w[0] == "dma" and w[1] == grp:
                stt_[0] = ("dma", grp, tot)

    def wait_final(self, dma_grp):
        self.final_waits.append(dma_grp)

    def emit(self, nc):
        used = set()
        for e in self.ENGS:
            cnts = {}
            for op in self.ops[e]:
                if op.flag and op.dma_grp is None:
                    cnts[op.epoch] = cnts.get(op.epoch, 0) + 1
                    op.count = cnts[op.epoch]
                    used.add((e, op.epoch))
        with contextlib.ExitStack() as es:
            esem = {}
            for (e, ep) in sorted(used):
                esem[(e, ep)] = es.enter_context(nc.semaphore(f"s_{e}_{ep}"))
            dsem = {}
            for g in sorted(self.dma_cnt):
                dsem[g] = es.enter_context(nc.semaphore(f"d_{g}"))
            block = es.enter_context(nc.Block())

            def run(e, name):
                for op in self.ops[name]:
                    for w in op.waits:
                        if w[0] == "eng":
                            dop = w[2]
                            e.wait_ge(esem[(w[1], dop.epoch)], dop.count)
                        else:
                            e.wait_ge(dsem[w[1]], w[2])
                    ins = op.fn(e)
                    if op.dma_grp is not None:
                        ins.then_inc(dsem[op.dma_grp], 16)
                    elif op.flag:
                        ins.then_inc(esem[(name, op.epoch)], 1)
                if name == "sp":
                    for g in self.final_waits:
                        e.wait_ge(dsem[g], self.dma_cnt[g])

            @block.tensor
            def _(e):
                run(e, "pe")

            @block.scalar
            def _(e):
                run(e, "act")

            @block.vector
            def _(e):
                run(e, "dve")

            @block.gpsimd
            def _(e):
                run(e, "pool")

            @block.sync
            def _(e):
                run(e, "sp")


def _const_tables():
    f32 = np.float32
    c = {}
    c["identF"] = np.eye(128, dtype=f32)
    c["identB"] = np.eye(128, dtype=f32).astype(ml_dtypes.bfloat16)
    inv = (1.0 / (f32(10000.0) ** (np.arange(0, 64, 2, dtype=f32) / f32(64)))).astype(f32)
    pos = np.arange(SEQ, dtype=f32)
    ang = (pos[:, None] * inv[None, :]).astype(f32)
    cos = np.cos(ang).astype(f32)
    sin = np.sin(ang).astype(f32)
    p = np.arange(128)
    cs = np.zeros((128, 2, SEQ), f32)
    cs[:, 0, :] = cos[:, p % 32].T
    sgn = np.where((p % 64) < 32, -1.0, 1.0).astype(f32)
    cs[:, 1, :] = sin[:, p % 32].T * sgn[:, None]
    c["cs"] = cs
    H = 4
    log_g = np.log(1.0 - 2.0 ** (-5.0 - np.arange(H, dtype=np.float64)))
    n = np.arange(128, dtype=np.float64)
    rel = n[None, :] - n[:, None]
    decT = np.zeros((128, 2, 2, 128), f32)
    for h in range(H):
        decT[:, h % 2, h // 2, :] = np.where(rel >= 0, np.exp(np.maximum(rel, 0) * log_g[h]), 0.0) * 0.125
    c["decT"] = decT
    xi = np.exp((n[None, :] + 1.0) * log_g[:, None])
    zeta = np.exp((127.0 - n[None, :]) * log_g[:, None])
    gch = np.exp(128.0 * log_g)
    xit = np.zeros((128, 2, 128), f32)
    gdec = np.zeros((128, 2), f32)
    for hp in range(2):
        for hh in range(2):
            xit[hh * 64:(hh + 1) * 64, hp, :] = xi[2 * hp + hh][None, :] * 0.125
            gdec[hh * 64:(hh + 1) * 64, hp] = gch[2 * hp + hh]
    c["xit"] = xit
    c["gdec"] = gdec
    zt = np.zeros((128, 256), f32)
    for h in range(H):
        zt[:, h * 64:(h + 1) * 64] = zeta[h][:, None]
    c["zt"] = zt
    k = np.arange(128)
    c["mask"] = (k[:, None] <= k[None, :]).astype(f32).astype(ml_dtypes.bfloat16)
    return c


W_NAMES = ["ffn1_w_gate", "ffn1_w_up", "ffn1_w_down", "w_in", "w_out", "xattn_wq", "xattn_wkv",
           "xattn_wo", "ffn2_w_gate", "ffn2_w_up", "ffn2_w_down"]
V_NAMES = ["ffn1_norm", "mix_norm", "xattn_norm", "ffn2_norm", "mem_norm", "final_norm",
           "ret_out_norm", "diff_out_norm", "diff_lq1", "diff_lk1", "diff_lq2", "diff_lk2"]


def build(NT=8, dbg=False, stage=9):
    nc = bass.Bass("TRN2", target_bir_lowering=False)
    S = Sched()
    ntok = NT * TT

    def din(name, shape, dt=F32):
        return nc.dram_tensor(name, list(shape), dt, kind="ExternalInput").ap()

    X = din("x", [SEQ, D])
    MEM = din("mem", [NMEM, D])
    Wd_ = {}
    Wd_["ffn1_w_gate"] = din("ffn1_w_gate", [D, DFF])
    Wd_["ffn1_w_up"] = din("ffn1_w_up", [D, DFF])
    Wd_["ffn1_w_down"] = din("ffn1_w_down", [DFF, D])
    Wd_["w_in"] = din("w_in", [D, 3072])
    Wd_["w_out"] = din("w_out", [D, D])
    Wd_["xattn_wq"] = din("xattn_wq", [D, D])
    Wd_["xattn_wkv"] = din("xattn_wkv", [D, 2 * D])
    Wd_["xattn_wo"] = din("xattn_wo", [D, D])
    Wd_["ffn2_w_gate"] = din("ffn2_w_gate", [D, DFF])
    Wd_["ffn2_w_up"] = din("ffn2_w_up", [D, DFF])
    Wd_["ffn2_w_down"] = din("ffn2_w_down", [DFF, D])
    Vd = {}
    for nm in ["ffn1_norm", "mix_norm", "xattn_norm", "ffn2_norm", "mem_norm", "final_norm"]:
        Vd[nm] = din(nm, [1, D])
    Vd["ret_out_norm"] = din("ret_out_norm", [1, 512])
    Vd["diff_out_norm"] = din("diff_out_norm", [1, 512])
    for nm in ["diff_lq1", "diff_lk1", "diff_lq2", "diff_lk2"]:
        Vd[nm] = din(nm, [1, 64])
    C_identF = din("identF", [128, 128])
    C_identB = din("identB", [128, 128], BF16)
    C_cs = din("cs", [128, 2, SEQ])
    C_decT = din("decT", [128, 2, 2, 128])
    C_xit = din("xit", [128, 2, 128])
    C_gdec = din("gdec", [128, 2])
    C_zt = din("zt", [128, 256])
    C_mask = din("mask", [128, 128], BF16)
    OUT = nc.dram_tensor("out", [SEQ, D], F32, kind="ExternalOutput").ap()
    if dbg:
        DBG = nc.dram_tensor("dbg", [8, 512, D], F32, kind="ExternalOutput").ap()

    def dscratch(name, nu):
        return nc.dram_tensor(name, [nu, 128, SLOT], BF16, kind="Internal").ap()

    S_gu = [dscratch("s_gu1", 11), dscratch("s_gu2", 11)]
    S_dn = [dscratch("s_d1", 6), dscratch("s_d2", 6)]
    S_in = dscratch("s_in", 7)
    S_out = dscratch("s_out", 2)
    S_wq = dscratch("s_wq", 2)
    S_wo = dscratch("s_wo", 2)
    S_kv = dscratch("s_kv", 4)

    with contextlib.ExitStack() as es:
        def sb(name, shape, dt):
            return es.enter_context(nc.sbuf_tensor(name, list(shape), dt))

        x_sb = sb("x_sb", [128, NSUB, D], F32)
        Kc = sb("Kc", [128, 4, SEQ], BF16)
        Vc = sb("Vc", [128, SEQ // 128, 4, 130], BF16)
        wring = sb("wring", [128, RING, SLOT], BF16)
        tokb = sb("tokb", [128, NSUB, D], BF16)
        hT = sb("hT", [128, 8, TT], BF16)
        arenaA = sb("arenaA", [128, 5632], F32)
        hn = arenaA[:, 0:4096].rearrange("p (s d) -> p s d", s=NSUB)
        aA_bf = arenaA[:].bitcast(BF16)
        actT = aA_bf.rearrange("p (f t) -> p f t", f=NF)
        srg = arenaA[:, 0:2048].rearrange("p (s d) -> p s d", s=NSUB)
        rt1 = arenaA[:, 2048:3072].rearrange("p (b t) -> p b t", b=2)
        rt2 = arenaA[:, 3072:4096].rearrange("p (b t) -> p b t", b=2)
        dqT = aA_bf[:, 8192:10240].rearrange("p (h t) -> p h t", h=4)
        qT = aA_bf[:, 0:4096].rearrange("p (c t) -> p c t", c=8)
        PTx = aA_bf[:, 4096:5120].rearrange("p (m t) -> p m t", m=2)
        PTx2 = aA_bf[:, 5120:6144].rearrange("p (m t) -> p m t", m=2)
        sgpt = sb("sgpt", [128, 2 * TT], F32)
        sg = sgpt[:].rearrange("p (b t) -> p b t", b=2)
        rv_sb = sb("rv_sb", [128, NSUB, 512], BF16)
        arenaB = sb("arenaB", [128, 4096], F32)
        aB_bf = arenaB[:].bitcast(BF16)
        rqT = aB_bf[:, 0:1024].rearrange("p (c t) -> p c t", c=2)
        rkT = aB_bf[:, 1024:2048].rearrange("p (c t) -> p c t", c=2)
        qxT = aB_bf[:, 2048:3072].rearrange("p (c t) -> p c t", c=2)
        kz = aB_bf[:, 3072:4096].rearrange("p (c n) -> p c n", c=NSUB)
        inT = aB_bf[:, 4096:5120].rearrange("p (a b c n) -> p a b c n", a=2, b=2, c=2)
        Rb = aB_bf[:, 5120:6144].rearrange("p (c h e) -> p c h e", c=4, h=2)
        a0 = arenaB[:, 3072:3584].rearrange("p (b e) -> p b e", b=4)
        ad = arenaB[:, 3584:4096].rearrange("p (b e) -> p b e", b=4)
        hnB = arenaB[:].rearrange("p (s d) -> p s d", s=NSUB)
        PT = sgpt[:].bitcast(BF16).rearrange("p (b t) -> p b t", b=4)
        dqz1 = sb("dqz1", [128, 4, TT], BF16)
        Rm = sb("Rm", [128, 2, 128], F32)
        junk = sb("junk", [128, 4, 128], BF16)
        cs_sb = sb("cs_sb", [128, 2, TT], F32)
        tokb_flat = tokb[:].rearrange("p s d -> p (s d)")
        xn = [tokb_flat[:, 0:2048].bitcast(F32), tokb_flat[:, 2048:4096].bitcast(F32),
              rv_sb[:].rearrange("p s d -> p (s d)").bitcast(F32), cs_sb[:].rearrange("p a t -> p (a t)")]
        identF = sb("identF_sb", [128, 128], F32)
        identB = sb("identB_sb", [128, 128], BF16)
        decT = sb("decT_sb", [128, 2, 2, 128], F32)
        xit = sb("xit_sb", [128, 2, 128], F32)
        gdec = sb("gdec_sb", [128, 2], F32)
        zt = sb("zt_sb", [128, 256], F32)
        mask = sb("mask_sb", [128, 128], BF16)
        gret = sb("gret", [128, 512], F32)
        gdiff = sb("gdiff", [128, 512], F32)
        gfin = sb("gfin", [128, D], F32)
        gcol = sb("gcol", [128, 5, 8], F32)
        memKT = sb("memKT", [128, 8, NMEM], BF16)
        memV = sb("memV", [128, 2, 4, 258], BF16)
        lqk = sb("lqk", [128, 4, 64], F32)
        st = sb("st", [128, 64], F32)
        junkR = lqk[:].rearrange("p a b -> p (a b)").bitcast(BF16).rearrange("p (h e) -> p h e", h=4)
        eps_t = sb("eps_t", [128, 1], F32)
        neglam = sb("neglam", [128, 1], F32)
        ps = [es.enter_context(nc.psum_tensor(f"ps{i}", [128, 512], F32)) for i in range(4)]
        pp = [es.enter_context(nc.psum_tensor(f"pp{i}", [128, 1024], F32)) for i in range(2)]
        ps = ps + [pp[0][:, 0:512], pp[0][:, 512:1024], pp[1][:, 0:512], pp[1][:, 512:1024]]

        ss = st[:, 0:4]
        rstd = st[:, 4:8]
        sso = st[:, 8:12]
        rso = st[:, 12:16]
        rcp = st[:, 16:24]
        lam_s = st[:, 24:28]
        ssoR = st[:, 28:32]
        rsoR = st[:, 32:36]

        hn_keys = [("hn", s) for s in range(NSUB)]
        act_keys = [("actT", f) for f in range(NF)]
        mixA_keys = ["srg", ("rt1", 0), ("rt1", 1), ("rt2", 0), ("rt2", 1), "dqT"]
        xatA_keys = ["qT", "PTx", "PTx2"]
        fams = [hn_keys, act_keys, mixA_keys, xatA_keys]
        for fam in fams:
            others = [k2 for f2 in fams if f2 is not fam for k2 in f2]
            for k in fam:
                S.alias[k] = others
        for h_ in range(4):
            S.alias[("junkR", h_)] = ["lqk"]
        hnB_keys = [("hnB", s_) for s_ in range(NSUB)]
        tmpB_keys = ([("rqT", i) for i in range(2)] + [("rkT", i) for i in range(2)] + [("qxT", i) for i in range(2)]
                     + [("kz", i) for i in range(4)] + [("inT", i, j) for i in range(2) for j in range(2)]
                     + [("a0", i) for i in range(4)] + [("ad", i) for i in range(4)] + [("Rb", i) for i in range(4)])
        for k in hnB_keys:
            S.alias[k] = tmpB_keys
        for k in tmpB_keys:
            S.alias[k] = hnB_keys
        XN = [("xn", s_) for s_ in range(NSUB)]
        xn_al = {0: [("tokb", 0, 0), ("tokb", 0, 1), ("tokb", 1, 0), ("tokb", 1, 1)],
                 1: [("tokb", 2, 0), ("tokb", 2, 1), ("tokb", 3, 0), ("tokb", 3, 1)],
                 2: [("rv", i) for i in range(4)], 3: ["cs"]}
        for i_, ks in xn_al.items():
            S.alias[XN[i_]] = ks
            for k in ks:
                S.alias[k] = [XN[i_]]
        sg_keys = [("sg", i) for i in range(2)]
        pt_keys = [("PT", i) for i in range(4)]
        for k in sg_keys:
            S.alias[k] = pt_keys
        for k in pt_keys:
            S.alias[k] = sg_keys

        PSK = [("ps", i) for i in range(8)]

        def dma(eng, out, in_, reads, writes, grp, slow=False):
            if slow:
                S.add(eng, lambda e: e.dma_start(out=out, in_=in_, allow_slow_non_contiguous=True),
                      reads=reads, writes=writes, dma=grp)
            else:
                S.add(eng, lambda e: e.dma_start(out=out, in_=in_), reads=reads, writes=writes, dma=grp)

        def mm(out, lhsT, rhs, start, stop, reads, writes, skip=False):
            S.add("pe", lambda e: e.matmul(out, lhsT=lhsT, rhs=rhs, start=start, stop=stop,
                                           skip_group_check=skip), reads=reads, writes=writes)

        def tr(out, in_, ident, reads, writes):
            S.add("pe", lambda e: e.transpose(out=out, in_=in_, identity=ident), reads=reads, writes=writes)

        def act(out, in_, func, reads, writes, scale=1.0, bias=None, accum=None):
            def fn(e):
                kw = {}
                if bias is not None:
                    kw["bias"] = bias
                if accum is not None:
                    kw["accum_out"] = accum
                return e.activation(out=out, in_=in_, func=func, scale=scale, **kw)
            S.add("act", fn, reads=reads, writes=writes)

        def ts(eng, out, in0, s1, s2, op0, op1, reads, writes):
            if s2 is None:
                S.add(eng, lambda e: e.tensor_scalar(out=out, in0=in0, scalar1=s1, scalar2=None, op0=op0),
                      reads=reads, writes=writes)
            else:
                S.add(eng, lambda e: e.tensor_scalar(out=out, in0=in0, scalar1=s1, scalar2=s2, op0=op0, op1=op1),
                      reads=reads, writes=writes)

        def tt(eng, out, in0, in1, op, reads, writes):
            S.add(eng, lambda e: e.tensor_tensor(out=out, in0=in0, in1=in1, op=op), reads=reads, writes=writes)

        def stt(eng, out, in0, scalar, in1, op0, op1, reads, writes):
            S.add(eng, lambda e: e.scalar_tensor_tensor(out=out, in0=in0, scalar=scalar, in1=in1, op0=op0, op1=op1),
                  reads=reads, writes=writes)

        def cp(eng, out, in_, reads, writes):
            if eng == "act":
                S.add("act", lambda e: e.copy(out=out, in_=in_), reads=reads, writes=writes)
            else:
                S.add(eng, lambda e: e.tensor_copy(out=out, in_=in_), reads=reads, writes=writes)

        evac_rr = [0]

        def evac_eng():
            evac_rr[0] += 1
            return "dve" if evac_rr[0] % 2 else "act"

        def unit(sc, ui, n):
            return (sc[ui, :, 0:n], n, ("S", sc.tensor.name, ui))

        U_kv = [unit(S_kv, i, 4096) for i in range(4)]

        def units_tile():
            seq = []
            for i in range(11):
                seq.append(("gu1", i, unit(S_gu[0], i, 4096)))
            dn_nf = [8, 8, 6, 8, 8, 6]
            for i in range(6):
                seq.append(("d1", i, unit(S_dn[0], i, dn_nf[i] * 512)))
            for i in range(7):
                seq.append(("in", i, unit(S_in, i, 4096)))
            for i in range(2):
                seq.append(("out", i, unit(S_out, i, 4096)))
            for i in range(2):
                seq.append(("wq", i, unit(S_wq, i, 4096)))
            for i in range(2):
                seq.append(("wo", i, unit(S_wo, i, 4096)))
            for i in range(11):
                seq.append(("gu2", i, unit(S_gu[1], i, 4096)))
            for i in range(6):
                seq.append(("d2", i, unit(S_dn[1], i, dn_nf[i] * 512)))
            return seq

        useq = [("kv", i, U_kv[i]) for i in range(4)]
        for t in range(NT):
            useq += units_tile()
        ustate = {"next_load": 0, "next_use": 0}

        def unit_get():
            v = ustate["next_use"]
            while ustate["next_load"] < len(useq) and ustate["next_load"] <= v + RING - 2:
                u = ustate["next_load"]
                sl = u % RING
                src, n, skey = useq[u][2]
                dma("sp", wring[:, sl, 0:n], src, [skey], [("w", sl)], f"w{sl}")
                ustate["next_load"] += 1
            ustate["next_use"] += 1
            sl = v % RING
            n = useq[v][2][1]
            return wring[:, sl, 0:n], ("w", sl)

        dma("sp", identF[:], C_identF, [], ["identF"], "c0")
        dma("sp", identB[:], C_identB, [], ["identB"], "c0")
        dma("sp", decT[:], C_decT, [], ["decT"], "c0")
        dma("sp", xit[:], C_xit, [], ["xit"], "c0")
        dma("sp", gdec[:], C_gdec, [], ["gdec"], "c0")
        dma("sp", zt[:], C_zt, [], ["zt"], "c0")
        dma("sp", mask[:], C_mask, [], ["mask"], "c0")
        dma("sp", gret[:], Vd["ret_out_norm"].partition_broadcast(128), [], ["gret"], "c0")
        dma("sp", gdiff[:], Vd["diff_out_norm"].partition_broadcast(128), [], ["gdiff"], "c0")
        dma("sp", gfin[:], Vd["final_norm"].partition_broadcast(128), [], ["gfin"], "c0")
        for i, nm in enumerate(["diff_lq1", "diff_lk1", "diff_lq2", "diff_lk2"]):
            dma("sp", lqk[:, i, :], Vd[nm].partition_broadcast(128), [], ["lqk"], "c0")
        for i, nm in enumerate(["ffn1_norm", "mix_norm", "xattn_norm", "ffn2_norm", "mem_norm"]):
            src = Vd[nm].rearrange("o (kc p) -> p (o kc)", p=128)
            dma("sp", gcol[:, i, :], src, [], ["gcol"], "c0", slow=True)
        S.seal("c0")
        S.add("pool", lambda e: e.memset(eps_t[:], EPS), writes=["eps_t"])
        S.add("pool", lambda e: e.memset(Vc[:, :, :, 128:130], 1.0), writes=["Vc_ones"])
        S.add("pool", lambda e: e.memset(memV[:, :, :, 256:258], 1.0), writes=["memV_ones"])
        S.add("pool", lambda e: e.memset(Rm[:], 0.0), writes=["Rm"])
        S.add("pool", lambda e: e.memset(dqz1[0:64, :, :], 0.0), writes=["dqz1"])
        ts("dve", gdiff[:], gdiff[:], 0.8, None, ALU.mult, None, ["gdiff"], ["gdiff"])
        tt("dve", lqk[:, 0, :], lqk[:, 0, :], lqk[:, 1, :], ALU.mult, ["lqk"], ["lqk"])
        tt("dve", lqk[:, 2, :], lqk[:, 2, :], lqk[:, 3, :], ALU.mult, ["lqk"], ["lqk"])
        act(lqk[:, 1, :], lqk[:, 0, :], AF.Identity, ["lqk"], ["lqk", "lam"], accum=lam_s[:, 0:1])
        act(lqk[:, 3, :], lqk[:, 2, :], AF.Identity, ["lqk"], ["lqk", "lam"], accum=lam_s[:, 1:2])
        act(lam_s[:, 2:4], lam_s[:, 0:2], AF.Exp, ["lam"], ["lam"])
        tt("dve", neglam[:], lam_s[:, 3:4], lam_s[:, 2:3], ALU.subtract, ["lam"], ["neglam"])
        ts("dve", neglam[:], neglam[:], -0.2, None, ALU.add, None, ["neglam"], ["neglam"])

        pre_rr = [0]

        def wview(name):
            return Wd_[name].rearrange("(kc p) n -> p kc n", p=128)

        def prepass_cols(dst_unit, name, c0, ncols, off=0):
            dst, n, key = dst_unit
            d = dst[:, off:off + 8 * ncols].rearrange("p (kc n) -> p kc n", kc=8)
            g = pre_rr[0] % NPRE
            pre_rr[0] += 1
            dma("pool", d, wview(name)[:, :, c0:c0 + ncols], [], [key, ("preg", g)], f"pre{g}")

        def prepass_ffn(idx):
            g, u, dn = [("ffn1_w_gate", "ffn1_w_up", "ffn1_w_down"), ("ffn2_w_gate", "ffn2_w_up", "ffn2_w_down")][idx]
            for i in range(11):
                uu = unit(S_gu[idx], i, 4096)
                prepass_cols(uu, g, i * 256, 256, 0)
                prepass_cols(uu, u, i * 256, 256, 2048)
            wd = Wd_[dn].rearrange("(fc p) n -> p fc n", p=128)
            f0s = [0, 8, 16, 0, 8, 16]
            nfs = [8, 8, 6, 8, 8, 6]
            for i in range(6):
                h = i // 3
                dst, n, key = unit(S_dn[idx], i, nfs[i] * 512)
                d = dst.rearrange("p (f n) -> p f n", f=nfs[i])
                g = pre_rr[0] % NPRE
                pre_rr[0] += 1
                dma("pool", d, wd[:, f0s[i]:f0s[i] + nfs[i], h * 512:(h + 1) * 512], [], [key, ("preg", g)], f"pre{g}")

        PRO = int(os.environ.get("PRO", "255"))
        wtmp = arenaA[:, 0:4096].rearrange("p (g two j) -> p g two j", two=2, j=32)
        wtmp_b = hT[:].rearrange("p c t -> p (c t)").rearrange("p (g two j) -> p g two j", two=2, j=32)
        dma("sp", arenaA[:, 0:4096].rearrange("p (kc n) -> p kc n", kc=8), wview("w_in")[:, :, 0:512],
            [], hn_keys, "c1")
        cp("dve", wtmp_b[:, :, 0, :], wtmp[:, :, 1, :], hn_keys, ["hT_all"])
        cp("act", wtmp_b[:, :, 1, :], wtmp[:, :, 0, :], hn_keys, ["hT_all2"])
        urot = unit(S_in, 1, 4096)
        dma("sp", urot[0], hT[:].rearrange("p c t -> p (c t)"), ["hT_all", "hT_all2"], [urot[2]], "c2")
        if PRO & 4:
            for i in range(4):
                prepass_cols(unit(S_kv, i, 4096), "xattn_wkv", i * 512, 512)
        if PRO & 32:
            prepass_ffn(0)
        in_cols = {0: 0, 2: 1536, 3: 2048, 4: 512, 5: 1024, 6: 2560}
        for ui in ([0, 2, 3, 4, 5, 6] if PRO & 8 else []):
            prepass_cols(unit(S_in, ui, 4096), "w_in", in_cols[ui], 512)
        if PRO & 16:
            for i in range(2):
                prepass_cols(unit(S_out, i, 4096), "w_out", i * 512, 512)
            for i in range(2):
                prepass_cols(unit(S_wq, i, 4096), "xattn_wq", i * 512, 512)
            for i in range(2):
                prepass_cols(unit(S_wo, i, 4096), "xattn_wo", i * 512, 512)
            prepass_ffn(1)

        HT_KEYS = [("hT", kc) for kc in range(8)]
        for k in HT_KEYS:
            S.state[k] = [None, {"dma:c2": ("dma", "c2", S.dma_cnt["c2"])}]

        bank_rr = [0]

        def nb(pool=(0, 1, 2, 3, 4, 5, 6, 7)):
            bank_rr[0] += 1
            return pool[bank_rr[0] % len(pool)]

        def norm_stats(src, src_keys, nsub, hb, hk):
            def sq(s):
                act(hb[:, s, :], src[s], AF.Square, [src_keys[s]], [(hk, s), ("ss", s)],
                    scale=1.0 / 32.0, accum=ss[:, s:s + 1])

            def rs(s):
                act(rstd[:, s:s + 1], ss[:, s:s + 1], AF.Ln, [("ss", s), "eps_t"], [("rstd", s)], bias=eps_t[:])
                act(rstd[:, s:s + 1], rstd[:, s:s + 1], AF.Exp, [("rstd", s)], [("rstd", s)], scale=-0.5)
                ts("dve", hb[:, s, :], src[s], rstd[:, s:s + 1], None, ALU.mult, None,
                   [src_keys[s], ("rstd", s)], [(hk, s)])

            sq(0)
            for s in range(nsub):
                if s + 1 < nsub:
                    sq(s + 1)
                rs(s)

        def norm_transposes(nsub, gi, hb, hk, pool=(0, 1, 2, 3, 4, 5, 6, 7)):
            for kc in range(8):
                b = nb(pool)
                for s in range(nsub):
                    tr(ps[b][:, s * 128:(s + 1) * 128], hb[:, s, kc * 128:(kc + 1) * 128], identF[:],
                       [(hk, s), "identF"], [PSK[b]])
                eng = evac_eng()
                n = nsub * 128
                if eng == "dve":
                    ts("dve", hT[:, kc, 0:n], ps[b][:, 0:n], gcol[:, gi, kc:kc + 1], None, ALU.mult, None,
                       [PSK[b], "gcol"], [("hT", kc)])
                else:
                    act(hT[:, kc, 0:n], ps[b][:, 0:n], AF.Identity, [PSK[b], "gcol"], [("hT", kc)],
                        scale=gcol[:, gi, kc:kc + 1])

        def norm_to_hT(src, src_keys, nsub, gi, ncols_tok):
            norm_stats([src[:, s, :] for s in range(nsub)], src_keys, nsub, hn, "hn")
            norm_transposes(nsub, gi, hn, "hn")

        XK = [("x", s) for s in range(NSUB)]

        def ffn(gi, pre_normed=False, resid=None, resid_keys=None, hook_start=None, hook_mid=None):
            if not pre_normed:
                norm_to_hT(x_sb, XK, NSUB, gi, TT)
            if resid is None:
                resid = [x_sb[:, s, :] for s in range(NSUB)]
                resid_keys = XK
            gub = (0, 1, 2, 3)
            for u in range(11):
                slot, wk = unit_get()
                wg = slot[:, 0:2048].rearrange("p (kc n) -> p kc n", kc=8)
                wu = slot[:, 2048:4096].rearrange("p (kc n) -> p kc n", kc=8)
                for fc in range(2):
                    f = 2 * u + fc
                    bg = gub[(2 * f) % 4]
                    bu = gub[(2 * f + 1) % 4]
                    for kc in range(8):
                        mm(ps[bg][:], wg[:, kc, fc * 128:(fc + 1) * 128], hT[:, kc, :], kc == 0, kc == 7,
                           [wk, ("hT", kc)], [PSK[bg]])
                    for kc in range(8):
                        mm(ps[bu][:], wu[:, kc, fc * 128:(fc + 1) * 128], hT[:, kc, :], kc == 0, kc == 7,
                           [wk, ("hT", kc)], [PSK[bu]])
                    sgb = sg[:, f % 2, :]
                    act(sgb, ps[bg][:], AF.Silu, [PSK[bg]], [("sg", f % 2)])
                    tt("dve", actT[:, f, :], sgb, ps[bu][:], ALU.mult, [("sg", f % 2), PSK[bu]], [("actT", f)])
            f0s = [0, 8, 16]
            nfs = [8, 8, 6]
            if hook_start is not None:
                hook_start()
            for h in range(2):
                banks = [(4, 5, 6, 7), (0, 1, 2, 3)][h]
                for g in range(3):
                    slot, wk = unit_get()
                    wd = slot.rearrange("p (f n) -> p f n", f=nfs[g])
                    for s in range(NSUB):
                        b = banks[s]
                        for fi in range(nfs[g]):
                            f = f0s[g] + fi
                            mm(ps[b][:], actT[:, f, s * 128:(s + 1) * 128], wd[:, fi, :],
                               f == 0, f == NF - 1, [wk, ("actT", f)], [PSK[b]])
                    if h == 1 and g == 0 and hook_mid is not None:
                        hook_mid()
                for s in range(NSUB):
                    b = banks[s]
                    xs = x_sb[:, s, h * 512:(h + 1) * 512]
                    stt("dve", xs, ps[b][:], 0.5, resid[s][:, h * 512:(h + 1) * 512], ALU.mult, ALU.add,
                        [PSK[b], resid_keys[s], ("x", s)], [("x", s)])

        def tok_to_hT(src_keys, pool=(0, 1, 2, 3, 4, 5, 6, 7)):
            for kc in range(8):
                b = nb(pool)
                pbv = ps[b][:].bitcast(BF16)
                for s in range(NSUB):
                    tr(pbv[:, s * 128:(s + 1) * 128], tokb[:, s, kc * 128:(kc + 1) * 128], identB[:],
                       [src_keys[s][kc // 4], "identB"], [PSK[b]])
                cp(evac_eng(), hT[:, kc, :], pbv[:, 0:512], [PSK[b]], [("hT", kc)])

        def proj_to_x(src_tag):
            for half in range(2):
                slot, wk = unit_get()
                w = slot.rearrange("p (kc n) -> p kc n", kc=8)
                for s in range(NSUB):
                    b = nb()
                    for kc in range(8):
                        mm(ps[b][:], hT[:, kc, s * 128:(s + 1) * 128], w[:, kc, :], kc == 0, kc == 7,
                           [wk, ("hT", kc)], [PSK[b]])
                    xs = x_sb[:, s, half * 512:(half + 1) * 512]
                    tt("dve", xs, ps[b][:], xs, ALU.add, [PSK[b], ("x", s)], [("x", s)])

        TOKB = [("tokb", s) for s in range(NSUB)]
        TOKB2 = [[("tokb", s, 0), ("tokb", s, 1)] for s in range(NSUB)]

        def mix(t):
            norm_to_hT(x_sb, XK, NSUB, 1, TT)
            dma("sp", cs_sb[:], C_cs[:, :, t * TT:(t + 1) * TT], [], ["cs"], "cs")

            def fm_chunk(w, wk, c, b):
                for kc in range(8):
                    mm(ps[b][:], w[:, kc, c * 128:(c + 1) * 128], hT[:, kc, :], kc == 0, kc == 7,
                       [wk, ("hT", kc)], [PSK[b]])

            s0, k0 = unit_get()
            w0 = s0.rearrange("p (kc n) -> p kc n", kc=8)
            s1, k1 = unit_get()
            w1 = s1.rearrange("p (kc n) -> p kc n", kc=8)
            for c in range(4):
                ba = nb()
                bb = nb()
                fm_chunk(w0, k0, c, ba)
                fm_chunk(w1, k1, c, bb)
                tt("dve", rt1[:, c % 2, :], ps[ba][:], cs_sb[:, 0, :], ALU.mult, [PSK[ba], "cs"], [("rt1", c % 2)])
                tt("dve", rt2[:, c % 2, :], ps[bb][:], cs_sb[:, 1, :], ALU.mult, [PSK[bb], "cs"], [("rt2", c % 2)])
                dst = rqT[:, c, :] if c < 2 else rkT[:, c - 2, :]
                dk_ = ("rqT", c) if c < 2 else ("rkT", c - 2)
                tt(POOL_ELT, dst, rt1[:, c % 2, :], rt2[:, c % 2, :], ALU.add,
                   [("rt1", c % 2), ("rt2", c % 2)], [dk_])
                if c < 2:
                    for cc in range(NSUB):
                        tt(POOL_ELT, qxT[:, c, cc * 128:(cc + 1) * 128], rqT[:, c, cc * 128:(cc + 1) * 128],
                           xit[:, c, :], ALU.mult, [dk_, "xit"], [("qxT", c)])
            S.add("dve", lambda e: e.memset(dqT[64:128, :, :], 0.0), writes=["dqT"])
            s2, k2 = unit_get()
            w2 = s2.rearrange("p (kc n) -> p kc n", kc=8)
            for h in range(4):
                b = nb()
                fm_chunk(w2, k2, h, b)
                cp(evac_eng(), dqT[0:64, h, :], ps[b][0:64, :], [PSK[b]], ["dqT"])
                cp(evac_eng(), dqz1[64:128, h, :], ps[b][64:128, :], [PSK[b]], ["dqz1"])
            s3, k3 = unit_get()
            w3 = s3.rearrange("p (kc n) -> p kc n", kc=8)
            for h in range(4):
                b = nb()
                fm_chunk(w3, k3, h, b)
                cp(evac_eng(), Kc[:, h, t * TT:(t + 1) * TT], ps[b][:], [PSK[b]], [("Kc", h)])

            def tm_group(evac):
                slot, wk = unit_get()
                w = slot.rearrange("p (kc n) -> p kc n", kc=8)
                for s in range(NSUB):
                    b = nb()
                    for kc in range(8):
                        mm(ps[b][:], hT[:, kc, s * 128:(s + 1) * 128], w[:, kc, :], kc == 0, kc == 7,
                           [wk, ("hT", kc)], [PSK[b]])
                    evac(s, b)

            tm_group(lambda s, b: cp(evac_eng(), rv_sb[:, s, :], ps[b][:], [PSK[b]], [("rv", s)]))

            def ev_rg(s, b):
                act(srg[:, s, :], ps[b][:], AF.Silu, [PSK[b]], ["srg"])
                tt(POOL_ELT, srg[:, s, :], srg[:, s, :], gret[:], ALU.mult, ["srg", "gret"], ["srg"])
            tm_group(ev_rg)

            def ev_dv(s, b):
                j = 4 * t + s
                cp(evac_eng(), Vc[:, j, :, 0:128], ps[b][:].rearrange("p (h e) -> p h e", h=4), [PSK[b]], [("Vc", j)])
            tm_group(ev_dv)

            SB4 = (4, 5, 6)

            def ret_gen():
                RB = 7
                for c in range(NSUB):
                    pbv = ps[RB][:].bitcast(BF16)
                    for hp in range(2):
                        tr(pbv[:, hp * 128:(hp + 1) * 128], rkT[:, hp, c * 128:(c + 1) * 128], identB[:],
                           [("rkT", hp), "identB"], [PSK[RB]])
                    tt("dve", kz[:, c, :], pbv[:, 0:256], zt[:], ALU.mult, [PSK[RB], "zt"], [("kz", c)])
                    yield
                    for hh in range(2):
                        psI = ps[RB][:, 0:256].rearrange("p (hp n) -> p hp n", hp=2)
                        pr = slice(hh * 64, (hh + 1) * 64)
                        for hp in range(2):
                            mm(psI[:, hp, :], rkT[pr, hp, c * 128:(c + 1) * 128], rqT[pr, hp, c * 128:(c + 1) * 128],
                               True, True, [("rkT", hp), ("rqT", hp)], [PSK[RB]])
                        tt("dve", inT[:, c % 2, hh, :, :], psI, decT[:, hh, :, :], ALU.mult,
                           [PSK[RB], "decT"], [("inT", c % 2, hh)])
                        yield
                    psU = ps[RB][:, 0:256].rearrange("p (hp e) -> p hp e", hp=2)
                    for h in range(4):
                        hp, hh = h // 2, h % 2
                        mm(psU[hh * 64:(hh + 1) * 64, hp, :], kz[:, c, h * 64:(h + 1) * 64],
                           rv_sb[:, c, h * 128:(h + 1) * 128], True, True, [("kz", c), ("rv", c)], [PSK[RB]])
                    cp("act", Rb[:, c, :, :], Rm[:], ["Rm"], [("Rb", c)])
                    for hp in range(2):
                        stt("dve", Rm[:, hp, :], Rm[:, hp, :], gdec[:, hp:hp + 1], psU[:, hp, :], ALU.mult, ALU.add,
                            ["Rm", "gdec", PSK[RB]], ["Rm"])
                    yield
                    for hh in range(2):
                        psO = ps[RB][:, 0:256].rearrange("p (hp e) -> p hp e", hp=2)
                        pr = slice(hh * 64, (hh + 1) * 64)
                        for hp in range(2):
                            h = 2 * hp + hh
                            mm(psO[:, hp, :], inT[:, c % 2, hh, hp, :], rv_sb[:, c, h * 128:(h + 1) * 128], True, False,
                               [("inT", c % 2, hh), ("rv", c)], [PSK[RB]])
                            mm(psO[:, hp, :], qxT[pr, hp, c * 128:(c + 1) * 128], Rb[pr, c, hp, :], False, True,
                               [("qxT", hp), ("Rb", c)], [PSK[RB]])
                        for hp in range(2):
                            col = hh * 2 + hp
                            act(junkR[:, col, :], psO[:, hp, :], AF.Square, [PSK[RB]], [("junkR", col), ("ssoR", col)],
                                scale=1.0 / math.sqrt(128.0), accum=ssoR[:, col:col + 1])
                        cs2 = slice(hh * 2, hh * 2 + 2)
                        act(rsoR[:, cs2], ssoR[:, cs2], AF.Ln, [("ssoR", hh * 2), ("ssoR", hh * 2 + 1), "eps_t"],
                            [("rsoR", hh)], bias=eps_t[:])
                        act(rsoR[:, cs2], rsoR[:, cs2], AF.Exp, [("rsoR", hh)], [("rsoR", hh)], scale=-0.5)
                        for hp in range(2):
                            h = 2 * hp + hh
                            col = hh * 2 + hp
                            stt("dve", tokb[:, c, h * 128:(h + 1) * 128], psO[:, hp, :], rsoR[:, col:col + 1],
                                srg[:, c, h * 128:(h + 1) * 128], ALU.mult, ALU.mult,
                                [PSK[RB], ("rsoR", hh), "srg"], [("tokb", c, 0)])
                        yield

            rgen = ret_gen()
            ret_left = [6 * NSUB]

            def ret_step():
                if ret_left[0] > 0:
                    next(rgen)
                    ret_left[0] -= 1

            accsets = [(0, 1), (2, 3)]
            pending = []
            flat = []
            rnd = 0
            nk = 4 * t + 4
            n_iter_total = 8 * (2 * t + 4)
            pair_rr = [0]
            stride = max(1, n_iter_total // (6 * NSUB + 2))
            it_count = [0]
            for h in range(4):
                for cc in range(2):
                    bA, bB = accsets[rnd % 2]
                    rnd += 1
                    accA = ps[bA][:, 0:387].rearrange("p (b e) -> p b e", e=129)
                    accB = ps[bB][:, 0:129]
                    firstAB = [True, True]
                    dqz = dqT if cc == 0 else dqz1

                    def qk(item, h=h, cc=cc, dqz=dqz):
                        kjs, pj = item
                        pr_ = pair_rr[0] % 2
                        pair_rr[0] += 1
                        item.append(pr_)
                        for i_, kj in enumerate(kjs):
                            a = max(0, kj - 4 * t)
                            nq = TT - 128 * a
                            bs = 4 + 2 * pr_ + i_
                            mm(ps[bs][:, 0:nq], Kc[:, h, kj * 128:(kj + 1) * 128], dqz[:, h, a * 128:TT], True, True,
                               [("Kc", h), "dqT", "dqz1"], [PSK[bs]])
                        if len(kjs) == 2:
                            act(PT[:, 2 * pr_:2 * pr_ + 2, :], pp[pr_][:].rearrange("p (b t) -> p b t", b=2), AF.Exp,
                                [PSK[4 + 2 * pr_], PSK[5 + 2 * pr_]], [("PT", 2 * pr_), ("PT", 2 * pr_ + 1)], scale=0.125)
                        else:
                            kj = kjs[0]
                            a = max(0, kj - 4 * t)
                            nq = TT - 128 * a
                            bs = 4 + 2 * pr_
                            pi = 2 * pr_
                            act(PT[:, pi, 0:nq], ps[bs][:, 0:nq], AF.Exp, [PSK[bs]], [("PT", pi)], scale=0.125)
                            if kj >= 4 * t:
                                tt(POOL_ELT, PT[:, pi, 0:128], PT[:, pi, 0:128], mask[:], ALU.mult,
                                   [("PT", pi), "mask"], [("PT", pi)])

                    def pv(item, h=h, accA=accA, accB=accB, bA=bA, bB=bB, firstAB=firstAB):
                        kjs, pj, pr_ = item
                        for i_, kj in enumerate(kjs):
                            a = max(0, kj - 4 * t)
                            pi = 2 * pr_ + i_
                            for bq in range(a, 4):
                                lhs = PT[:, pi, (bq - a) * 128:(bq - a + 1) * 128]
                                rhs = Vc[:, kj, h, 0:129]
                                if bq < 3:
                                    mm(accA[:, bq, :], lhs, rhs, firstAB[0], kj == 4 * t + bq,
                                       [("PT", pi), ("Vc", kj), "Vc_ones"], [PSK[bA]], skip=True)
                                    firstAB[0] = False
                                else:
                                    mm(accB, lhs, rhs, firstAB[1], kj == 4 * t + bq,
                                       [("PT", pi), ("Vc", kj), "Vc_ones"], [PSK[bB]], skip=True)
                                    firstAB[1] = False

                    def fin1(h=h, cc=cc, accA=accA, accB=accB, bA=bA, bB=bB):
                        for bq in range(4):
                            acc = accA[:, bq, :] if bq < 3 else accB
                            bk = PSK[bA] if bq < 3 else PSK[bB]
                            rc = rcp[:, bq:bq + 1]
                            S.add("dve", lambda e, acc=acc, rc=rc: e.reciprocal(out=rc, in_=acc[:, 128:129]),
                                  reads=[bk], writes=[("rcp", bq)])
                            if cc == 0:
                                ts("dve", a0[:, bq, :], acc[:, 0:128], rc, None, ALU.mult, None,
                                   [bk, ("rcp", bq)], [("a0", bq)])
                            else:
                                tt("dve", rc, rc, neglam[:], ALU.mult, [("rcp", bq), "neglam"], [("rcp", bq)])
                                stt("dve", ad[:, bq, :], acc[:, 0:128], rc, a0[:, bq, :], ALU.mult, ALU.add,
                                    [bk, ("rcp", bq), ("a0", bq)], [("ad", bq)])

                    def fin2(h=h):
                        for bq in range(4):
                            act(junk[:, bq, :], ad[:, bq, :], AF.Square, [("ad", bq)], [("junk", bq), ("sso", bq)],
                                scale=1.0 / math.sqrt(128.0), accum=sso[:, bq:bq + 1])
                        act(rso[:], sso[:], AF.Ln, [("sso", bq) for bq in range(4)] + ["eps_t"], ["rso"], bias=eps_t[:])
                        act(rso[:], rso[:], AF.Exp, ["rso"], ["rso"], scale=-0.5)

                    def fin3(h=h):
                        for bq in range(4):
                            stt("dve", tokb[:, bq, 512 + h * 128:512 + (h + 1) * 128], ad[:, bq, :], rso[:, bq:bq + 1],
                                gdiff[:, h * 128:(h + 1) * 128], ALU.mult, ALU.mult,
                                [("ad", bq), "rso", "gdiff"], [("tokb", bq, 1)])

                    items = [[[kj, kj + 1], 0] for kj in range(0, 4 * t, 2)] + [[[kj], 0] for kj in range(4 * t, nk)]
                    for ii, it_ in enumerate(items):
                        last = ii == len(items) - 1
                        flat.append((qk, pv, it_, (fin1, fin2, fin3) if (last and cc == 1) else
                                     ((fin1,) if last else ())))
            LA = 1
            for i in range(min(LA, len(flat))):
                flat[i][0](flat[i][2])
            for i in range(len(flat)):
                if i + LA < len(flat):
                    flat[i + LA][0](flat[i + LA][2])
                flat[i][1](flat[i][2])
                if pending:
                    pending.pop(0)()
                for f_ in flat[i][3]:
                    pending.append(f_)
                it_count[0] += 1
                if it_count[0] % stride == 0:
                    ret_step()
            while pending:
                pending.pop(0)()
            while ret_left[0] > 0:
                ret_step()
            if dbg and t == 0:
                dma("pool", DBG[4].rearrange("(s p) d -> p s d", p=128), tokb[:], [k for kk in TOKB2 for k in kk], ["DBG4"], "dbg")
            tok_to_hT(TOKB2, pool=(4, 5, 6, 7))
            proj_to_x("out")

        def xattn():
            norm_to_hT(x_sb, XK, NSUB, 2, TT)
            for half in range(2):
                slot, wk = unit_get()
                w = slot.rearrange("p (kc n) -> p kc n", kc=8)
                for c4 in range(4):
                    dc = half * 4 + c4
                    b = nb()
                    for kc in range(8):
                        mm(ps[b][:], w[:, kc, c4 * 128:(c4 + 1) * 128], hT[:, kc, :], kc == 0, kc == 7,
                           [wk, ("hT", kc)], [PSK[b]])
                    cp(evac_eng(), qT[:, dc, :], ps[b][:], [PSK[b]], ["qT"])
            def xs_scores(h):
                P = PTx if h % 2 == 0 else PTx2
                pk = "PTx" if h % 2 == 0 else "PTx2"
                for mt in range(2):
                    b = nb()
                    for i in range(2):
                        mm(ps[b][:], memKT[:, 2 * h + i, mt * 128:(mt + 1) * 128], qT[:, 2 * h + i, :], i == 0, i == 1,
                           ["memKT", "qT"], [PSK[b]])
                    act(P[:, mt, :], ps[b][:], AF.Exp, [PSK[b]], [pk], scale=1.0 / 16.0)

            def xs_pv(h):
                P = PTx if h % 2 == 0 else PTx2
                pk = "PTx" if h % 2 == 0 else "PTx2"
                for s in range(NSUB):
                    b = nb()
                    for mt in range(2):
                        mm(ps[b][:, 0:257], P[:, mt, s * 128:(s + 1) * 128], memV[:, mt, h, 0:257], mt == 0, mt == 1,
                           [pk, "memV", "memV_ones"], [PSK[b]])
                    rc = rcp[:, 4 + s:5 + s]
                    S.add("dve", lambda e, b=b, rc=rc: e.reciprocal(out=rc, in_=ps[b][:, 256:257]),
                          reads=[PSK[b]], writes=[("rcpx", s)])
                    ts("dve", tokb[:, s, h * 256:(h + 1) * 256], ps[b][:, 0:256], rc, None, ALU.mult, None,
                       [PSK[b], ("rcpx", s)], [("tokb", s, h // 2)])

            xs_scores(0)
            for h in range(4):
                if h + 1 < 4:
                    xs_scores(h + 1)
                xs_pv(h)
            tok_to_hT(TOKB2)
            proj_to_x("wo")

        def mem_kv():
            dma("sp", x_sb[:, 0:2, :], MEM.rearrange("(s p) d -> p s d", p=128), [], [("x", 0), ("x", 1)], "x")
            norm_to_hT(x_sb, XK, 2, 4, NMEM)
            for u in range(2):
                slot, wk = unit_get()
                w = slot.rearrange("p (kc n) -> p kc n", kc=8)
                for c4 in range(4):
                    dc = u * 4 + c4
                    b = nb()
                    for kc in range(8):
                        mm(ps[b][:, 0:NMEM], w[:, kc, c4 * 128:(c4 + 1) * 128], hT[:, kc, 0:NMEM], kc == 0, kc == 7,
                           [wk, ("hT", kc)], [PSK[b]])
                    cp(evac_eng(), memKT[:, dc, :], ps[b][:, 0:NMEM], [PSK[b]], ["memKT"])
            for u in range(2):
                slot, wk = unit_get()
                w = slot.rearrange("p (kc n) -> p kc n", kc=8)
                for mt in range(2):
                    b = nb()
                    for kc in range(8):
                        mm(ps[b][:], hT[:, kc, mt * 128:(mt + 1) * 128], w[:, kc, :], kc == 0, kc == 7,
                           [wk, ("hT", kc)], [PSK[b]])
                    cp(evac_eng(), memV[:, mt, 2 * u:2 * u + 2, 0:256], ps[b][:].rearrange("p (h e) -> p h e", h=2),
                       [PSK[b]], ["memV"])

        def final_norm_store(t):
            for s in range(NSUB):
                act(hn[:, s, :], x_sb[:, s, :], AF.Square, [("x", s)], [("hn", s), ("ss", s)],
                    scale=1.0 / 32.0, accum=ss[:, s:s + 1])
                act(rstd[:, s:s + 1], ss[:, s:s + 1], AF.Ln, [("ss", s), "eps_t"], [("rstd", s)], bias=eps_t[:])
                act(rstd[:, s:s + 1], rstd[:, s:s + 1], AF.Exp, [("rstd", s)], [("rstd", s)], scale=-0.5)
                stt("dve", x_sb[:, s, :], x_sb[:, s, :], rstd[:, s:s + 1], gfin[:], ALU.mult, ALU.mult,
                    [("x", s), ("rstd", s), "gfin"], [("x", s)])
            for s in range(NSUB):
                dma("sp", OUT[t * TT + s * 128:t * TT + (s + 1) * 128, :], x_sb[:, s, :], [XK[s]], [("OUT", s)], f"out{s}")

        def dump(i):
            if dbg:
                dma("sp", DBG[i].rearrange("(s p) d -> p s d", p=128), x_sb[:], XK, ["DBG"], "dbg")

        if stage >= 2:
            mem_kv()
        def load_xn(t):
            for s_ in range(NSUB):
                dma("sp", xn[s_], X[t * TT + s_ * 128:t * TT + (s_ + 1) * 128, :], [], [XN[s_]], f"x{s_}")

        load_xn(0)
        norm_stats(xn, XN, NSUB, hnB, "hnB")
        norm_transposes(NSUB, 0, hnB, "hnB")
        for t in range(NT):
            S.epoch = t + 1
            if stage >= 3:
                ffn(0, pre_normed=True, resid=xn, resid_keys=XN)
            if t == 0:
                dump(0)
            if stage >= 4:
                mix(t)
            if t == 0:
                dump(1)
            if stage >= 5:
                xattn()
            if t == 0:
                dump(2)
            nxt = t + 1 < NT
            if nxt:
                load_xn(t + 1)
            if stage >= 6:
                if nxt:
                    ffn(3, hook_start=lambda: norm_stats(xn, XN, NSUB, hnB, "hnB"),
                        hook_mid=lambda: norm_transposes(NSUB, 0, hnB, "hnB", pool=(4, 5, 6, 7)))
                else:
                    ffn(3)
            if t == 0:
                dump(3)
            final_norm_store(t)
        for s_ in range(NSUB):
            S.wait_final(f"out{s_}")
        if dbg:
            S.wait_final("dbg")
        S.emit(nc)
    return nc


_CACHE = {}


def kernel(**inputs):
    n = 8
    consts = _const_tables()
    if "nc" not in _CACHE:
        _CACHE["nc"] = build(8, False)
    nc = _CACHE["nc"]
    x = np.asarray(inputs["x"], dtype=np.float32)
    mem = np.asarray(inputs["mem"], dtype=np.float32)
    shared = {}
    for nm in W_NAMES:
        shared[nm] = np.ascontiguousarray(np.asarray(inputs[nm], dtype=np.float32)[0])
    for nm in ["ffn1_norm", "mix_norm", "xattn_norm", "ffn2_norm", "mem_norm"]:
        shared[nm] = np.ascontiguousarray(np.asarray(inputs[nm], dtype=np.float32).reshape(1, D))
    shared["final_norm"] = np.ascontiguousarray(np.asarray(inputs["final_norm"], dtype=np.float32).reshape(1, D))
    shared["ret_out_norm"] = np.ascontiguousarray(np.asarray(inputs["ret_out_norm"], dtype=np.float32).reshape(1, 512))
    shared["diff_out_norm"] = np.ascontiguousarray(np.asarray(inputs["diff_out_norm"], dtype=np.float32).reshape(1, 512))
    for nm in ["diff_lq1", "diff_lk1", "diff_lq2", "diff_lk2"]:
        shared[nm] = np.ascontiguousarray(np.asarray(inputs[nm], dtype=np.float32).reshape(1, 64))
    shared.update(consts)
    in_maps = []
    for b in range(n):
        m = dict(shared)
        m["x"] = np.ascontiguousarray(x[b])
        m["mem"] = np.ascontiguousarray(mem[b])
        in_maps.append(m)
    res = run_bass_kernel_spmd(nc, in_maps, core_ids=list(range(n)))
    out = np.stack([np.asarray(r["out"], dtype=np.float32) for r in res.results], axis=0)
    return out
```

```python
import contextlib
import os
import math
import numpy as np
import ml_dtypes
import concourse.bass as bass
import concourse.mybir as mybir
from concourse.bass_utils import run_bass_kernel_spmd

F32 = mybir.dt.float32
BF16 = mybir.dt.bfloat16
ALU = mybir.AluOpType
AF = mybir.ActivationFunctionType

D = 1024
SEQ = 4096
NMEM = 256
DFF = 2816
NF = 22
EPS = 1e-6
TT = 512
NSUB = 4
RING = 4
POOL_ELT = os.environ.get("POOL_ELT", "dve")
NPRE = int(os.environ.get("NPRE", "4"))
SLOT = 4096


class _Op:
    __slots__ = ("eng", "fn", "waits", "flag", "idx", "dma_grp", "dma_cnt", "epoch", "count", "vc")


class Sched:
    ENGS = ("pe", "act", "dve", "pool", "sp")

    def __init__(self):
        self.ops = {e: [] for e in self.ENGS}
        self.state = {}
        self.vc = {e: {} for e in self.ENGS}
        self.dma_cnt = {}
        self.dma_vc = {}
        self.epoch = 0
        self.final_waits = []
        self.alias = {}

    @staticmethod
    def _join(a, b):
        for k, v in b.items():
            if a.get(k, -1) < v:
                a[k] = v

    def add(self, eng, fn, reads=(), writes=(), dma=None):
        op = _Op()
        op.eng = eng
        op.fn = fn
        op.waits = []
        op.flag = False
        op.idx = len(self.ops[eng])
        op.dma_grp = dma
        op.dma_cnt = 0
        op.epoch = self.epoch
        op.count = 0
        if dma is not None:
            self.dma_cnt[dma] = self.dma_cnt.get(dma, 0) + 16
            op.dma_cnt = self.dma_cnt[dma]
            ref = ("dma", dma, op.dma_cnt)
            rkey = "dma:" + dma
        else:
            ref = ("eng", eng, op.idx, op)
            rkey = eng
        if self.alias:
            w2 = list(writes)
            for k in writes:
                if k in self.alias:
                    w2.extend(self.alias[k])
            writes = w2
        for k in reads:
            st = self.state.get(k)
            if st is not None and st[0] is not None:
                self._need(op, st[0])
        for k in writes:
            st = self.state.get(k)
            if st is not None:
                if st[0] is not None:
                    self._need(op, st[0])
                for r in st[1].values():
                    self._need(op, r)
        for k in reads:
            st = self.state.get(k)
            if st is None:
                st = [None, {}]
                self.state[k] = st
            st[1][rkey] = ref
        for k in writes:
            self.state[k] = [ref, {}]
        vc = dict(self.vc[eng])
        if dma is not None:
            prev = self.dma_vc.get((dma, op.dma_cnt - 16))
            if prev is not None:
                self._join(vc, prev)
            vc["dma:" + dma] = op.dma_cnt
            self.dma_vc[(dma, op.dma_cnt)] = vc
        else:
            vc[eng] = op.idx
        op.vc = vc
        self.ops[eng].append(op)
        return op

    def _need(self, op, d):
        evc = self.vc[op.eng]
        if d[0] == "eng":
            X, idx, dop = d[1], d[2], d[3]
            if X == "pe" and op.eng == "pe" and op.dma_grp is None:
                return
            if evc.get(X, -1) >= idx:
                return
            dop.flag = True
            op.waits.append(("eng", X, dop))
            self._join(evc, dop.vc)
        else:
            G, cnt = d[1], d[2]
            key = "dma:" + G
            if evc.get(key, -1) >= cnt:
                return
            op.waits.append(("dma", G, cnt))
            self._join(evc, self.dma_vc[(G, cnt)])

    def seal(self, grp):
        tot = self.dma_cnt[grp]
        for k, stt_ in self.state.items():
            w = stt_[0]
            if w is not None and w[0] == "dma" and w[1] == grp:
                stt_[0] = ("dma", grp, tot)

    def wait_final(self, dma_grp):
        self.final_waits.append(dma_grp)

    def emit(self, nc):
        used = set()
        for e in self.ENGS:
            cnts = {}
            for op in self.ops[e]:
                if op.flag and op.dma_grp is None:
                    cnts[op.epoch] = cnts.get(op.epoch, 0) + 1
                    op.count = cnts[op.epoch]
                    used.add((e, op.epoch))
        with contextlib.ExitStack() as es:
            esem = {}
            for (e, ep) in sorted(used):
                esem[(e, ep)] = es.enter_context(nc.semaphore(f"s_{e}_{ep}"))
            dsem = {}
            for g in sorted(self.dma_cnt):
                dsem[g] = es.enter_context(nc.semaphore(f"d_{g}"))
            block = es.enter_context(nc.Block())

            def run(e, name):
                for op in self.ops[name]:
                    for w in op.waits:
                        if w[0] == "eng":
                            dop = w[2]
                            e.wait_ge(esem[(w[1], dop.epoch)], dop.count)
                        else:
                            e.wait_ge(dsem[w[1]], w[2])
                    ins = op.fn(e)
                    if op.dma_grp is not None:
                        ins.then_inc(dsem[op.dma_grp], 16)
                    elif op.flag:
                        ins.then_inc(esem[(name, op.epoch)], 1)
                if name == "sp":
                    for g in self.final_waits:
                        e.wait_ge(dsem[g], self.dma_cnt[g])

            @block.tensor
            def _(e):
                run(e, "pe")

            @block.scalar
            def _(e):
                run(e, "act")

            @block.vector
            def _(e):
                run(e, "dve")

            @block.gpsimd
            def _(e):
                run(e, "pool")

            @block.sync
            def _(e):
                run(e, "sp")


def _const_tables():
    f32 = np.float32
    c = {}
    c["identF"] = np.eye(128, dtype=f32)
    c["identB"] = np.eye(128, dtype=f32).astype(ml_dtypes.bfloat16)
    inv = (1.0 / (f32(10000.0) ** (np.arange(0, 64, 2, dtype=f32) / f32(64)))).astype(f32)
    pos = np.arange(SEQ, dtype=f32)
    ang = (pos[:, None] * inv[None, :]).astype(f32)
    cos = np.cos(ang).astype(f32)
    sin = np.sin(ang).astype(f32)
    p = np.arange(128)
    cs = np.zeros((128, 2, SEQ), f32)
    cs[:, 0, :] = cos[:, p % 32].T
    sgn = np.where((p % 64) < 32, -1.0, 1.0).astype(f32)
    cs[:, 1, :] = sin[:, p % 32].T * sgn[:, None]
    c["cs"] = cs
    H = 4
    log_g = np.log(1.0 - 2.0 ** (-5.0 - np.arange(H, dtype=np.float64)))
    n = np.arange(128, dtype=np.float64)
    rel = n[None, :] - n[:, None]
    decT = np.zeros((128, 2, 2, 128), f32)
    for h in range(H):
        decT[:, h % 2, h // 2, :] = np.where(rel >= 0, np.exp(np.maximum(rel, 0) * log_g[h]), 0.0) * 0.125
    c["decT"] = decT
    xi = np.exp((n[None, :] + 1.0) * log_g[:, None])
    zeta = np.exp((127.0 - n[None, :]) * log_g[:, None])
    gch = np.exp(128.0 * log_g)
    xit = np.zeros((128, 2, 128), f32)
    gdec = np.zeros((128, 2), f32)
    for hp in range(2):
        for hh in range(2):
            xit[hh * 64:(hh + 1) * 64, hp, :] = xi[2 * hp + hh][None, :] * 0.125
            gdec[hh * 64:(hh + 1) * 64, hp] = gch[2 * hp + hh]
    c["xit"] = xit
    c["gdec"] = gdec
    zt = np.zeros((128, 256), f32)
    for h in range(H):
        zt[:, h * 64:(h + 1) * 64] = zeta[h][:, None]
    c["zt"] = zt
    k = np.arange(128)
    c["mask"] = (k[:, None] <= k[None, :]).astype(f32).astype(ml_dtypes.bfloat16)
    return c


W_NAMES = ["ffn1_w_gate", "ffn1_w_up", "ffn1_w_down", "w_in", "w_out", "xattn_wq", "xattn_wkv",
           "xattn_wo", "ffn2_w_gate", "ffn2_w_up", "ffn2_w_down"]
V_NAMES = ["ffn1_norm", "mix_norm", "xattn_norm", "ffn2_norm", "mem_norm", "final_norm",
           "ret_out_norm", "diff_out_norm", "diff_lq1", "diff_lk1", "diff_lq2", "diff_lk2"]


def build(NT=8, dbg=False, stage=9):
    nc = bass.Bass("TRN2", target_bir_lowering=False)
    S = Sched()
    ntok = NT * TT

    def din(name, shape, dt=F32):
        return nc.dram_tensor(name, list(shape), dt, kind="ExternalInput").ap()

    X = din("x", [SEQ, D])
    MEM = din("mem", [NMEM, D])
    Wd_ = {}
    Wd_["ffn1_w_gate"] = din("ffn1_w_gate", [D, DFF])
    Wd_["ffn1_w_up"] = din("ffn1_w_up", [D, DFF])
    Wd_["ffn1_w_down"] = din("ffn1_w_down", [DFF, D])
    Wd_["w_in"] = din("w_in", [D, 3072])
    Wd_["w_out"] = din("w_out", [D, D])
    Wd_["xattn_wq"] = din("xattn_wq", [D, D])
    Wd_["xattn_wkv"] = din("xattn_wkv", [D, 2 * D])
    Wd_["xattn_wo"] = din("xattn_wo", [D, D])
    Wd_["ffn2_w_gate"] = din("ffn2_w_gate", [D, DFF])
    Wd_["ffn2_w_up"] = din("ffn2_w_up", [D, DFF])
    Wd_["ffn2_w_down"] = din("ffn2_w_down", [DFF, D])
    Vd = {}
    for nm in ["ffn1_norm", "mix_norm", "xattn_norm", "ffn2_norm", "mem_norm", "final_norm"]:
        Vd[nm] = din(nm, [1, D])
    Vd["ret_out_norm"] = din("ret_out_norm", [1, 512])
    Vd["diff_out_norm"] = din("diff_out_norm", [1, 512])
    for nm in ["diff_lq1", "diff_lk1", "diff_lq2", "diff_lk2"]:
        Vd[nm] = din(nm, [1, 64])
    C_identF = din("identF", [128, 128])
    C_identB = din("identB", [128, 128], BF16)
    C_cs = din("cs", [128, 2, SEQ])
    C_decT = din("decT", [128, 2, 2, 128])
    C_xit = din("xit", [128, 2, 128])
    C_gdec = din("gdec", [128, 2])
    C_zt = din("zt", [128, 256])
    C_mask = din("mask", [128, 128], BF16)
    OUT = nc.dram_tensor("out", [SEQ, D], F32, kind="ExternalOutput").ap()
    if dbg:
        DBG = nc.dram_tensor("dbg", [8, 512, D], F32, kind="ExternalOutput").ap()

    def dscratch(name, nu):
        return nc.dram_tensor(name, [nu, 128, SLOT], BF16, kind="Internal").ap()

    S_gu = [dscratch("s_gu1", 11), dscratch("s_gu2", 11)]
    S_dn = [dscratch("s_d1", 6), dscratch("s_d2", 6)]
    S_in = dscratch("s_in", 7)
    S_out = dscratch("s_out", 2)
    S_wq = dscratch("s_wq", 2)
    S_wo = dscratch("s_wo", 2)
    S_kv = dscratch("s_kv", 4)

    with contextlib.ExitStack() as es:
        def sb(name, shape, dt):
            return es.enter_context(nc.sbuf_tensor(name, list(shape), dt))

        x_sb = sb("x_sb", [128, NSUB, D], F32)
        Kc = sb("Kc", [128, 4, SEQ], BF16)
        Vc = sb("Vc", [128, SEQ // 128, 4, 130], BF16)
        wring = sb("wring", [128, RING, SLOT], BF16)
        tokb = sb("tokb", [128, NSUB, D], BF16)
        hT = sb("hT", [128, 8, TT], BF16)
        arenaA = sb("arenaA", [128, 5632], F32)
        hn = arenaA[:, 0:4096].rearrange("p (s d) -> p s d", s=NSUB)
        aA_bf = arenaA[:].bitcast(BF16)
        actT = aA_bf.rearrange("p (f t) -> p f t", f=NF)
        srg = arenaA[:, 0:2048].rearrange("p (s d) -> p s d", s=NSUB)
        rt1 = arenaA[:, 2048:3072].rearrange("p (b t) -> p b t", b=2)
        rt2 = arenaA[:, 3072:4096].rearrange("p (b t) -> p b t", b=2)
        dqT = aA_bf[:, 8192:10240].rearrange("p (h t) -> p h t", h=4)
        qT = aA_bf[:, 0:4096].rearrange("p (c t) -> p c t", c=8)
        PTx = aA_bf[:, 4096:5120].rearrange("p (m t) -> p m t", m=2)
        PTx2 = aA_bf[:, 5120:6144].rearrange("p (m t) -> p m t", m=2)
        sgpt = sb("sgpt", [128, 2 * TT], F32)
        sg = sgpt[:].rearrange("p (b t) -> p b t", b=2)
        rv_sb = sb("rv_sb", [128, NSUB, 512], BF16)
        arenaB = sb("arenaB", [128, 4096], F32)
        aB_bf = arenaB[:].bitcast(BF16)
        rqT = aB_bf[:, 0:1024].rearrange("p (c t) -> p c t", c=2)
        rkT = aB_bf[:, 1024:2048].rearrange("p (c t) -> p c t", c=2)
        qxT = aB_bf[:, 2048:3072].rearrange("p (c t) -> p c t", c=2)
        kz = aB_bf[:, 3072:4096].rearrange("p (c n) -> p c n", c=NSUB)
        inT = aB_bf[:, 4096:5120].rearrange("p (a b c n) -> p a b c n", a=2, b=2, c=2)
        Rb = aB_bf[:, 5120:6144].rearrange("p (c h e) -> p c h e", c=4, h=2)
        a0 = arenaB[:, 3072:3584].rearrange("p (b e) -> p b e", b=4)
        ad = arenaB[:, 3584:4096].rearrange("p (b e) -> p b e", b=4)
        hnB = arenaB[:].rearrange("p (s d) -> p s d", s=NSUB)
        PT = sgpt[:].bitcast(BF16).rearrange("p (b t) -> p b t", b=4)
        dqz1 = sb("dqz1", [128, 4, TT], BF16)
        Rm = sb("Rm", [128, 2, 128], F32)
        junk = sb("junk", [128, 4, 128], BF16)
        cs_sb = sb("cs_sb", [128, 2, TT], F32)
        tokb_flat = tokb[:].rearrange("p s d -> p (s d)")
        xn = [tokb_flat[:, 0:2048].bitcast(F32), tokb_flat[:, 2048:4096].bitcast(F32),
              rv_sb[:].rearrange("p s d -> p (s d)").bitcast(F32), cs_sb[:].rearrange("p a t -> p (a t)")]
        identF = sb("identF_sb", [128, 128], F32)
        identB = sb("identB_sb", [128, 128], BF16)
        decT = sb("decT_sb", [128, 2, 2, 128], F32)
        xit = sb("xit_sb", [128, 2, 128], F32)
        gdec = sb("gdec_sb", [128, 2], F32)
        zt = sb("zt_sb", [128, 256], F32)
        mask = sb("mask_sb", [128, 128], BF16)
        gret = sb("gret", [128, 512], F32)
        gdiff = sb("gdiff", [128, 512], F32)
        gfin = sb("gfin", [128, D], F32)
        gcol = sb("gcol", [128, 5, 8], F32)
        memKT = sb("memKT", [128, 8, NMEM], BF16)
        memV = sb("memV", [128, 2, 4, 258], BF16)
        lqk = sb("lqk", [128, 4, 64], F32)
        st = sb("st", [128, 64], F32)
        junkR = lqk[:].rearrange("p a b -> p (a b)").bitcast(BF16).rearrange("p (h e) -> p h e", h=4)
        eps_t = sb("eps_t", [128, 1], F32)
        neglam = sb("neglam", [128, 1], F32)
        ps = [es.enter_context(nc.psum_tensor(f"ps{i}", [128, 512], F32)) for i in range(4)]
        pp = [es.enter_context(nc.psum_tensor(f"pp{i}", [128, 1024], F32)) for i in range(2)]
        ps = ps + [pp[0][:, 0:512], pp[0][:, 512:1024], pp[1][:, 0:512], pp[1][:, 512:1024]]

        ss = st[:, 0:4]
        rstd = st[:, 4:8]
        sso = st[:, 8:12]
        rso = st[:, 12:16]
        rcp = st[:, 16:24]
        lam_s = st[:, 24:28]
        ssoR = st[:, 28:32]
        rsoR = st[:, 32:36]

        hn_keys = [("hn", s) for s in range(NSUB)]
        act_keys = [("actT", f) for f in range(NF)]
        mixA_keys = ["srg", ("rt1", 0), ("rt1", 1), ("rt2", 0), ("rt2", 1), "dqT"]
        xatA_keys = ["qT", "PTx", "PTx2"]
        fams = [hn_keys, act_keys, mixA_keys, xatA_keys]
        for fam in fams:
            others = [k2 for f2 in fams if f2 is not fam for k2 in f2]
            for k in fam:
                S.alias[k] = others
        for h_ in range(4):
            S.alias[("junkR", h_)] = ["lqk"]
        hnB_keys = [("hnB", s_) for s_ in range(NSUB)]
        tmpB_keys = ([("rqT", i) for i in range(2)] + [("rkT", i) for i in range(2)] + [("qxT", i) for i in range(2)]
                     + [("kz", i) for i in range(4)] + [("inT", i, j) for i in range(2) for j in range(2)]
                     + [("a0", i) for i in range(4)] + [("ad", i) for i in range(4)] + [("Rb", i) for i in range(4)])
        for k in hnB_keys:
            S.alias[k] = tmpB_keys
        for k in tmpB_keys:
            S.alias[k] = hnB_keys
        XN = [("xn", s_) for s_ in range(NSUB)]
        xn_al = {0: [("tokb", 0, 0), ("tokb", 0, 1), ("tokb", 1, 0), ("tokb", 1, 1)],
                 1: [("tokb", 2, 0), ("tokb", 2, 1), ("tokb", 3, 0), ("tokb", 3, 1)],
                 2: [("rv", i) for i in range(4)], 3: ["cs"]}
        for i_, ks in xn_al.items():
            S.alias[XN[i_]] = ks
            for k in ks:
                S.alias[k] = [XN[i_]]
        sg_keys = [("sg", i) for i in range(2)]
        pt_keys = [("PT", i) for i in range(4)]
        for k in sg_keys:
            S.alias[k] = pt_keys
        for k in pt_keys:
            S.alias[k] = sg_keys

        PSK = [("ps", i) for i in range(8)]

        def dma(eng, out, in_, reads, writes, grp, slow=False):
            if slow:
                S.add(eng, lambda e: e.dma_start(out=out, in_=in_, allow_slow_non_contiguous=True),
                      reads=reads, writes=writes, dma=grp)
            else:
                S.add(eng, lambda e: e.dma_start(out=out, in_=in_), reads=reads, writes=writes, dma=grp)

        def mm(out, lhsT, rhs, start, stop, reads, writes, skip=False):
            S.add("pe", lambda e: e.matmul(out, lhsT=lhsT, rhs=rhs, start=start, stop=stop,
                                           skip_group_check=skip), reads=reads, writes=writes)

        def tr(out, in_, ident, reads, writes):
            S.add("pe", lambda e: e.transpose(out=out, in_=in_, identity=ident), reads=reads, writes=writes)

        def act(out, in_, func, reads, writes, scale=1.0, bias=None, accum=None):
            def fn(e):
                kw = {}
                if bias is not None:
                    kw["bias"] = bias
                if accum is not None:
                    kw["accum_out"] = accum
                return e.activation(out=out, in_=in_, func=func, scale=scale, **kw)
            S.add("act", fn, reads=reads, writes=writes)

        def ts(eng, out, in0, s1, s2, op0, op1, reads, writes):
            if s2 is None:
                S.add(eng, lambda e: e.tensor_scalar(out=out, in0=in0, scalar1=s1, scalar2=None, op0=op0),
                      reads=reads, writes=writes)
            else:
                S.add(eng, lambda e: e.tensor_scalar(out=out, in0=in0, scalar1=s1, scalar2=s2, op0=op0, op1=op1),
                      reads=reads, writes=writes)

        def tt(eng, out, in0, in1, op, reads, writes):
            S.add(eng, lambda e: e.tensor_tensor(out=out, in0=in0, in1=in1, op=op), reads=reads, writes=writes)

        def stt(eng, out, in0, scalar, in1, op0, op1, reads, writes):
            S.add(eng, lambda e: e.scalar_tensor_tensor(out=out, in0=in0, scalar=scalar, in1=in1, op0=op0, op1=op1),
                  reads=reads, writes=writes)

        def cp(eng, out, in_, reads, writes):
            if eng == "act":
                S.add("act", lambda e: e.copy(out=out, in_=in_), reads=reads, writes=writes)
            else:
                S.add(eng, lambda e: e.tensor_copy(out=out, in_=in_), reads=reads, writes=writes)

        evac_rr = [0]

        def evac_eng():
            evac_rr[0] += 1
            return "dve" if evac_rr[0] % 2 else "act"

        def unit(sc, ui, n):
            return (sc[ui, :, 0:n], n, ("S", sc.tensor.name, ui))

        U_kv = [unit(S_kv, i, 4096) for i in range(4)]

        def units_tile():
            seq = []
            for i in range(11):
                seq.append(("gu1", i, unit(S_gu[0], i, 4096)))
            dn_nf = [8, 8, 6, 8, 8, 6]
            for i in range(6):
                seq.append(("d1", i, unit(S_dn[0], i, dn_nf[i] * 512)))
            for i in range(7):
                seq.append(("in", i, unit(S_in, i, 4096)))
            for i in range(2):
                seq.append(("out", i, unit(S_out, i, 4096)))
            for i in range(2):
                seq.append(("wq", i, unit(S_wq, i, 4096)))
            for i in range(2):
                seq.append(("wo", i, unit(S_wo, i, 4096)))
            for i in range(11):
                seq.append(("gu2", i, unit(S_gu[1], i, 4096)))
            for i in range(6):
                seq.append(("d2", i, unit(S_dn[1], i, dn_nf[i] * 512)))
            return seq

        useq = [("kv", i, U_kv[i]) for i in range(4)]
        for t in range(NT):
            useq += units_tile()
        ustate = {"next_load": 0, "next_use": 0}

        def unit_get():
            v = ustate["next_use"]
            while ustate["next_load"] < len(useq) and ustate["next_load"] <= v + RING - 2:
                u = ustate["next_load"]
                sl = u % RING
                src, n, skey = useq[u][2]
                dma("sp", wring[:, sl, 0:n], src, [skey], [("w", sl)], f"w{sl}")
                ustate["next_load"] += 1
            ustate["next_use"] += 1
            sl = v % RING
            n = useq[v][2][1]
            return wring[:, sl, 0:n], ("w", sl)

        dma("sp", identF[:], C_identF, [], ["identF"], "c0")
        dma("sp", identB[:], C_identB, [], ["identB"], "c0")
        dma("sp", decT[:], C_decT, [], ["decT"], "c0")
        dma("sp", xit[:], C_xit, [], ["xit"], "c0")
        dma("sp", gdec[:], C_gdec, [], ["gdec"], "c0")
        dma("sp", zt[:], C_zt, [], ["zt"], "c0")
        dma("sp", mask[:], C_mask, [], ["mask"], "c0")
        dma("sp", gret[:], Vd["ret_out_norm"].partition_broadcast(128), [], ["gret"], "c0")
        dma("sp", gdiff[:], Vd["diff_out_norm"].partition_broadcast(128), [], ["gdiff"], "c0")
        dma("sp", gfin[:], Vd["final_norm"].partition_broadcast(128), [], ["gfin"], "c0")
        for i, nm in enumerate(["diff_lq1", "diff_lk1", "diff_lq2", "diff_lk2"]):
            dma("sp", lqk[:, i, :], Vd[nm].partition_broadcast(128), [], ["lqk"], "c0")
        for i, nm in enumerate(["ffn1_norm", "mix_norm", "xattn_norm", "ffn2_norm", "mem_norm"]):
            src = Vd[nm].rearrange("o (kc p) -> p (o kc)", p=128)
            dma("sp", gcol[:, i, :], src, [], ["gcol"], "c0", slow=True)
        S.seal("c0")
        S.add("pool", lambda e: e.memset(eps_t[:], EPS), writes=["eps_t"])
        S.add("pool", lambda e: e.memset(Vc[:, :, :, 128:130], 1.0), writes=["Vc_ones"])
        S.add("pool", lambda e: e.memset(memV[:, :, :, 256:258], 1.0), writes=["memV_ones"])
        S.add("pool", lambda e: e.memset(Rm[:], 0.0), writes=["Rm"])
        S.add("pool", lambda e: e.memset(dqz1[0:64, :, :], 0.0), writes=["dqz1"])
        ts("dve", gdiff[:], gdiff[:], 0.8, None, ALU.mult, None, ["gdiff"], ["gdiff"])
        tt("dve", lqk[:, 0, :], lqk[:, 0, :], lqk[:, 1, :], ALU.mult, ["lqk"], ["lqk"])
        tt("dve", lqk[:, 2, :], lqk[:, 2, :], lqk[:, 3, :], ALU.mult, ["lqk"], ["lqk"])
        act(lqk[:, 1, :], lqk[:, 0, :], AF.Identity, ["lqk"], ["lqk", "lam"], accum=lam_s[:, 0:1])
        act(lqk[:, 3, :], lqk[:, 2, :], AF.Identity, ["lqk"], ["lqk", "lam"], accum=lam_s[:, 1:2])
        act(lam_s[:, 2:4], lam_s[:, 0:2], AF.Exp, ["lam"], ["lam"])
        tt("dve", neglam[:], lam_s[:, 3:4], lam_s[:, 2:3], ALU.subtract, ["lam"], ["neglam"])
        ts("dve", neglam[:], neglam[:], -0.2, None, ALU.add, None, ["neglam"], ["neglam"])

        pre_rr = [0]

        def wview(name):
            return Wd_[name].rearrange("(kc p) n -> p kc n", p=128)

        def prepass_cols(dst_unit, name, c0, ncols, off=0):
            dst, n, key = dst_unit
            d = dst[:, off:off + 8 * ncols].rearrange("p (kc n) -> p kc n", kc=8)
            g = pre_rr[0] % NPRE
            pre_rr[0] += 1
            dma("pool", d, wview(name)[:, :, c0:c0 + ncols], [], [key, ("preg", g)], f"pre{g}")

        def prepass_ffn(idx):
            g, u, dn = [("ffn1_w_gate", "ffn1_w_up", "ffn1_w_down"), ("ffn2_w_gate", "ffn2_w_up", "ffn2_w_down")][idx]
            for i in range(11):
                uu = unit(S_gu[idx], i, 4096)
                prepass_cols(uu, g, i * 256, 256, 0)
                prepass_cols(uu, u, i * 256, 256, 2048)
            wd = Wd_[dn].rearrange("(fc p) n -> p fc n", p=128)
            f0s = [0, 8, 16, 0, 8, 16]
            nfs = [8, 8, 6, 8, 8, 6]
            for i in range(6):
                h = i // 3
                dst, n, key = unit(S_dn[idx], i, nfs[i] * 512)
                d = dst.rearrange("p (f n) -> p f n", f=nfs[i])
                g = pre_rr[0] % NPRE
                pre_rr[0] += 1
                dma("pool", d, wd[:, f0s[i]:f0s[i] + nfs[i], h * 512:(h + 1) * 512], [], [key, ("preg", g)], f"pre{g}")

        PRO = int(os.environ.get("PRO", "255"))
        wtmp = arenaA[:, 0:4096].rearrange("p (g two j) -> p g two j", two=2, j=32)
        wtmp_b = hT[:].rearrange("p c t -> p (c t)").rearrange("p (g two j) -> p g two j", two=2, j=32)
        dma("sp", arenaA[:, 0:4096].rearrange("p (kc n) -> p kc n", kc=8), wview("w_in")[:, :, 0:512],
            [], hn_keys, "c1")
        cp("dve", wtmp_b[:, :, 0, :], wtmp[:, :, 1, :], hn_keys, ["hT_all"])
        cp("act", wtmp_b[:, :, 1, :], wtmp[:, :, 0, :], hn_keys, ["hT_all2"])
        urot = unit(S_in, 1, 4096)
        dma("sp", urot[0], hT[:].rearrange("p c t -> p (c t)"), ["hT_all", "hT_all2"], [urot[2]], "c2")
        if PRO & 4:
            for i in range(4):
                prepass_cols(unit(S_kv, i, 4096), "xattn_wkv", i * 512, 512)
        if PRO & 32:
            prepass_ffn(0)
        in_cols = {0: 0, 2: 1536, 3: 2048, 4: 512, 5: 1024, 6: 2560}
        for ui in ([0, 2, 3, 4, 5, 6] if PRO & 8 else []):
            prepass_cols(unit(S_in, ui, 4096), "w_in", in_cols[ui], 512)
        if PRO & 16:
            for i in range(2):
                prepass_cols(unit(S_out, i, 4096), "w_out", i * 512, 512)
            for i in range(2):
                prepass_cols(unit(S_wq, i, 4096), "xattn_wq", i * 512, 512)
            for i in range(2):
                prepass_cols(unit(S_wo, i, 4096), "xattn_wo", i * 512, 512)
            prepass_ffn(1)

        HT_KEYS = [("hT", kc) for kc in range(8)]
        for k in HT_KEYS:
            S.state[k] = [None, {"dma:c2": ("dma", "c2", S.dma_cnt["c2"])}]

        bank_rr = [0]

        def nb(pool=(0, 1, 2, 3, 4, 5, 6, 7)):
            bank_rr[0] += 1
            return pool[bank_rr[0] % len(pool)]

        def norm_stats(src, src_keys, nsub, hb, hk):
            def sq(s):
                act(hb[:, s, :], src[s], AF.Square, [src_keys[s]], [(hk, s), ("ss", s)],
                    scale=1.0 / 32.0, accum=ss[:, s:s + 1])

            def rs(s):
                act(rstd[:, s:s + 1], ss[:, s:s + 1], AF.Ln, [("ss", s), "eps_t"], [("rstd", s)], bias=eps_t[:])
                act(rstd[:, s:s + 1], rstd[:, s:s + 1], AF.Exp, [("rstd", s)], [("rstd", s)], scale=-0.5)
                ts("dve", hb[:, s, :], src[s], rstd[:, s:s + 1], None, ALU.mult, None,
                   [src_keys[s], ("rstd", s)], [(hk, s)])

            sq(0)
            for s in range(nsub):
                if s + 1 < nsub:
                    sq(s + 1)
                rs(s)

        def norm_transposes(nsub, gi, hb, hk, pool=(0, 1, 2, 3, 4, 5, 6, 7)):
            for kc in range(8):
                b = nb(pool)
                for s in range(nsub):
                    tr(ps[b][:, s * 128:(s + 1) * 128], hb[:, s, kc * 128:(kc + 1) * 128], identF[:],
                       [(hk, s), "identF"], [PSK[b]])
                eng = evac_eng()
                n = nsub * 128
                if eng == "dve":
                    ts("dve", hT[:, kc, 0:n], ps[b][:, 0:n], gcol[:, gi, kc:kc + 1], None, ALU.mult, None,
                       [PSK[b], "gcol"], [("hT", kc)])
                else:
                    act(hT[:, kc, 0:n], ps[b][:, 0:n], AF.Identity, [PSK[b], "gcol"], [("hT", kc)],
                        scale=gcol[:, gi, kc:kc + 1])

        def norm_to_hT(src, src_keys, nsub, gi, ncols_tok):
            norm_stats([src[:, s, :] for s in range(nsub)], src_keys, nsub, hn, "hn")
            norm_transposes(nsub, gi, hn, "hn")

        XK = [("x", s) for s in range(NSUB)]

        def ffn(gi, pre_normed=False, resid=None, resid_keys=None, hook_start=None, hook_mid=None):
            if not pre_normed:
                norm_to_hT(x_sb, XK, NSUB, gi, TT)
            if resid is None:
                resid = [x_sb[:, s, :] for s in range(NSUB)]
                resid_keys = XK
            gub = (0, 1, 2, 3)
            for u in range(11):
                slot, wk = unit_get()
                wg = slot[:, 0:2048].rearrange("p (kc n) -> p kc n", kc=8)
                wu = slot[:, 2048:4096].rearrange("p (kc n) -> p kc n", kc=8)
                for fc in range(2):
                    f = 2 * u + fc
                    bg = gub[(2 * f) % 4]
                    bu = gub[(2 * f + 1) % 4]
                    for kc in range(8):
                        mm(ps[bg][:], wg[:, kc, fc * 128:(fc + 1) * 128], hT[:, kc, :], kc == 0, kc == 7,
                           [wk, ("hT", kc)], [PSK[bg]])
                    for kc in range(8):
                        mm(ps[bu][:], wu[:, kc, fc * 128:(fc + 1) * 128], hT[:, kc, :], kc == 0, kc == 7,
                           [wk, ("hT", kc)], [PSK[bu]])
                    sgb = sg[:, f % 2, :]
                    act(sgb, ps[bg][:], AF.Silu, [PSK[bg]], [("sg", f % 2)])
                    tt("dve", actT[:, f, :], sgb, ps[bu][:], ALU.mult, [("sg", f % 2), PSK[bu]], [("actT", f)])
            f0s = [0, 8, 16]
            nfs = [8, 8, 6]
            if hook_start is not None:
                hook_start()
            for h in range(2):
                banks = [(4, 5, 6, 7), (0, 1, 2, 3)][h]
                for g in range(3):
                    slot, wk = unit_get()
                    wd = slot.rearrange("p (f n) -> p f n", f=nfs[g])
                    for s in range(NSUB):
                        b = banks[s]
                        for fi in range(nfs[g]):
                            f = f0s[g] + fi
                            mm(ps[b][:], actT[:, f, s * 128:(s + 1) * 128], wd[:, fi, :],
                               f == 0, f == NF - 1, [wk, ("actT", f)], [PSK[b]])
                    if h == 1 and g == 0 and hook_mid is not None:
                        hook_mid()
                for s in range(NSUB):
                    b = banks[s]
                    xs = x_sb[:, s, h * 512:(h + 1) * 512]
                    stt("dve", xs, ps[b][:], 0.5, resid[s][:, h * 512:(h + 1) * 512], ALU.mult, ALU.add,
                        [PSK[b], resid_keys[s], ("x", s)], [("x", s)])

        def tok_to_hT(src_keys, pool=(0, 1, 2, 3, 4, 5, 6, 7)):
            for kc in range(8):
                b = nb(pool)
                pbv = ps[b][:].bitcast(BF16)
                for s in range(NSUB):
                    tr(pbv[:, s * 128:(s + 1) * 128], tokb[:, s, kc * 128:(kc + 1) * 128], identB[:],
                       [src_keys[s][kc // 4], "identB"], [PSK[b]])
                cp(evac_eng(), hT[:, kc, :], pbv[:, 0:512], [PSK[b]], [("hT", kc)])

        def proj_to_x(src_tag):
            for half in range(2):
                slot, wk = unit_get()
                w = slot.rearrange("p (kc n) -> p kc n", kc=8)
                for s in range(NSUB):
                    b = nb()
                    for kc in range(8):
                        mm(ps[b][:], hT[:, kc, s * 128:(s + 1) * 128], w[:, kc, :], kc == 0, kc == 7,
                           [wk, ("hT", kc)], [PSK[b]])
                    xs = x_sb[:, s, half * 512:(half + 1) * 512]
                    tt("dve", xs, ps[b][:], xs, ALU.add, [PSK[b], ("x", s)], [("x", s)])

        TOKB = [("tokb", s) for s in range(NSUB)]
        TOKB2 = [[("tokb", s, 0), ("tokb", s, 1)] for s in range(NSUB)]

        def mix(t):
            norm_to_hT(x_sb, XK, NSUB, 1, TT)
            dma("sp", cs_sb[:], C_cs[:, :, t * TT:(t + 1) * TT], [], ["cs"], "cs")

            def fm_chunk(w, wk, c, b):
                for kc in range(8):
                    mm(ps[b][:], w[:, kc, c * 128:(c + 1) * 128], hT[:, kc, :], kc == 0, kc == 7,
                       [wk, ("hT", kc)], [PSK[b]])

            s0, k0 = unit_get()
            w0 = s0.rearrange("p (kc n) -> p kc n", kc=8)
            s1, k1 = unit_get()
            w1 = s1.rearrange("p (kc n) -> p kc n", kc=8)
            for c in range(4):
                ba = nb()
                bb = nb()
                fm_chunk(w0, k0, c, ba)
                fm_chunk(w1, k1, c, bb)
                tt("dve", rt1[:, c % 2, :], ps[ba][:], cs_sb[:, 0, :], ALU.mult, [PSK[ba], "cs"], [("rt1", c % 2)])
                tt("dve", rt2[:, c % 2, :], ps[bb][:], cs_sb[:, 1, :], ALU.mult, [PSK[bb], "cs"], [("rt2", c % 2)])
                dst = rqT[:, c, :] if c < 2 else rkT[:, c - 2, :]
                dk_ = ("rqT", c) if c < 2 else ("rkT", c - 2)
                tt(POOL_ELT, dst, rt1[:, c % 2, :], rt2[:, c % 2, :], ALU.add,
                   [("rt1", c % 2), ("rt2", c % 2)], [dk_])
                if c < 2:
                    for cc in range(NSUB):
                        tt(POOL_ELT, qxT[:, c, cc * 128:(cc + 1) * 128], rqT[:, c, cc * 128:(cc + 1) * 128],
                           xit[:, c, :], ALU.mult, [dk_, "xit"], [("qxT", c)])
            S.add("dve", lambda e: e.memset(dqT[64:128, :, :], 0.0), writes=["dqT"])
            s2, k2 = unit_get()
            w2 = s2.rearrange("p (kc n) -> p kc n", kc=8)
            for h in range(4):
                b = nb()
                fm_chunk(w2, k2, h, b)
                cp(evac_eng(), dqT[0:64, h, :], ps[b][0:64, :], [PSK[b]], ["dqT"])
                cp(evac_eng(), dqz1[64:128, h, :], ps[b][64:128, :], [PSK[b]], ["dqz1"])
            s3, k3 = unit_get()
            w3 = s3.rearrange("p (kc n) -> p kc n", kc=8)
            for h in range(4):
                b = nb()
                fm_chunk(w3, k3, h, b)
                cp(evac_eng(), Kc[:, h, t * TT:(t + 1) * TT], ps[b][:], [PSK[b]], [("Kc", h)])

            def tm_group(evac):
                slot, wk = unit_get()
                w = slot.rearrange("p (kc n) -> p kc n", kc=8)
                for s in range(NSUB):
                    b = nb()
                    for kc in range(8):
                        mm(ps[b][:], hT[:, kc, s * 128:(s + 1) * 128], w[:, kc, :], kc == 0, kc == 7,
                           [wk, ("hT", kc)], [PSK[b]])
                    evac(s, b)

            tm_group(lambda s, b: cp(evac_eng(), rv_sb[:, s, :], ps[b][:], [PSK[b]], [("rv", s)]))

            def ev_rg(s, b):
                act(srg[:, s, :], ps[b][:], AF.Silu, [PSK[b]], ["srg"])
                tt(POOL_ELT, srg[:, s, :], srg[:, s, :], gret[:], ALU.mult, ["srg", "gret"], ["srg"])
            tm_group(ev_rg)

            def ev_dv(s, b):
                j = 4 * t + s
                cp(evac_eng(), Vc[:, j, :, 0:128], ps[b][:].rearrange("p (h e) -> p h e", h=4), [PSK[b]], [("Vc", j)])
            tm_group(ev_dv)

            SB4 = (4, 5, 6)

            def ret_gen():
                for c in range(NSUB):
                    pbv = ps[6][:].bitcast(BF16)
                    for hp in range(2):
                        tr(pbv[:, hp * 128:(hp + 1) * 128], rkT[:, hp, c * 128:(c + 1) * 128], identB[:],
                           [("rkT", hp), "identB"], [PSK[6]])
                    tt("dve", kz[:, c, :], pbv[:, 0:256], zt[:], ALU.mult, [PSK[6], "zt"], [("kz", c)])

                    def inner(hh, RB):
                        psI = ps[RB][:, 0:256].rearrange("p (hp n) -> p hp n", hp=2)
                        pr = slice(hh * 64, (hh + 1) * 64)
                        for hp in range(2):
                            mm(psI[:, hp, :], rkT[pr, hp, c * 128:(c + 1) * 128], rqT[pr, hp, c * 128:(c + 1) * 128],
                               True, True, [("rkT", hp), ("rqT", hp)], [PSK[RB]])
                        tt("dve", inT[:, c % 2, hh, :, :], psI, decT[:, hh, :, :], ALU.mult,
                           [PSK[RB], "decT"], [("inT", c % 2, hh)])

                    inner(0, 7)
                    yield
                    inner(1, 6)
                    psU = ps[7][:, 0:256].rearrange("p (hp e) -> p hp e", hp=2)
                    for h in range(4):
                        hp, hh = h // 2, h % 2
                        mm(psU[hh * 64:(hh + 1) * 64, hp, :], kz[:, c, h * 64:(h + 1) * 64],
                           rv_sb[:, c, h * 128:(h + 1) * 128], True, True, [("kz", c), ("rv", c)], [PSK[7]])
                    cp("act", Rb[:, c, :, :], Rm[:], ["Rm"], [("Rb", c)])
                    for hp in range(2):
                        stt("dve", Rm[:, hp, :], Rm[:, hp, :], gdec[:, hp:hp + 1], psU[:, hp, :], ALU.mult, ALU.add,
                            ["Rm", "gdec", PSK[7]], ["Rm"])
                    yield
                    for hh in range(2):
                        RB = 6 + hh
                        psO = ps[RB][:, 0:256].rearrange("p (hp e) -> p hp e", hp=2)
                        pr = slice(hh * 64, (hh + 1) * 64)
                        for hp in range(2):
                            h = 2 * hp + hh
                            mm(psO[:, hp, :], inT[:, c % 2, hh, hp, :], rv_sb[:, c, h * 128:(h + 1) * 128], True, False,
                               [("inT", c % 2, hh), ("rv", c)], [PSK[RB]])
                            mm(psO[:, hp, :], qxT[pr, hp, c * 128:(c + 1) * 128], Rb[pr, c, hp, :], False, True,
                               [("qxT", hp), ("Rb", c)], [PSK[RB]])
                        for hp in range(2):
                            col = hh * 2 + hp
                            act(junkR[:, col, :], psO[:, hp, :], AF.Square, [PSK[RB]], [("junkR", col), ("ssoR", col)],
                                scale=1.0 / math.sqrt(128.0), accum=ssoR[:, col:col + 1])
                        cs2 = slice(hh * 2, hh * 2 + 2)
                        act(rsoR[:, cs2], ssoR[:, cs2], AF.Ln, [("ssoR", hh * 2), ("ssoR", hh * 2 + 1), "eps_t"],
                            [("rsoR", hh)], bias=eps_t[:])
                        act(rsoR[:, cs2], rsoR[:, cs2], AF.Exp, [("rsoR", hh)], [("rsoR", hh)], scale=-0.5)
                        for hp in range(2):
                            h = 2 * hp + hh
                            col = hh * 2 + hp
                            stt("dve", tokb[:, c, h * 128:(h + 1) * 128], psO[:, hp, :], rsoR[:, col:col + 1],
                                srg[:, c, h * 128:(h + 1) * 128], ALU.mult, ALU.mult,
                                [PSK[RB], ("rsoR", hh), "srg"], [("tokb", c, 0)])
                    yield

            rgen = ret_gen()
            ret_left = [3 * NSUB]

            def ret_step():
                if ret_left[0] > 0:
                    next(rgen)
                    ret_left[0] -= 1

            accsets = [(0, 1), (2, 3)]
            pending = []
            flat = []
            rnd = 0
            nk = 4 * t + 4
            n_iter_total = 8 * (2 * t + 4)
            pair_rr = [0]
            stride = max(1, n_iter_total // (3 * NSUB + 2))
            it_count = [0]
            for h in range(4):
                for cc in range(2):
                    bA, bB = accsets[rnd % 2]
                    rnd += 1
                    accA = ps[bA][:, 0:387].rearrange("p (b e) -> p b e", e=129)
                    accB = ps[bB][:, 0:129]
                    firstAB = [True, True]
                    dqz = dqT if cc == 0 else dqz1

                    def qk(item, h=h, cc=cc, dqz=dqz):
                        kjs, pj = item
                        pr_ = pair_rr[0] % 2
                        pair_rr[0] += 1
                        item.append(pr_)
                        for i_, kj in enumerate(kjs):
                            a = max(0, kj - 4 * t)
                            nq = TT - 128 * a
                            bs = 4 + 2 * pr_ + i_
                            mm(ps[bs][:, 0:nq], Kc[:, h, kj * 128:(kj + 1) * 128], dqz[:, h, a * 128:TT], True, True,
                               [("Kc", h), "dqT", "dqz1"], [PSK[bs]])
                        if len(kjs) == 2:
                            act(PT[:, 2 * pr_:2 * pr_ + 2, :], pp[pr_][:].rearrange("p (b t) -> p b t", b=2), AF.Exp,
                                [PSK[4 + 2 * pr_], PSK[5 + 2 * pr_]], [("PT", 2 * pr_), ("PT", 2 * pr_ + 1)], scale=0.125)
                        else:
                            kj = kjs[0]
                            a = max(0, kj - 4 * t)
                            nq = TT - 128 * a
                            bs = 4 + 2 * pr_
                            pi = 2 * pr_
                            act(PT[:, pi, 0:nq], ps[bs][:, 0:nq], AF.Exp, [PSK[bs]], [("PT", pi)], scale=0.125)
                            if kj >= 4 * t:
                                tt(POOL_ELT, PT[:, pi, 0:128], PT[:, pi, 0:128], mask[:], ALU.mult,
                                   [("PT", pi), "mask"], [("PT", pi)])

                    def pv(item, h=h, accA=accA, accB=accB, bA=bA, bB=bB, firstAB=firstAB):
                        kjs, pj, pr_ = item
                        for i_, kj in enumerate(kjs):
                            a = max(0, kj - 4 * t)
                            pi = 2 * pr_ + i_
                            for bq in range(a, 4):
                                lhs = PT[:, pi, (bq - a) * 128:(bq - a + 1) * 128]
                                rhs = Vc[:, kj, h, 0:129]
                                if bq < 3:
                                    mm(accA[:, bq, :], lhs, rhs, firstAB[0], kj == 4 * t + bq,
                                       [("PT", pi), ("Vc", kj), "Vc_ones"], [PSK[bA]], skip=True)
                                    firstAB[0] = False
                                else:
                                    mm(accB, lhs, rhs, firstAB[1], kj == 4 * t + bq,
                                       [("PT", pi), ("Vc", kj), "Vc_ones"], [PSK[bB]], skip=True)
                                    firstAB[1] = False

                    def fin1(h=h, cc=cc, accA=accA, accB=accB, bA=bA, bB=bB):
                        for bq in range(4):
                            acc = accA[:, bq, :] if bq < 3 else accB
                            bk = PSK[bA] if bq < 3 else PSK[bB]
                            rc = rcp[:, bq:bq + 1]
                            S.add("dve", lambda e, acc=acc, rc=rc: e.reciprocal(out=rc, in_=acc[:, 128:129]),
                                  reads=[bk], writes=[("rcp", bq)])
                            if cc == 0:
                                ts("dve", a0[:, bq, :], acc[:, 0:128], rc, None, ALU.mult, None,
                                   [bk, ("rcp", bq)], [("a0", bq)])
                            else:
                                tt("dve", rc, rc, neglam[:], ALU.mult, [("rcp", bq), "neglam"], [("rcp", bq)])
                                stt("dve", ad[:, bq, :], acc[:, 0:128], rc, a0[:, bq, :], ALU.mult, ALU.add,
                                    [bk, ("rcp", bq), ("a0", bq)], [("ad", bq)])

                    def fin2(h=h):
                        for bq in range(4):
                            act(junk[:, bq, :], ad[:, bq, :], AF.Square, [("ad", bq)], [("junk", bq), ("sso", bq)],
                                scale=1.0 / math.sqrt(128.0), accum=sso[:, bq:bq + 1])
                        act(rso[:], sso[:], AF.Ln, [("sso", bq) for bq in range(4)] + ["eps_t"], ["rso"], bias=eps_t[:])
                        act(rso[:], rso[:], AF.Exp, ["rso"], ["rso"], scale=-0.5)

                    def fin3(h=h):
                        for bq in range(4):
                            stt("dve", tokb[:, bq, 512 + h * 128:512 + (h + 1) * 128], ad[:, bq, :], rso[:, bq:bq + 1],
                                gdiff[:, h * 128:(h + 1) * 128], ALU.mult, ALU.mult,
                                [("ad", bq), "rso", "gdiff"], [("tokb", bq, 1)])

                    items = [[[kj, kj + 1], 0] for kj in range(0, 4 * t, 2)] + [[[kj], 0] for kj in range(4 * t, nk)]
                    for ii, it_ in enumerate(items):
                        last = ii == len(items) - 1
                        flat.append((qk, pv, it_, (fin1, fin2, fin3) if (last and cc == 1) else
                                     ((fin1,) if last else ())))
            LA = 1
            for i in range(min(LA, len(flat))):
                flat[i][0](flat[i][2])
            for i in range(len(flat)):
                if i + LA < len(flat):
                    flat[i + LA][0](flat[i + LA][2])
                flat[i][1](flat[i][2])
                if pending:
                    pending.pop(0)()
                for f_ in flat[i][3]:
                    pending.append(f_)
                it_count[0] += 1
                if it_count[0] % stride == 0:
                    ret_step()
            while pending:
                pending.pop(0)()
            while ret_left[0] > 0:
                ret_step()
            if dbg and t == 0:
                dma("pool", DBG[4].rearrange("(s p) d -> p s d", p=128), tokb[:], [k for kk in TOKB2 for k in kk], ["DBG4"], "dbg")
            tok_to_hT(TOKB2, pool=(4, 5, 6, 7))
            proj_to_x("out")

        def xattn():
            norm_to_hT(x_sb, XK, NSUB, 2, TT)
            for half in range(2):
                slot, wk = unit_get()
                w = slot.rearrange("p (kc n) -> p kc n", kc=8)
                for c4 in range(4):
                    dc = half * 4 + c4
                    b = nb()
                    for kc in range(8):
                        mm(ps[b][:], w[:, kc, c4 * 128:(c4 + 1) * 128], hT[:, kc, :], kc == 0, kc == 7,
                           [wk, ("hT", kc)], [PSK[b]])
                    cp(evac_eng(), qT[:, dc, :], ps[b][:], [PSK[b]], ["qT"])
            def xs_scores(h):
                P = PTx if h % 2 == 0 else PTx2
                pk = "PTx" if h % 2 == 0 else "PTx2"
                for mt in range(2):
                    b = nb()
                    for i in range(2):
                        mm(ps[b][:], memKT[:, 2 * h + i, mt * 128:(mt + 1) * 128], qT[:, 2 * h + i, :], i == 0, i == 1,
                           ["memKT", "qT"], [PSK[b]])
                    act(P[:, mt, :], ps[b][:], AF.Exp, [PSK[b]], [pk], scale=1.0 / 16.0)

            def xs_pv(h):
                P = PTx if h % 2 == 0 else PTx2
                pk = "PTx" if h % 2 == 0 else "PTx2"
                for s in range(NSUB):
                    b = nb()
                    for mt in range(2):
                        mm(ps[b][:, 0:257], P[:, mt, s * 128:(s + 1) * 128], memV[:, mt, h, 0:257], mt == 0, mt == 1,
                           [pk, "memV", "memV_ones"], [PSK[b]])
                    rc = rcp[:, 4 + s:5 + s]
                    S.add("dve", lambda e, b=b, rc=rc: e.reciprocal(out=rc, in_=ps[b][:, 256:257]),
                          reads=[PSK[b]], writes=[("rcpx", s)])
                    ts("dve", tokb[:, s, h * 256:(h + 1) * 256], ps[b][:, 0:256], rc, None, ALU.mult, None,
                       [PSK[b], ("rcpx", s)], [("tokb", s, h // 2)])

            xs_scores(0)
            for h in range(4):
                if h + 1 < 4:
                    xs_scores(h + 1)
                xs_pv(h)
            tok_to_hT(TOKB2)
            proj_to_x("wo")

        def mem_kv():
            dma("sp", x_sb[:, 0:2, :], MEM.rearrange("(s p) d -> p s d", p=128), [], [("x", 0), ("x", 1)], "x")
            norm_to_hT(x_sb, XK, 2, 4, NMEM)
            for u in range(2):
                slot, wk = unit_get()
                w = slot.rearrange("p (kc n) -> p kc n", kc=8)
                for c4 in range(4):
                    dc = u * 4 + c4
                    b = nb()
                    for kc in range(8):
                        mm(ps[b][:, 0:NMEM], w[:, kc, c4 * 128:(c4 + 1) * 128], hT[:, kc, 0:NMEM], kc == 0, kc == 7,
                           [wk, ("hT", kc)], [PSK[b]])
                    cp(evac_eng(), memKT[:, dc, :], ps[b][:, 0:NMEM], [PSK[b]], ["memKT"])
            for u in range(2):
                slot, wk = unit_get()
                w = slot.rearrange("p (kc n) -> p kc n", kc=8)
                for mt in range(2):
                    b = nb()
                    for kc in range(8):
                        mm(ps[b][:], hT[:, kc, mt * 128:(mt + 1) * 128], w[:, kc, :], kc == 0, kc == 7,
                           [wk, ("hT", kc)], [PSK[b]])
                    cp(evac_eng(), memV[:, mt, 2 * u:2 * u + 2, 0:256], ps[b][:].rearrange("p (h e) -> p h e", h=2),
                       [PSK[b]], ["memV"])

        def final_norm_store(t):
            for s in range(NSUB):
                act(hn[:, s, :], x_sb[:, s, :], AF.Square, [("x", s)], [("hn", s), ("ss", s)],
                    scale=1.0 / 32.0, accum=ss[:, s:s + 1])
                act(rstd[:, s:s + 1], ss[:, s:s + 1], AF.Ln, [("ss", s), "eps_t"], [("rstd", s)], bias=eps_t[:])
                act(rstd[:, s:s + 1], rstd[:, s:s + 1], AF.Exp, [("rstd", s)], [("rstd", s)], scale=-0.5)
                stt("dve", x_sb[:, s, :], x_sb[:, s, :], rstd[:, s:s + 1], gfin[:], ALU.mult, ALU.mult,
                    [("x", s), ("rstd", s), "gfin"], [("x", s)])
            for s in range(NSUB):
                dma("sp", OUT[t * TT + s * 128:t * TT + (s + 1) * 128, :], x_sb[:, s, :], [XK[s]], [("OUT", s)], f"out{s}")

        def dump(i):
            if dbg:
                dma("sp", DBG[i].rearrange("(s p) d -> p s d", p=128), x_sb[:], XK, ["DBG"], "dbg")

        if stage >= 2:
            mem_kv()
        def load_xn(t):
            for s_ in range(NSUB):
                dma("sp", xn[s_], X[t * TT + s_ * 128:t * TT + (s_ + 1) * 128, :], [], [XN[s_]], f"x{s_}")

        load_xn(0)
        norm_stats(xn, XN, NSUB, hnB, "hnB")
        norm_transposes(NSUB, 0, hnB, "hnB")
        for t in range(NT):
            S.epoch = t + 1
            if stage >= 3:
                ffn(0, pre_normed=True, resid=xn, resid_keys=XN)
            if t == 0:
                dump(0)
            if stage >= 4:
                mix(t)
            if t == 0:
                dump(1)
            if stage >= 5:
                xattn()
            if t == 0:
                dump(2)
            nxt = t + 1 < NT
            if nxt:
                load_xn(t + 1)
            if stage >= 6:
                if nxt:
                    ffn(3, hook_start=lambda: norm_stats(xn, XN, NSUB, hnB, "hnB"),
                        hook_mid=lambda: norm_transposes(NSUB, 0, hnB, "hnB", pool=(4, 5, 6, 7)))
                else:
                    ffn(3)
            if t == 0:
                dump(3)
            final_norm_store(t)
        for s_ in range(NSUB):
            S.wait_final(f"out{s_}")
        if dbg:
            S.wait_final("dbg")
        S.emit(nc)
    return nc


_CACHE = {}


def kernel(**inputs):
    n = 8
    consts = _const_tables()
    if "nc" not in _CACHE:
        _CACHE["nc"] = build(8, False)
    nc = _CACHE["nc"]
    x = np.asarray(inputs["x"], dtype=np.float32)
    mem = np.asarray(inputs["mem"], dtype=np.float32)
    shared = {}
    for nm in W_NAMES:
        shared[nm] = np.ascontiguousarray(np.asarray(inputs[nm], dtype=np.float32)[0])
    for nm in ["ffn1_norm", "mix_norm", "xattn_norm", "ffn2_norm", "mem_norm"]:
        shared[nm] = np.ascontiguousarray(np.asarray(inputs[nm], dtype=np.float32).reshape(1, D))
    shared["final_norm"] = np.ascontiguousarray(np.asarray(inputs["final_norm"], dtype=np.float32).reshape(1, D))
    shared["ret_out_norm"] = np.ascontiguousarray(np.asarray(inputs["ret_out_norm"], dtype=np.float32).reshape(1, 512))
    shared["diff_out_norm"] = np.ascontiguousarray(np.asarray(inputs["diff_out_norm"], dtype=np.float32).reshape(1, 512))
    for nm in ["diff_lq1", "diff_lk1", "diff_lq2", "diff_lk2"]:
        shared[nm] = np.ascontiguousarray(np.asarray(inputs[nm], dtype=np.float32).reshape(1, 64))
    shared.update(consts)
    in_maps = []
    for b in range(n):
        m = dict(shared)
        m["x"] = np.ascontiguousarray(x[b])
        m["mem"] = np.ascontiguousarray(mem[b])
        in_maps.append(m)
    res = run_bass_kernel_spmd(nc, in_maps, core_ids=list(range(n)))
    out = np.stack([np.asarray(r["out"], dtype=np.float32) for r in res.results], axis=0)
    return out
```

```python
import contextlib
import os
import math
import numpy as np
import ml_dtypes
import concourse.bass as bass
import concourse.mybir as mybir
from concourse.bass_utils import run_bass_kernel_spmd

F32 = mybir.dt.float32
BF16 = mybir.dt.bfloat16
ALU = mybir.AluOpType
AF = mybir.ActivationFunctionType

D = 1024
SEQ = 4096
NMEM = 256
DFF = 2816
NF = 22
EPS = 1e-6
TT = 512
NSUB = 4
RING = 4
POOL_ELT = os.environ.get("POOL_ELT", "dve")
NPRE = int(os.environ.get("NPRE", "4"))
SLOT = 4096


class _Op:
    __slots__ = ("eng", "fn", "waits", "flag", "idx", "dma_grp", "dma_cnt", "epoch", "count", "vc")


class Sched:
    ENGS = ("pe", "act", "dve", "pool", "sp")

    def __init__(self):
        self.ops = {e: [] for e in self.ENGS}
        self.state = {}
        self.vc = {e: {} for e in self.ENGS}
        self.dma_cnt = {}
        self.dma_vc = {}
        self.epoch = 0
        self.final_waits = []
        self.alias = {}

    @staticmethod
    def _join(a, b):
        for k, v in b.items():
            if a.get(k, -1) < v:
                a[k] = v

    def add(self, eng, fn, reads=(), writes=(), dma=None):
        op = _Op()
        op.eng = eng
        op.fn = fn
        op.waits = []
        op.flag = False
        op.idx = len(self.ops[eng])
        op.dma_grp = dma
        op.dma_cnt = 0
        op.epoch = self.epoch
        op.count = 0
        if dma is not None:
            self.dma_cnt[dma] = self.dma_cnt.get(dma, 0) + 16
            op.dma_cnt = self.dma_cnt[dma]
            ref = ("dma", dma, op.dma_cnt)
            rkey = "dma:" + dma
        else:
            ref = ("eng", eng, op.idx, op)
            rkey = eng
        if self.alias:
            w2 = list(writes)
            for k in writes:
                if k in self.alias:
                    w2.extend(self.alias[k])
            writes = w2
        for k in reads:
            st = self.state.get(k)
            if st is not None and st[0] is not None:
                self._need(op, st[0])
        for k in writes:
            st = self.state.get(k)
            if st is not None:
                if st[0] is not None:
                    self._need(op, st[0])
                for r in st[1].values():
                    self._need(op, r)
        for k in reads:
            st = self.state.get(k)
            if st is None:
                st = [None, {}]
                self.state[k] = st
            st[1][rkey] = ref
        for k in writes:
            self.state[k] = [ref, {}]
        vc = dict(self.vc[eng])
        if dma is not None:
            prev = self.dma_vc.get((dma, op.dma_cnt - 16))
            if prev is not None:
                self._join(vc, prev)
            vc["dma:" + dma] = op.dma_cnt
            self.dma_vc[(dma, op.dma_cnt)] = vc
        else:
            vc[eng] = op.idx
        op.vc = vc
        self.ops[eng].append(op)
        return op

    def _need(self, op, d):
        evc = self.vc[op.eng]
        if d[0] == "eng":
            X, idx, dop = d[1], d[2], d[3]
            if X == "pe" and op.eng == "pe" and op.dma_grp is None:
                return
            if evc.get(X, -1) >= idx:
                return
            dop.flag = True
            op.waits.append(("eng", X, dop))
            self._join(evc, dop.vc)
        else:
            G, cnt = d[1], d[2]
            key = "dma:" + G
            if evc.get(key, -1) >= cnt:
                return
            op.waits.append(("dma", G, cnt))
            self._join(evc, self.dma_vc[(G, cnt)])

    def seal(self, grp):
        tot = self.dma_cnt[grp]
        for k, stt_ in self.state.items():
            w = stt_[0]
            if w is not None and w[0] == "dma" and w[1] == grp:
                stt_[0] = ("dma", grp, tot)

    def wait_final(self, dma_grp):
        self.final_waits.append(dma_grp)

    def emit(self, nc):
        used = set()
        for e in self.ENGS:
            cnts = {}
            for op in self.ops[e]:
                if op.flag and op.dma_grp is None:
                    cnts[op.epoch] = cnts.get(op.epoch, 0) + 1
                    op.count = cnts[op.epoch]
                    used.add((e, op.epoch))
        with contextlib.ExitStack() as es:
            esem = {}
            for (e, ep) in sorted(used):
                esem[(e, ep)] = es.enter_context(nc.semaphore(f"s_{e}_{ep}"))
            dsem = {}
            for g in sorted(self.dma_cnt):
                dsem[g] = es.enter_context(nc.semaphore(f"d_{g}"))
            block = es.enter_context(nc.Block())

            def run(e, name):
                for op in self.ops[name]:
                    for w in op.waits:
                        if w[0] == "eng":
                            dop = w[2]
                            e.wait_ge(esem[(w[1], dop.epoch)], dop.count)
                        else:
                            e.wait_ge(dsem[w[1]], w[2])
                    ins = op.fn(e)
                    if op.dma_grp is not None:
                        ins.then_inc(dsem[op.dma_grp], 16)
                    elif op.flag:
                        ins.then_inc(esem[(name, op.epoch)], 1)
                if name == "sp":
                    for g in self.final_waits:
                        e.wait_ge(dsem[g], self.dma_cnt[g])

            @block.tensor
            def _(e):
                run(e, "pe")

            @block.scalar
            def _(e):
                run(e, "act")

            @block.vector
            def _(e):
                run(e, "dve")

            @block.gpsimd
            def _(e):
                run(e, "pool")

            @block.sync
            def _(e):
                run(e, "sp")


def _const_tables():
    f32 = np.float32
    c = {}
    c["identF"] = np.eye(128, dtype=f32)
    c["identB"] = np.eye(128, dtype=f32).astype(ml_dtypes.bfloat16)
    inv = (1.0 / (f32(10000.0) ** (np.arange(0, 64, 2, dtype=f32) / f32(64)))).astype(f32)
    pos = np.arange(SEQ, dtype=f32)
    ang = (pos[:, None] * inv[None, :]).astype(f32)
    cos = np.cos(ang).astype(f32)
    sin = np.sin(ang).astype(f32)
    p = np.arange(128)
    cs = np.zeros((128, 2, SEQ), f32)
    cs[:, 0, :] = cos[:, p % 32].T
    sgn = np.where((p % 64) < 32, -1.0, 1.0).astype(f32)
    cs[:, 1, :] = sin[:, p % 32].T * sgn[:, None]
    c["cs"] = cs
    H = 4
    log_g = np.log(1.0 - 2.0 ** (-5.0 - np.arange(H, dtype=np.float64)))
    n = np.arange(128, dtype=np.float64)
    rel = n[None, :] - n[:, None]
    decT = np.zeros((128, 2, 2, 128), f32)
    for h in range(H):
        decT[:, h % 2, h // 2, :] = np.where(rel >= 0, np.exp(np.maximum(rel, 0) * log_g[h]), 0.0) * 0.125
    c["decT"] = decT
    xi = np.exp((n[None, :] + 1.0) * log_g[:, None])
    zeta = np.exp((127.0 - n[None, :]) * log_g[:, None])
    gch = np.exp(128.0 * log_g)
    xit = np.zeros((128, 2, 128), f32)
    gdec = np.zeros((128, 2), f32)
    for hp in range(2):
        for hh in range(2):
            xit[hh * 64:(hh + 1) * 64, hp, :] = xi[2 * hp + hh][None, :] * 0.125
            gdec[hh * 64:(hh + 1) * 64, hp] = gch[2 * hp + hh]
    c["xit"] = xit
    c["gdec"] = gdec
    zt = np.zeros((128, 256), f32)
    for h in range(H):
        zt[:, h * 64:(h + 1) * 64] = zeta[h][:, None]
    c["zt"] = zt
    k = np.arange(128)
    c["mask"] = (k[:, None] <= k[None, :]).astype(f32).astype(ml_dtypes.bfloat16)
    return c


W_NAMES = ["ffn1_w_gate", "ffn1_w_up", "ffn1_w_down", "w_in", "w_out", "xattn_wq", "xattn_wkv",
           "xattn_wo", "ffn2_w_gate", "ffn2_w_up", "ffn2_w_down"]
V_NAMES = ["ffn1_norm", "mix_norm", "xattn_norm", "ffn2_norm", "mem_norm", "final_norm",
           "ret_out_norm", "diff_out_norm", "diff_lq1", "diff_lk1", "diff_lq2", "diff_lk2"]


def build(NT=8, dbg=False, stage=9):
    nc = bass.Bass("TRN2", target_bir_lowering=False)
    S = Sched()
    ntok = NT * TT

    def din(name, shape, dt=F32):
        return nc.dram_tensor(name, list(shape), dt, kind="ExternalInput").ap()

    X = din("x", [SEQ, D])
    MEM = din("mem", [NMEM, D])
    Wd_ = {}
    Wd_["ffn1_w_gate"] = din("ffn1_w_gate", [D, DFF])
    Wd_["ffn1_w_up"] = din("ffn1_w_up", [D, DFF])
    Wd_["ffn1_w_down"] = din("ffn1_w_down", [DFF, D])
    Wd_["w_in"] = din("w_in", [D, 3072])
    Wd_["w_out"] = din("w_out", [D, D])
    Wd_["xattn_wq"] = din("xattn_wq", [D, D])
    Wd_["xattn_wkv"] = din("xattn_wkv", [D, 2 * D])
    Wd_["xattn_wo"] = din("xattn_wo", [D, D])
    Wd_["ffn2_w_gate"] = din("ffn2_w_gate", [D, DFF])
    Wd_["ffn2_w_up"] = din("ffn2_w_up", [D, DFF])
    Wd_["ffn2_w_down"] = din("ffn2_w_down", [DFF, D])
    Vd = {}
    for nm in ["ffn1_norm", "mix_norm", "xattn_norm", "ffn2_norm", "mem_norm", "final_norm"]:
        Vd[nm] = din(nm, [1, D])
    Vd["ret_out_norm"] = din("ret_out_norm", [1, 512])
    Vd["diff_out_norm"] = din("diff_out_norm", [1, 512])
    for nm in ["diff_lq1", "diff_lk1", "diff_lq2", "diff_lk2"]:
        Vd[nm] = din(nm, [1, 64])
    C_identF = din("identF", [128, 128])
    C_identB = din("identB", [128, 128], BF16)
    C_cs = din("cs", [128, 2, SEQ])
    C_decT = din("decT", [128, 2, 2, 128])
    C_xit = din("xit", [128, 2, 128])
    C_gdec = din("gdec", [128, 2])
    C_zt = din("zt", [128, 256])
    C_mask = din("mask", [128, 128], BF16)
    OUT = nc.dram_tensor("out", [SEQ, D], F32, kind="ExternalOutput").ap()
    if dbg:
        DBG = nc.dram_tensor("dbg", [8, 512, D], F32, kind="ExternalOutput").ap()

    def dscratch(name, nu):
        return nc.dram_tensor(name, [nu, 128, SLOT], BF16, kind="Internal").ap()

    S_gu = [dscratch("s_gu1", 11), dscratch("s_gu2", 11)]
    S_dn = [dscratch("s_d1", 6), dscratch("s_d2", 6)]
    S_in = dscratch("s_in", 7)
    S_out = dscratch("s_out", 2)
    S_wq = dscratch("s_wq", 2)
    S_wo = dscratch("s_wo", 2)
    S_kv = dscratch("s_kv", 4)

    with contextlib.ExitStack() as es:
        def sb(name, shape, dt):
            return es.enter_context(nc.sbuf_tensor(name, list(shape), dt))

        x_sb = sb("x_sb", [128, NSUB, D], F32)
        Kc = sb("Kc", [128, 4, SEQ], BF16)
        Vc = sb("Vc", [128, SEQ // 128, 4, 130], BF16)
        wring = sb("wring", [128, RING, SLOT], BF16)
        tokb = sb("tokb", [128, NSUB, D], BF16)
        hT = sb("hT", [128, 8, TT], BF16)
        arenaA = sb("arenaA", [128, 5632], F32)
        hn = arenaA[:, 0:4096].rearrange("p (s d) -> p s d", s=NSUB)
        aA_bf = arenaA[:].bitcast(BF16)
        actT = aA_bf.rearrange("p (f t) -> p f t", f=NF)
        srg = arenaA[:, 0:2048].rearrange("p (s d) -> p s d", s=NSUB)
        rt1 = arenaA[:, 2048:3072].rearrange("p (b t) -> p b t", b=2)
        rt2 = arenaA[:, 3072:4096].rearrange("p (b t) -> p b t", b=2)
        dqT = aA_bf[:, 8192:10240].rearrange("p (h t) -> p h t", h=4)
        qT = aA_bf[:, 0:4096].rearrange("p (c t) -> p c t", c=8)
        PTx = aA_bf[:, 4096:5120].rearrange("p (m t) -> p m t", m=2)
        PTx2 = aA_bf[:, 5120:6144].rearrange("p (m t) -> p m t", m=2)
        sgpt = sb("sgpt", [128, 2 * TT], F32)
        sg = sgpt[:].rearrange("p (b t) -> p b t", b=2)
        rv_sb = sb("rv_sb", [128, NSUB, 512], BF16)
        arenaB = sb("arenaB", [128, 4096], F32)
        aB_bf = arenaB[:].bitcast(BF16)
        rqT = aB_bf[:, 0:1024].rearrange("p (c t) -> p c t", c=2)
        rkT = aB_bf[:, 1024:2048].rearrange("p (c t) -> p c t", c=2)
        qxT = aB_bf[:, 2048:3072].rearrange("p (c t) -> p c t", c=2)
        kz = aB_bf[:, 3072:4096].rearrange("p (c n) -> p c n", c=NSUB)
        inT = aB_bf[:, 4096:5120].rearrange("p (a b c n) -> p a b c n", a=2, b=2, c=2)
        Rb = aB_bf[:, 5120:6144].rearrange("p (c h e) -> p c h e", c=4, h=2)
        a0 = arenaB[:, 3072:3584].rearrange("p (b e) -> p b e", b=4)
        ad = arenaB[:, 3584:4096].rearrange("p (b e) -> p b e", b=4)
        hnB = arenaB[:].rearrange("p (s d) -> p s d", s=NSUB)
        PT = sgpt[:].bitcast(BF16).rearrange("p (b t) -> p b t", b=4)
        dqz1 = sb("dqz1", [128, 4, TT], BF16)
        Rm = sb("Rm", [128, 2, 128], F32)
        junk = sb("junk", [128, 4, 128], BF16)
        cs_sb = sb("cs_sb", [128, 2, TT], F32)
        tokb_flat = tokb[:].rearrange("p s d -> p (s d)")
        xn = [tokb_flat[:, 0:2048].bitcast(F32), tokb_flat[:, 2048:4096].bitcast(F32),
              rv_sb[:].rearrange("p s d -> p (s d)").bitcast(F32), cs_sb[:].rearrange("p a t -> p (a t)")]
        identF = sb("identF_sb", [128, 128], F32)
        identB = sb("identB_sb", [128, 128], BF16)
        decT = sb("decT_sb", [128, 2, 2, 128], F32)
        xit = sb("xit_sb", [128, 2, 128], F32)
        gdec = sb("gdec_sb", [128, 2], F32)
        zt = sb("zt_sb", [128, 256], F32)
        mask = sb("mask_sb", [128, 128], BF16)
        gret = sb("gret", [128, 512], F32)
        gdiff = sb("gdiff", [128, 512], F32)
        gfin = sb("gfin", [128, D], F32)
        gcol = sb("gcol", [128, 5, 8], F32)
        memKT = sb("memKT", [128, 8, NMEM], BF16)
        memV = sb("memV", [128, 2, 4, 258], BF16)
        lqk = sb("lqk", [128, 4, 64], F32)
        st = sb("st", [128, 64], F32)
        junkR = lqk[:].rearrange("p a b -> p (a b)").bitcast(BF16).rearrange("p (h e) -> p h e", h=4)
        eps_t = sb("eps_t", [128, 1], F32)
        neglam = sb("neglam", [128, 1], F32)
        ps = [es.enter_context(nc.psum_tensor(f"ps{i}", [128, 512], F32)) for i in range(4)]
        pp = [es.enter_context(nc.psum_tensor(f"pp{i}", [128, 1024], F32)) for i in range(2)]
        ps = ps + [pp[0][:, 0:512], pp[0][:, 512:1024], pp[1][:, 0:512], pp[1][:, 512:1024]]

        ss = st[:, 0:4]
        rstd = st[:, 4:8]
        sso = st[:, 8:12]
        rso = st[:, 12:16]
        rcp = st[:, 16:24]
        lam_s = st[:, 24:28]
        ssoR = st[:, 28:32]
        rsoR = st[:, 32:36]

        hn_keys = [("hn", s) for s in range(NSUB)]
        act_keys = [("actT", f) for f in range(NF)]
        mixA_keys = ["srg", ("rt1", 0), ("rt1", 1), ("rt2", 0), ("rt2", 1), "dqT"]
        xatA_keys = ["qT", "PTx", "PTx2"]
        fams = [hn_keys, act_keys, mixA_keys, xatA_keys]
        for fam in fams:
            others = [k2 for f2 in fams if f2 is not fam for k2 in f2]
            for k in fam:
                S.alias[k] = others
        for h_ in range(4):
            S.alias[("junkR", h_)] = ["lqk"]
        hnB_keys = [("hnB", s_) for s_ in range(NSUB)]
        tmpB_keys = ([("rqT", i) for i in range(2)] + [("rkT", i) for i in range(2)] + [("qxT", i) for i in range(2)]
                     + [("kz", i) for i in range(4)] + [("inT", i, j) for i in range(2) for j in range(2)]
                     + [("a0", i) for i in range(4)] + [("ad", i) for i in range(4)] + [("Rb", i) for i in range(4)])
        for k in hnB_keys:
            S.alias[k] = tmpB_keys
        for k in tmpB_keys:
            S.alias[k] = hnB_keys
        XN = [("xn", s_) for s_ in range(NSUB)]
        xn_al = {0: [("tokb", 0, 0), ("tokb", 0, 1), ("tokb", 1, 0), ("tokb", 1, 1)],
                 1: [("tokb", 2, 0), ("tokb", 2, 1), ("tokb", 3, 0), ("tokb", 3, 1)],
                 2: [("rv", i) for i in range(4)], 3: ["cs"]}
        for i_, ks in xn_al.items():
            S.alias[XN[i_]] = ks
            for k in ks:
                S.alias[k] = [XN[i_]]
        sg_keys = [("sg", i) for i in range(2)]
        pt_keys = [("PT", i) for i in range(4)]
        for k in sg_keys:
            S.alias[k] = pt_keys
        for k in pt_keys:
            S.alias[k] = sg_keys

        PSK = [("ps", i) for i in range(8)]

        def dma(eng, out, in_, reads, writes, grp, slow=False):
            if slow:
                S.add(eng, lambda e: e.dma_start(out=out, in_=in_, allow_slow_non_contiguous=True),
                      reads=reads, writes=writes, dma=grp)
            else:
                S.add(eng, lambda e: e.dma_start(out=out, in_=in_), reads=reads, writes=writes, dma=grp)

        def mm(out, lhsT, rhs, start, stop, reads, writes, skip=False):
            S.add("pe", lambda e: e.matmul(out, lhsT=lhsT, rhs=rhs, start=start, stop=stop,
                                           skip_group_check=skip), reads=reads, writes=writes)

        def tr(out, in_, ident, reads, writes):
            S.add("pe", lambda e: e.transpose(out=out, in_=in_, identity=ident), reads=reads, writes=writes)

        def act(out, in_, func, reads, writes, scale=1.0, bias=None, accum=None):
            def fn(e):
                kw = {}
                if bias is not None:
                    kw["bias"] = bias
                if accum is not None:
                    kw["accum_out"] = accum
                return e.activation(out=out, in_=in_, func=func, scale=scale, **kw)
            S.add("act", fn, reads=reads, writes=writes)

        def ts(eng, out, in0, s1, s2, op0, op1, reads, writes):
            if s2 is None:
                S.add(eng, lambda e: e.tensor_scalar(out=out, in0=in0, scalar1=s1, scalar2=None, op0=op0),
                      reads=reads, writes=writes)
            else:
                S.add(eng, lambda e: e.tensor_scalar(out=out, in0=in0, scalar1=s1, scalar2=s2, op0=op0, op1=op1),
                      reads=reads, writes=writes)

        def tt(eng, out, in0, in1, op, reads, writes):
            S.add(eng, lambda e: e.tensor_tensor(out=out, in0=in0, in1=in1, op=op), reads=reads, writes=writes)

        def stt(eng, out, in0, scalar, in1, op0, op1, reads, writes):
            S.add(eng, lambda e: e.scalar_tensor_tensor(out=out, in0=in0, scalar=scalar, in1=in1, op0=op0, op1=op1),
                  reads=reads, writes=writes)

        def cp(eng, out, in_, reads, writes):
            if eng == "act":
                S.add("act", lambda e: e.copy(out=out, in_=in_), reads=reads, writes=writes)
            else:
                S.add(eng, lambda e: e.tensor_copy(out=out, in_=in_), reads=reads, writes=writes)

        evac_rr = [0]

        def evac_eng():
            evac_rr[0] += 1
            return "dve" if evac_rr[0] % 2 else "act"

        def unit(sc, ui, n):
            return (sc[ui, :, 0:n], n, ("S", sc.tensor.name, ui))

        U_kv = [unit(S_kv, i, 4096) for i in range(4)]

        def units_tile():
            seq = []
            for i in range(11):
                seq.append(("gu1", i, unit(S_gu[0], i, 4096)))
            dn_nf = [8, 8, 6, 8, 8, 6]
            for i in range(6):
                seq.append(("d1", i, unit(S_dn[0], i, dn_nf[i] * 512)))
            for i in (0, 1, 4, 5, 2, 3, 6):
                seq.append(("in", i, unit(S_in, i, 4096)))
            for i in range(2):
                seq.append(("out", i, unit(S_out, i, 4096)))
            for i in range(2):
                seq.append(("wq", i, unit(S_wq, i, 4096)))
            for i in range(2):
                seq.append(("wo", i, unit(S_wo, i, 4096)))
            for i in range(11):
                seq.append(("gu2", i, unit(S_gu[1], i, 4096)))
            for i in range(6):
                seq.append(("d2", i, unit(S_dn[1], i, dn_nf[i] * 512)))
            return seq

        useq = [("kv", i, U_kv[i]) for i in range(4)]
        for t in range(NT):
            useq += units_tile()
        ustate = {"next_load": 0, "next_use": 0}

        def unit_get():
            v = ustate["next_use"]
            while ustate["next_load"] < len(useq) and ustate["next_load"] <= v + RING - 2:
                u = ustate["next_load"]
                sl = u % RING
                src, n, skey = useq[u][2]
                dma("sp", wring[:, sl, 0:n], src, [skey], [("w", sl)], f"w{sl}")
                ustate["next_load"] += 1
            ustate["next_use"] += 1
            sl = v % RING
            n = useq[v][2][1]
            return wring[:, sl, 0:n], ("w", sl)

        dma("sp", identF[:], C_identF, [], ["identF"], "c0")
        dma("sp", identB[:], C_identB, [], ["identB"], "c0")
        dma("sp", decT[:], C_decT, [], ["decT"], "c0")
        dma("sp", xit[:], C_xit, [], ["xit"], "c0")
        dma("sp", gdec[:], C_gdec, [], ["gdec"], "c0")
        dma("sp", zt[:], C_zt, [], ["zt"], "c0")
        dma("sp", mask[:], C_mask, [], ["mask"], "c0")
        dma("sp", gret[:], Vd["ret_out_norm"].partition_broadcast(128), [], ["gret"], "c0")
        dma("sp", gdiff[:], Vd["diff_out_norm"].partition_broadcast(128), [], ["gdiff"], "c0")
        dma("sp", gfin[:], Vd["final_norm"].partition_broadcast(128), [], ["gfin"], "c0")
        for i, nm in enumerate(["diff_lq1", "diff_lk1", "diff_lq2", "diff_lk2"]):
            dma("sp", lqk[:, i, :], Vd[nm].partition_broadcast(128), [], ["lqk"], "c0")
        for i, nm in enumerate(["ffn1_norm", "mix_norm", "xattn_norm", "ffn2_norm", "mem_norm"]):
            src = Vd[nm].rearrange("o (kc p) -> p (o kc)", p=128)
            dma("sp", gcol[:, i, :], src, [], ["gcol"], "c0", slow=True)
        S.seal("c0")
        S.add("pool", lambda e: e.memset(eps_t[:], EPS), writes=["eps_t"])
        S.add("pool", lambda e: e.memset(Vc[:, :, :, 128:130], 1.0), writes=["Vc_ones"])
        S.add("pool", lambda e: e.memset(memV[:, :, :, 256:258], 1.0), writes=["memV_ones"])
        S.add("pool", lambda e: e.memset(Rm[:], 0.0), writes=["Rm"])
        S.add("pool", lambda e: e.memset(dqz1[0:64, :, :], 0.0), writes=["dqz1"])
        ts("dve", gdiff[:], gdiff[:], 0.8, None, ALU.mult, None, ["gdiff"], ["gdiff"])
        tt("dve", lqk[:, 0, :], lqk[:, 0, :], lqk[:, 1, :], ALU.mult, ["lqk"], ["lqk"])
        tt("dve", lqk[:, 2, :], lqk[:, 2, :], lqk[:, 3, :], ALU.mult, ["lqk"], ["lqk"])
        act(lqk[:, 1, :], lqk[:, 0, :], AF.Identity, ["lqk"], ["lqk", "lam"], accum=lam_s[:, 0:1])
        act(lqk[:, 3, :], lqk[:, 2, :], AF.Identity, ["lqk"], ["lqk", "lam"], accum=lam_s[:, 1:2])
        act(lam_s[:, 2:4], lam_s[:, 0:2], AF.Exp, ["lam"], ["lam"])
        tt("dve", neglam[:], lam_s[:, 3:4], lam_s[:, 2:3], ALU.subtract, ["lam"], ["neglam"])
        ts("dve", neglam[:], neglam[:], -0.2, None, ALU.add, None, ["neglam"], ["neglam"])

        pre_rr = [0]

        def wview(name):
            return Wd_[name].rearrange("(kc p) n -> p kc n", p=128)

        def prepass_cols(dst_unit, name, c0, ncols, off=0):
            dst, n, key = dst_unit
            d = dst[:, off:off + 8 * ncols].rearrange("p (kc n) -> p kc n", kc=8)
            g = pre_rr[0] % NPRE
            pre_rr[0] += 1
            dma("pool", d, wview(name)[:, :, c0:c0 + ncols], [], [key, ("preg", g)], f"pre{g}")

        def prepass_ffn(idx):
            g, u, dn = [("ffn1_w_gate", "ffn1_w_up", "ffn1_w_down"), ("ffn2_w_gate", "ffn2_w_up", "ffn2_w_down")][idx]
            for i in range(11):
                uu = unit(S_gu[idx], i, 4096)
                prepass_cols(uu, g, i * 256, 256, 0)
                prepass_cols(uu, u, i * 256, 256, 2048)
            wd = Wd_[dn].rearrange("(fc p) n -> p fc n", p=128)
            f0s = [0, 8, 16, 0, 8, 16]
            nfs = [8, 8, 6, 8, 8, 6]
            for i in range(6):
                h = i // 3
                dst, n, key = unit(S_dn[idx], i, nfs[i] * 512)
                d = dst.rearrange("p (f n) -> p f n", f=nfs[i])
                g = pre_rr[0] % NPRE
                pre_rr[0] += 1
                dma("pool", d, wd[:, f0s[i]:f0s[i] + nfs[i], h * 512:(h + 1) * 512], [], [key, ("preg", g)], f"pre{g}")

        PRO = int(os.environ.get("PRO", "255"))
        wtmp = arenaA[:, 0:4096].rearrange("p (g two j) -> p g two j", two=2, j=32)
        wtmp_b = hT[:].rearrange("p c t -> p (c t)").rearrange("p (g two j) -> p g two j", two=2, j=32)
        dma("sp", arenaA[:, 0:4096].rearrange("p (kc n) -> p kc n", kc=8), wview("w_in")[:, :, 0:512],
            [], hn_keys, "c1")
        cp("dve", wtmp_b[:, :, 0, :], wtmp[:, :, 1, :], hn_keys, ["hT_all"])
        cp("act", wtmp_b[:, :, 1, :], wtmp[:, :, 0, :], hn_keys, ["hT_all2"])
        urot = unit(S_in, 1, 4096)
        dma("sp", urot[0], hT[:].rearrange("p c t -> p (c t)"), ["hT_all", "hT_all2"], [urot[2]], "c2")
        if PRO & 4:
            for i in range(4):
                prepass_cols(unit(S_kv, i, 4096), "xattn_wkv", i * 512, 512)
        if PRO & 32:
            prepass_ffn(0)
        in_cols = {0: 0, 2: 1536, 3: 2048, 4: 512, 5: 1024, 6: 2560}
        for ui in ([0, 4, 5, 2, 3, 6] if PRO & 8 else []):
            prepass_cols(unit(S_in, ui, 4096), "w_in", in_cols[ui], 512)
        if PRO & 16:
            for i in range(2):
                prepass_cols(unit(S_out, i, 4096), "w_out", i * 512, 512)
            for i in range(2):
                prepass_cols(unit(S_wq, i, 4096), "xattn_wq", i * 512, 512)
            for i in range(2):
                prepass_cols(unit(S_wo, i, 4096), "xattn_wo", i * 512, 512)
            prepass_ffn(1)

        HT_KEYS = [("hT", kc) for kc in range(8)]
        for k in HT_KEYS:
            S.state[k] = [None, {"dma:c2": ("dma", "c2", S.dma_cnt["c2"])}]

        bank_rr = [0]

        def nb(pool=(0, 1, 2, 3, 4, 5, 6, 7)):
            bank_rr[0] += 1
            return pool[bank_rr[0] % len(pool)]

        def norm_stats(src, src_keys, nsub, hb, hk):
            def sq(s):
                act(hb[:, s, :], src[s], AF.Square, [src_keys[s]], [(hk, s), ("ss", s)],
                    scale=1.0 / 32.0, accum=ss[:, s:s + 1])

            def rs(s):
                act(rstd[:, s:s + 1], ss[:, s:s + 1], AF.Ln, [("ss", s), "eps_t"], [("rstd", s)], bias=eps_t[:])
                act(rstd[:, s:s + 1], rstd[:, s:s + 1], AF.Exp, [("rstd", s)], [("rstd", s)], scale=-0.5)
                ts("dve", hb[:, s, :], src[s], rstd[:, s:s + 1], None, ALU.mult, None,
                   [src_keys[s], ("rstd", s)], [(hk, s)])

            sq(0)
            for s in range(nsub):
                if s + 1 < nsub:
                    sq(s + 1)
                rs(s)

        def norm_transposes(nsub, gi, hb, hk, pool=(0, 1, 2, 3, 4, 5, 6, 7)):
            for kc in range(8):
                b = nb(pool)
                for s in range(nsub):
                    tr(ps[b][:, s * 128:(s + 1) * 128], hb[:, s, kc * 128:(kc + 1) * 128], identF[:],
                       [(hk, s), "identF"], [PSK[b]])
                eng = evac_eng()
                n = nsub * 128
                if eng == "dve":
                    ts("dve", hT[:, kc, 0:n], ps[b][:, 0:n], gcol[:, gi, kc:kc + 1], None, ALU.mult, None,
                       [PSK[b], "gcol"], [("hT", kc)])
                else:
                    act(hT[:, kc, 0:n], ps[b][:, 0:n], AF.Identity, [PSK[b], "gcol"], [("hT", kc)],
                        scale=gcol[:, gi, kc:kc + 1])

        def norm_to_hT(src, src_keys, nsub, gi, ncols_tok):
            norm_stats([src[:, s, :] for s in range(nsub)], src_keys, nsub, hn, "hn")
            norm_transposes(nsub, gi, hn, "hn")

        XK = [("x", s) for s in range(NSUB)]

        def ffn(gi, pre_normed=False, resid=None, resid_keys=None, hook_start=None, hook_mid=None):
            if not pre_normed:
                norm_to_hT(x_sb, XK, NSUB, gi, TT)
            if resid is None:
                resid = [x_sb[:, s, :] for s in range(NSUB)]
                resid_keys = XK
            gub = (0, 1, 2, 3)
            for u in range(11):
                slot, wk = unit_get()
                wg = slot[:, 0:2048].rearrange("p (kc n) -> p kc n", kc=8)
                wu = slot[:, 2048:4096].rearrange("p (kc n) -> p kc n", kc=8)
                for fc in range(2):
                    f = 2 * u + fc
                    bg = gub[(2 * f) % 4]
                    bu = gub[(2 * f + 1) % 4]
                    for kc in range(8):
                        mm(ps[bg][:], wg[:, kc, fc * 128:(fc + 1) * 128], hT[:, kc, :], kc == 0, kc == 7,
                           [wk, ("hT", kc)], [PSK[bg]])
                    for kc in range(8):
                        mm(ps[bu][:], wu[:, kc, fc * 128:(fc + 1) * 128], hT[:, kc, :], kc == 0, kc == 7,
                           [wk, ("hT", kc)], [PSK[bu]])
                    sgb = sg[:, f % 2, :]
                    act(sgb, ps[bg][:], AF.Silu, [PSK[bg]], [("sg", f % 2)])
                    tt("dve", actT[:, f, :], sgb, ps[bu][:], ALU.mult, [("sg", f % 2), PSK[bu]], [("actT", f)])
            f0s = [0, 8, 16]
            nfs = [8, 8, 6]
            if hook_start is not None:
                hook_start()
            for h in range(2):
                banks = [(4, 5, 6, 7), (0, 1, 2, 3)][h]
                for g in range(3):
                    slot, wk = unit_get()
                    wd = slot.rearrange("p (f n) -> p f n", f=nfs[g])
                    for s in range(NSUB):
                        b = banks[s]
                        for fi in range(nfs[g]):
                            f = f0s[g] + fi
                            mm(ps[b][:], actT[:, f, s * 128:(s + 1) * 128], wd[:, fi, :],
                               f == 0, f == NF - 1, [wk, ("actT", f)], [PSK[b]])
                    if h == 1 and g == 0 and hook_mid is not None:
                        hook_mid()
                for s in range(NSUB):
                    b = banks[s]
                    xs = x_sb[:, s, h * 512:(h + 1) * 512]
                    stt("dve", xs, ps[b][:], 0.5, resid[s][:, h * 512:(h + 1) * 512], ALU.mult, ALU.add,
                        [PSK[b], resid_keys[s], ("x", s)], [("x", s)])

        def tok_to_hT(src_keys, pool=(0, 1, 2, 3, 4, 5, 6, 7)):
            for kc in range(8):
                b = nb(pool)
                pbv = ps[b][:].bitcast(BF16)
                for s in range(NSUB):
                    tr(pbv[:, s * 128:(s + 1) * 128], tokb[:, s, kc * 128:(kc + 1) * 128], identB[:],
                       [src_keys[s][kc // 4], "identB"], [PSK[b]])
                cp(evac_eng(), hT[:, kc, :], pbv[:, 0:512], [PSK[b]], [("hT", kc)])

        def proj_to_x(src_tag):
            for half in range(2):
                slot, wk = unit_get()
                w = slot.rearrange("p (kc n) -> p kc n", kc=8)
                for s in range(NSUB):
                    b = nb()
                    for kc in range(8):
                        mm(ps[b][:], hT[:, kc, s * 128:(s + 1) * 128], w[:, kc, :], kc == 0, kc == 7,
                           [wk, ("hT", kc)], [PSK[b]])
                    xs = x_sb[:, s, half * 512:(half + 1) * 512]
                    tt("dve", xs, ps[b][:], xs, ALU.add, [PSK[b], ("x", s)], [("x", s)])

        TOKB = [("tokb", s) for s in range(NSUB)]
        TOKB2 = [[("tokb", s, 0), ("tokb", s, 1)] for s in range(NSUB)]

        def mix(t):
            norm_to_hT(x_sb, XK, NSUB, 1, TT)
            dma("sp", cs_sb[:], C_cs[:, :, t * TT:(t + 1) * TT], [], ["cs"], "cs")

            def fm_chunk(w, wk, c, b):
                for kc in range(8):
                    mm(ps[b][:], w[:, kc, c * 128:(c + 1) * 128], hT[:, kc, :], kc == 0, kc == 7,
                       [wk, ("hT", kc)], [PSK[b]])

            s0, k0 = unit_get()
            w0 = s0.rearrange("p (kc n) -> p kc n", kc=8)
            s1, k1 = unit_get()
            w1 = s1.rearrange("p (kc n) -> p kc n", kc=8)
            for c in range(4):
                ba = nb()
                bb = nb()
                fm_chunk(w0, k0, c, ba)
                fm_chunk(w1, k1, c, bb)
                tt("dve", rt1[:, c % 2, :], ps[ba][:], cs_sb[:, 0, :], ALU.mult, [PSK[ba], "cs"], [("rt1", c % 2)])
                tt("dve", rt2[:, c % 2, :], ps[bb][:], cs_sb[:, 1, :], ALU.mult, [PSK[bb], "cs"], [("rt2", c % 2)])
                dst = rqT[:, c, :] if c < 2 else rkT[:, c - 2, :]
                dk_ = ("rqT", c) if c < 2 else ("rkT", c - 2)
                tt(POOL_ELT, dst, rt1[:, c % 2, :], rt2[:, c % 2, :], ALU.add,
                   [("rt1", c % 2), ("rt2", c % 2)], [dk_])
                if c < 2:
                    for cc in range(NSUB):
                        tt(POOL_ELT, qxT[:, c, cc * 128:(cc + 1) * 128], rqT[:, c, cc * 128:(cc + 1) * 128],
                           xit[:, c, :], ALU.mult, [dk_, "xit"], [("qxT", c)])
            def ret_gen():
                for c in range(NSUB):
                    bK = nb()
                    pbv = ps[bK][:].bitcast(BF16)
                    for hp in range(2):
                        tr(pbv[:, hp * 128:(hp + 1) * 128], rkT[:, hp, c * 128:(c + 1) * 128], identB[:],
                           [("rkT", hp), "identB"], [PSK[bK]])
                    tt("dve", kz[:, c, :], pbv[:, 0:256], zt[:], ALU.mult, [PSK[bK], "zt"], [("kz", c)])

                    def inner(hh, RB):
                        psI = ps[RB][:, 0:256].rearrange("p (hp n) -> p hp n", hp=2)
                        pr = slice(hh * 64, (hh + 1) * 64)
                        for hp in range(2):
                            mm(psI[:, hp, :], rkT[pr, hp, c * 128:(c + 1) * 128], rqT[pr, hp, c * 128:(c + 1) * 128],
                               True, True, [("rkT", hp), ("rqT", hp)], [PSK[RB]])
                        tt("dve", inT[:, c % 2, hh, :, :], psI, decT[:, hh, :, :], ALU.mult,
                           [PSK[RB], "decT"], [("inT", c % 2, hh)])

                    inner(0, nb())
                    yield
                    inner(1, nb())
                    bU = nb()
                    psU = ps[bU][:, 0:256].rearrange("p (hp e) -> p hp e", hp=2)
                    for h in range(4):
                        hp, hh = h // 2, h % 2
                        mm(psU[hh * 64:(hh + 1) * 64, hp, :], kz[:, c, h * 64:(h + 1) * 64],
                           rv_sb[:, c, h * 128:(h + 1) * 128], True, True, [("kz", c), ("rv", c)], [PSK[bU]])
                    cp("act", Rb[:, c, :, :], Rm[:], ["Rm"], [("Rb", c)])
                    for hp in range(2):
                        stt("dve", Rm[:, hp, :], Rm[:, hp, :], gdec[:, hp:hp + 1], psU[:, hp, :], ALU.mult, ALU.add,
                            ["Rm", "gdec", PSK[bU]], ["Rm"])
                    yield
                    for hh in range(2):
                        RB = nb()
                        psO = ps[RB][:, 0:256].rearrange("p (hp e) -> p hp e", hp=2)
                        pr = slice(hh * 64, (hh + 1) * 64)
                        for hp in range(2):
                            h = 2 * hp + hh
                            mm(psO[:, hp, :], inT[:, c % 2, hh, hp, :], rv_sb[:, c, h * 128:(h + 1) * 128], True, False,
                               [("inT", c % 2, hh), ("rv", c)], [PSK[RB]])
                            mm(psO[:, hp, :], qxT[pr, hp, c * 128:(c + 1) * 128], Rb[pr, c, hp, :], False, True,
                               [("qxT", hp), ("Rb", c)], [PSK[RB]])
                        for hp in range(2):
                            col = hh * 2 + hp
                            act(junkR[:, col, :], psO[:, hp, :], AF.Square, [PSK[RB]], [("junkR", col), ("ssoR", col)],
                                scale=1.0 / math.sqrt(128.0), accum=ssoR[:, col:col + 1])
                        cs2 = slice(hh * 2, hh * 2 + 2)
                        act(rsoR[:, cs2], ssoR[:, cs2], AF.Ln, [("ssoR", hh * 2), ("ssoR", hh * 2 + 1), "eps_t"],
                            [("rsoR", hh)], bias=eps_t[:])
                        act(rsoR[:, cs2], rsoR[:, cs2], AF.Exp, [("rsoR", hh)], [("rsoR", hh)], scale=-0.5)
                        for hp in range(2):
                            h = 2 * hp + hh
                            col = hh * 2 + hp
                            stt("dve", tokb[:, c, h * 128:(h + 1) * 128], psO[:, hp, :], rsoR[:, col:col + 1],
                                srg[:, c, h * 128:(h + 1) * 128], ALU.mult, ALU.mult,
                                [PSK[RB], ("rsoR", hh), "srg"], [("tokb", c, 0)])
                    yield

            def tm_group(evac, after=None):
                slot, wk = unit_get()
                w = slot.rearrange("p (kc n) -> p kc n", kc=8)
                for s in range(NSUB):
                    b = nb()
                    for kc in range(8):
                        mm(ps[b][:], hT[:, kc, s * 128:(s + 1) * 128], w[:, kc, :], kc == 0, kc == 7,
                           [wk, ("hT", kc)], [PSK[b]])
                    evac(s, b)
                    if after is not None:
                        after()

            tm_group(lambda s, b: cp(evac_eng(), rv_sb[:, s, :], ps[b][:], [PSK[b]], [("rv", s)]))

            def ev_rg(s, b):
                act(srg[:, s, :], ps[b][:], AF.Silu, [PSK[b]], ["srg"])
                tt(POOL_ELT, srg[:, s, :], srg[:, s, :], gret[:], ALU.mult, ["srg", "gret"], ["srg"])
            tm_group(ev_rg)

            rgen = ret_gen()
            ret_left = [3 * NSUB]

            def ret_step():
                if ret_left[0] > 0:
                    next(rgen)
                    ret_left[0] -= 1

            S.add("dve", lambda e: e.memset(dqT[64:128, :, :], 0.0), writes=["dqT"])
            s2, k2 = unit_get()
            w2 = s2.rearrange("p (kc n) -> p kc n", kc=8)
            for h in range(4):
                b = nb()
                fm_chunk(w2, k2, h, b)
                cp(evac_eng(), dqT[0:64, h, :], ps[b][0:64, :], [PSK[b]], ["dqT"])
                cp(evac_eng(), dqz1[64:128, h, :], ps[b][64:128, :], [PSK[b]], ["dqz1"])
                ret_step()
            s3, k3 = unit_get()
            w3 = s3.rearrange("p (kc n) -> p kc n", kc=8)
            for h in range(4):
                b = nb()
                fm_chunk(w3, k3, h, b)
                cp(evac_eng(), Kc[:, h, t * TT:(t + 1) * TT], ps[b][:], [PSK[b]], [("Kc", h)])
                ret_step()

            def ev_dv(s, b):
                j = 4 * t + s
                cp(evac_eng(), Vc[:, j, :, 0:128], ps[b][:].rearrange("p (h e) -> p h e", h=4), [PSK[b]], [("Vc", j)])
            tm_group(ev_dv, after=ret_step)
            while ret_left[0] > 0:
                ret_step()

            SB4 = (4, 5, 6)

            accsets = [(0, 1), (2, 3)]
            pending = []
            flat = []
            rnd = 0
            nk = 4 * t + 4
            n_iter_total = 8 * (2 * t + 4)
            pair_rr = [0]
            stride = max(1, n_iter_total // (3 * NSUB + 2))
            it_count = [0]
            for h in range(4):
                for cc in range(2):
                    bA, bB = accsets[rnd % 2]
                    rnd += 1
                    accA = ps[bA][:, 0:387].rearrange("p (b e) -> p b e", e=129)
                    accB = ps[bB][:, 0:129]
                    firstAB = [True, True]
                    dqz = dqT if cc == 0 else dqz1

                    def qk(item, h=h, cc=cc, dqz=dqz):
                        kjs, pj = item
                        pr_ = pair_rr[0] % 2
                        pair_rr[0] += 1
                        item.append(pr_)
                        for i_, kj in enumerate(kjs):
                            a = max(0, kj - 4 * t)
                            nq = TT - 128 * a
                            bs = 4 + 2 * pr_ + i_
                            mm(ps[bs][:, 0:nq], Kc[:, h, kj * 128:(kj + 1) * 128], dqz[:, h, a * 128:TT], True, True,
                               [("Kc", h), "dqT", "dqz1"], [PSK[bs]])
                        if len(kjs) == 2:
                            act(PT[:, 2 * pr_:2 * pr_ + 2, :], pp[pr_][:].rearrange("p (b t) -> p b t", b=2), AF.Exp,
                                [PSK[4 + 2 * pr_], PSK[5 + 2 * pr_]], [("PT", 2 * pr_), ("PT", 2 * pr_ + 1)], scale=0.125)
                        else:
                            kj = kjs[0]
                            a = max(0, kj - 4 * t)
                            nq = TT - 128 * a
                            bs = 4 + 2 * pr_
                            pi = 2 * pr_
                            act(PT[:, pi, 0:nq], ps[bs][:, 0:nq], AF.Exp, [PSK[bs]], [("PT", pi)], scale=0.125)
                            if kj >= 4 * t:
                                tt(POOL_ELT, PT[:, pi, 0:128], PT[:, pi, 0:128], mask[:], ALU.mult,
                                   [("PT", pi), "mask"], [("PT", pi)])

                    def pv(item, h=h, accA=accA, accB=accB, bA=bA, bB=bB, firstAB=firstAB):
                        kjs, pj, pr_ = item
                        for i_, kj in enumerate(kjs):
                            a = max(0, kj - 4 * t)
                            pi = 2 * pr_ + i_
                            for bq in range(a, 4):
                                lhs = PT[:, pi, (bq - a) * 128:(bq - a + 1) * 128]
                                rhs = Vc[:, kj, h, 0:129]
                                if bq < 3:
                                    mm(accA[:, bq, :], lhs, rhs, firstAB[0], kj == 4 * t + bq,
                                       [("PT", pi), ("Vc", kj), "Vc_ones"], [PSK[bA]], skip=True)
                                    firstAB[0] = False
                                else:
                                    mm(accB, lhs, rhs, firstAB[1], kj == 4 * t + bq,
                                       [("PT", pi), ("Vc", kj), "Vc_ones"], [PSK[bB]], skip=True)
                                    firstAB[1] = False

                    def fin1(h=h, cc=cc, accA=accA, accB=accB, bA=bA, bB=bB):
                        for bq in range(4):
                            acc = accA[:, bq, :] if bq < 3 else accB
                            bk = PSK[bA] if bq < 3 else PSK[bB]
                            rc = rcp[:, bq:bq + 1]
                            S.add("dve", lambda e, acc=acc, rc=rc: e.reciprocal(out=rc, in_=acc[:, 128:129]),
                                  reads=[bk], writes=[("rcp", bq)])
                            if cc == 0:
                                ts("dve", a0[:, bq, :], acc[:, 0:128], rc, None, ALU.mult, None,
                                   [bk, ("rcp", bq)], [("a0", bq)])
                            else:
                                tt("dve", rc, rc, neglam[:], ALU.mult, [("rcp", bq), "neglam"], [("rcp", bq)])
                                stt("dve", ad[:, bq, :], acc[:, 0:128], rc, a0[:, bq, :], ALU.mult, ALU.add,
                                    [bk, ("rcp", bq), ("a0", bq)], [("ad", bq)])

                    def fin2(h=h):
                        for bq in range(4):
                            act(junk[:, bq, :], ad[:, bq, :], AF.Square, [("ad", bq)], [("junk", bq), ("sso", bq)],
                                scale=1.0 / math.sqrt(128.0), accum=sso[:, bq:bq + 1])
                        act(rso[:], sso[:], AF.Ln, [("sso", bq) for bq in range(4)] + ["eps_t"], ["rso"], bias=eps_t[:])
                        act(rso[:], rso[:], AF.Exp, ["rso"], ["rso"], scale=-0.5)

                    def fin3(h=h):
                        for bq in range(4):
                            stt("dve", tokb[:, bq, 512 + h * 128:512 + (h + 1) * 128], ad[:, bq, :], rso[:, bq:bq + 1],
                                gdiff[:, h * 128:(h + 1) * 128], ALU.mult, ALU.mult,
                                [("ad", bq), "rso", "gdiff"], [("tokb", bq, 1)])

                    items = [[[kj, kj + 1], 0] for kj in range(0, 4 * t, 2)] + [[[kj], 0] for kj in range(4 * t, nk)]
                    for ii, it_ in enumerate(items):
                        last = ii == len(items) - 1
                        flat.append((qk, pv, it_, (fin1, fin2, fin3) if (last and cc == 1) else
                                     ((fin1,) if last else ())))
            LA = 1
            for i in range(min(LA, len(flat))):
                flat[i][0](flat[i][2])
            for i in range(len(flat)):
                if i + LA < len(flat):
                    flat[i + LA][0](flat[i + LA][2])
                flat[i][1](flat[i][2])
                if pending:
                    pending.pop(0)()
                for f_ in flat[i][3]:
                    pending.append(f_)
            while pending:
                pending.pop(0)()
            if dbg and t == 0:
                dma("pool", DBG[4].rearrange("(s p) d -> p s d", p=128), tokb[:], [k for kk in TOKB2 for k in kk], ["DBG4"], "dbg")
            tok_to_hT(TOKB2, pool=(4, 5, 6, 7))
            proj_to_x("out")

        def xattn():
            norm_to_hT(x_sb, XK, NSUB, 2, TT)
            for half in range(2):
                slot, wk = unit_get()
                w = slot.rearrange("p (kc n) -> p kc n", kc=8)
                for c4 in range(4):
                    dc = half * 4 + c4
                    b = nb()
                    for kc in range(8):
                        mm(ps[b][:], w[:, kc, c4 * 128:(c4 + 1) * 128], hT[:, kc, :], kc == 0, kc == 7,
                           [wk, ("hT", kc)], [PSK[b]])
                    cp(evac_eng(), qT[:, dc, :], ps[b][:], [PSK[b]], ["qT"])
            def xs_scores(h):
                P = PTx if h % 2 == 0 else PTx2
                pk = "PTx" if h % 2 == 0 else "PTx2"
                for mt in range(2):
                    b = nb()
                    for i in range(2):
                        mm(ps[b][:], memKT[:, 2 * h + i, mt * 128:(mt + 1) * 128], qT[:, 2 * h + i, :], i == 0, i == 1,
                           ["memKT", "qT"], [PSK[b]])
                    act(P[:, mt, :], ps[b][:], AF.Exp, [PSK[b]], [pk], scale=1.0 / 16.0)

            def xs_pv(h):
                P = PTx if h % 2 == 0 else PTx2
                pk = "PTx" if h % 2 == 0 else "PTx2"
                for s in range(NSUB):
                    b = nb()
                    for mt in range(2):
                        mm(ps[b][:, 0:257], P[:, mt, s * 128:(s + 1) * 128], memV[:, mt, h, 0:257], mt == 0, mt == 1,
                           [pk, "memV", "memV_ones"], [PSK[b]])
                    rc = rcp[:, 4 + s:5 + s]
                    S.add("dve", lambda e, b=b, rc=rc: e.reciprocal(out=rc, in_=ps[b][:, 256:257]),
                          reads=[PSK[b]], writes=[("rcpx", s)])
                    ts("dve", tokb[:, s, h * 256:(h + 1) * 256], ps[b][:, 0:256], rc, None, ALU.mult, None,
                       [PSK[b], ("rcpx", s)], [("tokb", s, h // 2)])

            xs_scores(0)
            for h in range(4):
                if h + 1 < 4:
                    xs_scores(h + 1)
                xs_pv(h)
            tok_to_hT(TOKB2)
            proj_to_x("wo")

        def mem_kv():
            dma("sp", x_sb[:, 0:2, :], MEM.rearrange("(s p) d -> p s d", p=128), [], [("x", 0), ("x", 1)], "x")
            norm_to_hT(x_sb, XK, 2, 4, NMEM)
            for u in range(2):
                slot, wk = unit_get()
                w = slot.rearrange("p (kc n) -> p kc n", kc=8)
                for c4 in range(4):
                    dc = u * 4 + c4
                    b = nb()
                    for kc in range(8):
                        mm(ps[b][:, 0:NMEM], w[:, kc, c4 * 128:(c4 + 1) * 128], hT[:, kc, 0:NMEM], kc == 0, kc == 7,
                           [wk, ("hT", kc)], [PSK[b]])
                    cp(evac_eng(), memKT[:, dc, :], ps[b][:, 0:NMEM], [PSK[b]], ["memKT"])
            for u in range(2):
                slot, wk = unit_get()
                w = slot.rearrange("p (kc n) -> p kc n", kc=8)
                for mt in range(2):
                    b = nb()
                    for kc in range(8):
                        mm(ps[b][:], hT[:, kc, mt * 128:(mt + 1) * 128], w[:, kc, :], kc == 0, kc == 7,
                           [wk, ("hT", kc)], [PSK[b]])
                    cp(evac_eng(), memV[:, mt, 2 * u:2 * u + 2, 0:256], ps[b][:].rearrange("p (h e) -> p h e", h=2),
                       [PSK[b]], ["memV"])

        def final_norm_store(t):
            for s in range(NSUB):
                act(hn[:, s, :], x_sb[:, s, :], AF.Square, [("x", s)], [("hn", s), ("ss", s)],
                    scale=1.0 / 32.0, accum=ss[:, s:s + 1])
                act(rstd[:, s:s + 1], ss[:, s:s + 1], AF.Ln, [("ss", s), "eps_t"], [("rstd", s)], bias=eps_t[:])
                act(rstd[:, s:s + 1], rstd[:, s:s + 1], AF.Exp, [("rstd", s)], [("rstd", s)], scale=-0.5)
                stt("dve", x_sb[:, s, :], x_sb[:, s, :], rstd[:, s:s + 1], gfin[:], ALU.mult, ALU.mult,
                    [("x", s), ("rstd", s), "gfin"], [("x", s)])
            for s in range(NSUB):
                dma("sp", OUT[t * TT + s * 128:t * TT + (s + 1) * 128, :], x_sb[:, s, :], [XK[s]], [("OUT", s)], f"out{s}")

        def dump(i):
            if dbg:
                dma("sp", DBG[i].rearrange("(s p) d -> p s d", p=128), x_sb[:], XK, ["DBG"], "dbg")

        if stage >= 2:
            mem_kv()
        def load_xn(t):
            for s_ in range(NSUB):
                dma("sp", xn[s_], X[t * TT + s_ * 128:t * TT + (s_ + 1) * 128, :], [], [XN[s_]], f"x{s_}")

        load_xn(0)
        norm_stats(xn, XN, NSUB, hnB, "hnB")
        norm_transposes(NSUB, 0, hnB, "hnB")
        for t in range(NT):
            S.epoch = t + 1
            if stage >= 3:
                ffn(0, pre_normed=True, resid=xn, resid_keys=XN)
            if t == 0:
                dump(0)
            if stage >= 4:
                mix(t)
            if t == 0:
                dump(1)
            if stage >= 5:
                xattn()
            if t == 0:
                dump(2)
            nxt = t + 1 < NT
            if nxt:
                load_xn(t + 1)
            if stage >= 6:
                if nxt:
                    ffn(3, hook_start=lambda: norm_stats(xn, XN, NSUB, hnB, "hnB"),
                        hook_mid=lambda: norm_transposes(NSUB, 0, hnB, "hnB", pool=(4, 5, 6, 7)))
                else:
                    ffn(3)
            if t == 0:
                dump(3)
            final_norm_store(t)
        for s_ in range(NSUB):
            S.wait_final(f"out{s_}")
        if dbg:
            S.wait_final("dbg")
        S.emit(nc)
    return nc


_CACHE = {}


def kernel(**inputs):
    n = 8
    consts = _const_tables()
    if "nc" not in _CACHE:
        _CACHE["nc"] = build(8, False)
    nc = _CACHE["nc"]
    x = np.asarray(inputs["x"], dtype=np.float32)
    mem = np.asarray(inputs["mem"], dtype=np.float32)
    shared = {}
    for nm in W_NAMES:
        shared[nm] = np.ascontiguousarray(np.asarray(inputs[nm], dtype=np.float32)[0])
    for nm in ["ffn1_norm", "mix_norm", "xattn_norm", "ffn2_norm", "mem_norm"]:
        shared[nm] = np.ascontiguousarray(np.asarray(inputs[nm], dtype=np.float32).reshape(1, D))
    shared["final_norm"] = np.ascontiguousarray(np.asarray(inputs["final_norm"], dtype=np.float32).reshape(1, D))
    shared["ret_out_norm"] = np.ascontiguousarray(np.asarray(inputs["ret_out_norm"], dtype=np.float32).reshape(1, 512))
    shared["diff_out_norm"] = np.ascontiguousarray(np.asarray(inputs["diff_out_norm"], dtype=np.float32).reshape(1, 512))
    for nm in ["diff_lq1", "diff_lk1", "diff_lq2", "diff_lk2"]:
        shared[nm] = np.ascontiguousarray(np.asarray(inputs[nm], dtype=np.float32).reshape(1, 64))
    shared.update(consts)
    in_maps = []
    for b in range(n):
        m = dict(shared)
        m["x"] = np.ascontiguousarray(x[b])
        m["mem"] = np.ascontiguousarray(mem[b])
        in_maps.append(m)
    res = run_bass_kernel_spmd(nc, in_maps, core_ids=list(range(n)))
    out = np.stack([np.asarray(r["out"], dtype=np.float32) for r in res.results], axis=0)
    return out
```

```python
import contextlib
import os
import math
import numpy as np
import ml_dtypes
import concourse.bass as bass
import concourse.mybir as mybir
from concourse.bass_utils import run_bass_kernel_spmd

F32 = mybir.dt.float32
BF16 = mybir.dt.bfloat16
ALU = mybir.AluOpType
AF = mybir.ActivationFunctionType

D = 1024
SEQ = 4096
NMEM = 256
DFF = 2816
NF = 22
EPS = 1e-6
TT = 512
NSUB = 4
RING = 4
POOL_ELT = os.environ.get("POOL_ELT", "dve")
NPRE = int(os.environ.get("NPRE", "4"))
SLOT = 4096


class _Op:
    __slots__ = ("eng", "fn", "waits", "flag", "idx", "dma_grp", "dma_cnt", "epoch", "count", "vc")


class Sched:
    ENGS = ("pe", "act", "dve", "pool", "sp")

    def __init__(self):
        self.ops = {e: [] for e in self.ENGS}
        self.state = {}
        self.vc = {e: {} for e in self.ENGS}
        self.dma_cnt = {}
        self.dma_vc = {}
        self.epoch = 0
        self.final_waits = []
        self.alias = {}

    @staticmethod
    def _join(a, b):
        for k, v in b.items():
            if a.get(k, -1) < v:
                a[k] = v

    def add(self, eng, fn, reads=(), writes=(), dma=None):
        op = _Op()
        op.eng = eng
        op.fn = fn
        op.waits = []
        op.flag = False
        op.idx = len(self.ops[eng])
        op.dma_grp = dma
        op.dma_cnt = 0
        op.epoch = self.epoch
        op.count = 0
        if dma is not None:
            self.dma_cnt[dma] = self.dma_cnt.get(dma, 0) + 16
            op.dma_cnt = self.dma_cnt[dma]
            ref = ("dma", dma, op.dma_cnt)
            rkey = "dma:" + dma
        else:
            ref = ("eng", eng, op.idx, op)
            rkey = eng
        if self.alias:
            w2 = list(writes)
            for k in writes:
                if k in self.alias:
                    w2.extend(self.alias[k])
            writes = w2
        for k in reads:
            st = self.state.get(k)
            if st is not None and st[0] is not None:
                self._need(op, st[0])
        for k in writes:
            st = self.state.get(k)
            if st is not None:
                if st[0] is not None:
                    self._need(op, st[0])
                for r in st[1].values():
                    self._need(op, r)
        for k in reads:
            st = self.state.get(k)
            if st is None:
                st = [None, {}]
                self.state[k] = st
            st[1][rkey] = ref
        for k in writes:
            self.state[k] = [ref, {}]
        vc = dict(self.vc[eng])
        if dma is not None:
            prev = self.dma_vc.get((dma, op.dma_cnt - 16))
            if prev is not None:
                self._join(vc, prev)
            vc["dma:" + dma] = op.dma_cnt
            self.dma_vc[(dma, op.dma_cnt)] = vc
        else:
            vc[eng] = op.idx
        op.vc = vc
        self.ops[eng].append(op)
        return op

    def _need(self, op, d):
        evc = self.vc[op.eng]
        if d[0] == "eng":
            X, idx, dop = d[1], d[2], d[3]
            if X == "pe" and op.eng == "pe" and op.dma_grp is None:
                return
            if evc.get(X, -1) >= idx:
                return
            dop.flag = True
            op.waits.append(("eng", X, dop))
            self._join(evc, dop.vc)
        else:
            G, cnt = d[1], d[2]
            key = "dma:" + G
            if evc.get(key, -1) >= cnt:
                return
            op.waits.append(("dma", G, cnt))
            self._join(evc, self.dma_vc[(G, cnt)])

    def seal(self, grp):
        tot = self.dma_cnt[grp]
        for k, stt_ in self.state.items():
            w = stt_[0]
            if w is not None and w[0] == "dma" and w[1] == grp:
                stt_[0] = ("dma", grp, tot)

    def wait_final(self, dma_grp):
        self.final_waits.append(dma_grp)

    def emit(self, nc):
        used = set()
        for e in self.ENGS:
            cnts = {}
            for op in self.ops[e]:
                if op.flag and op.dma_grp is None:
                    cnts[op.epoch] = cnts.get(op.epoch, 0) + 1
                    op.count = cnts[op.epoch]
                    used.add((e, op.epoch))
        with contextlib.ExitStack() as es:
            esem = {}
            for (e, ep) in sorted(used):
                esem[(e, ep)] = es.enter_context(nc.semaphore(f"s_{e}_{ep}"))
            dsem = {}
            for g in sorted(self.dma_cnt):
                dsem[g] = es.enter_context(nc.semaphore(f"d_{g}"))
            block = es.enter_context(nc.Block())

            def run(e, name):
                for op in self.ops[name]:
                    for w in op.waits:
                        if w[0] == "eng":
                            dop = w[2]
                            e.wait_ge(esem[(w[1], dop.epoch)], dop.count)
                        else:
                            e.wait_ge(dsem[w[1]], w[2])
                    ins = op.fn(e)
                    if op.dma_grp is not None:
                        ins.then_inc(dsem[op.dma_grp], 16)
                    elif op.flag:
                        ins.then_inc(esem[(name, op.epoch)], 1)
                if name == "sp":
                    for g in self.final_waits:
                        e.wait_ge(dsem[g], self.dma_cnt[g])

            @block.tensor
            def _(e):
                run(e, "pe")

            @block.scalar
            def _(e):
                run(e, "act")

            @block.vector
            def _(e):
                run(e, "dve")

            @block.gpsimd
            def _(e):
                run(e, "pool")

            @block.sync
            def _(e):
                run(e, "sp")


def _const_tables():
    f32 = np.float32
    c = {}
    c["identF"] = np.eye(128, dtype=f32)
    c["identB"] = np.eye(128, dtype=f32).astype(ml_dtypes.bfloat16)
    inv = (1.0 / (f32(10000.0) ** (np.arange(0, 64, 2, dtype=f32) / f32(64)))).astype(f32)
    pos = np.arange(SEQ, dtype=f32)
    ang = (pos[:, None] * inv[None, :]).astype(f32)
    cos = np.cos(ang).astype(f32)
    sin = np.sin(ang).astype(f32)
    p = np.arange(128)
    cs = np.zeros((128, 2, SEQ), f32)
    cs[:, 0, :] = cos[:, p % 32].T
    sgn = np.where((p % 64) < 32, -1.0, 1.0).astype(f32)
    cs[:, 1, :] = sin[:, p % 32].T * sgn[:, None]
    c["cs"] = cs
    H = 4
    log_g = np.log(1.0 - 2.0 ** (-5.0 - np.arange(H, dtype=np.float64)))
    n = np.arange(128, dtype=np.float64)
    rel = n[None, :] - n[:, None]
    decT = np.zeros((128, 2, 2, 128), f32)
    for h in range(H):
        decT[:, h % 2, h // 2, :] = np.where(rel >= 0, np.exp(np.maximum(rel, 0) * log_g[h]), 0.0) * 0.125
    c["decT"] = decT
    xi = np.exp((n[None, :] + 1.0) * log_g[:, None])
    zeta = np.exp((127.0 - n[None, :]) * log_g[:, None])
    gch = np.exp(128.0 * log_g)
    xit = np.zeros((128, 2, 128), f32)
    gdec = np.zeros((128, 2), f32)
    for hp in range(2):
        for hh in range(2):
            xit[hh * 64:(hh + 1) * 64, hp, :] = xi[2 * hp + hh][None, :] * 0.125
            gdec[hh * 64:(hh + 1) * 64, hp] = gch[2 * hp + hh]
    c["xit"] = xit
    c["gdec"] = gdec
    zt = np.zeros((128, 256), f32)
    for h in range(H):
        zt[:, h * 64:(h + 1) * 64] = zeta[h][:, None]
    c["zt"] = zt
    k = np.arange(128)
    c["mask"] = (k[:, None] <= k[None, :]).astype(f32).astype(ml_dtypes.bfloat16)
    return c


W_NAMES = ["ffn1_w_gate", "ffn1_w_up", "ffn1_w_down", "w_in", "w_out", "xattn_wq", "xattn_wkv",
           "xattn_wo", "ffn2_w_gate", "ffn2_w_up", "ffn2_w_down"]
V_NAMES = ["ffn1_norm", "mix_norm", "xattn_norm", "ffn2_norm", "mem_norm", "final_norm",
           "ret_out_norm", "diff_out_norm", "diff_lq1", "diff_lk1", "diff_lq2", "diff_lk2"]


def build(NT=8, dbg=False, stage=9):
    nc = bass.Bass("TRN2", target_bir_lowering=False)
    S = Sched()
    ntok = NT * TT

    def din(name, shape, dt=F32):
        return nc.dram_tensor(name, list(shape), dt, kind="ExternalInput").ap()

    X = din("x", [SEQ, D])
    MEM = din("mem", [NMEM, D])
    Wd_ = {}
    Wd_["ffn1_w_gate"] = din("ffn1_w_gate", [D, DFF])
    Wd_["ffn1_w_up"] = din("ffn1_w_up", [D, DFF])
    Wd_["ffn1_w_down"] = din("ffn1_w_down", [DFF, D])
    Wd_["w_in"] = din("w_in", [D, 3072])
    Wd_["w_out"] = din("w_out", [D, D])
    Wd_["xattn_wq"] = din("xattn_wq", [D, D])
    Wd_["xattn_wkv"] = din("xattn_wkv", [D, 2 * D])
    Wd_["xattn_wo"] = din("xattn_wo", [D, D])
    Wd_["ffn2_w_gate"] = din("ffn2_w_gate", [D, DFF])
    Wd_["ffn2_w_up"] = din("ffn2_w_up", [D, DFF])
    Wd_["ffn2_w_down"] = din("ffn2_w_down", [DFF, D])
    Vd = {}
    for nm in ["ffn1_norm", "mix_norm", "xattn_norm", "ffn2_norm", "mem_norm", "final_norm"]:
        Vd[nm] = din(nm, [1, D])
    Vd["ret_out_norm"] = din("ret_out_norm", [1, 512])
    Vd["diff_out_norm"] = din("diff_out_norm", [1, 512])
    for nm in ["diff_lq1", "diff_lk1", "diff_lq2", "diff_lk2"]:
        Vd[nm] = din(nm, [1, 64])
    C_identF = din("identF", [128, 128])
    C_identB = din("identB", [128, 128], BF16)
    C_cs = din("cs", [128, 2, SEQ])
    C_decT = din("decT", [128, 2, 2, 128])
    C_xit = din("xit", [128, 2, 128])
    C_gdec = din("gdec", [128, 2])
    C_zt = din("zt", [128, 256])
    C_mask = din("mask", [128, 128], BF16)
    OUT = nc.dram_tensor("out", [SEQ, D], F32, kind="ExternalOutput").ap()
    if dbg:
        DBG = nc.dram_tensor("dbg", [8, 512, D], F32, kind="ExternalOutput").ap()

    def dscratch(name, nu):
        return nc.dram_tensor(name, [nu, 128, SLOT], BF16, kind="Internal").ap()

    S_gu = [dscratch("s_gu1", 11), dscratch("s_gu2", 11)]
    S_dn = [dscratch("s_d1", 6), dscratch("s_d2", 6)]
    S_in = dscratch("s_in", 7)
    S_out = dscratch("s_out", 2)
    S_wq = dscratch("s_wq", 2)
    S_wo = dscratch("s_wo", 2)
    S_kv = dscratch("s_kv", 4)

    with contextlib.ExitStack() as es:
        def sb(name, shape, dt):
            return es.enter_context(nc.sbuf_tensor(name, list(shape), dt))

        x_sb = sb("x_sb", [128, NSUB, D], F32)
        Kc = sb("Kc", [128, 4, SEQ], BF16)
        Vc = sb("Vc", [128, SEQ // 128, 4, 130], BF16)
        wring = sb("wring", [128, RING, SLOT], BF16)
        tokb = sb("tokb", [128, NSUB, D], BF16)
        hT = sb("hT", [128, 8, TT], BF16)
        arenaA = sb("arenaA", [128, 5632], F32)
        hn = arenaA[:, 0:4096].rearrange("p (s d) -> p s d", s=NSUB)
        aA_bf = arenaA[:].bitcast(BF16)
        actT = aA_bf.rearrange("p (f t) -> p f t", f=NF)
        srg = arenaA[:, 0:2048].rearrange("p (s d) -> p s d", s=NSUB)
        rt1 = arenaA[:, 2048:3072].rearrange("p (b t) -> p b t", b=2)
        rt2 = arenaA[:, 3072:4096].rearrange("p (b t) -> p b t", b=2)
        dqT = aA_bf[:, 8192:10240].rearrange("p (h t) -> p h t", h=4)
        qT = aA_bf[:, 0:4096].rearrange("p (c t) -> p c t", c=8)
        PTx = aA_bf[:, 4096:5120].rearrange("p (m t) -> p m t", m=2)
        PTx2 = aA_bf[:, 5120:6144].rearrange("p (m t) -> p m t", m=2)
        sgpt = sb("sgpt", [128, 2 * TT], F32)
        sg = sgpt[:].rearrange("p (b t) -> p b t", b=2)
        rv_sb = sb("rv_sb", [128, NSUB, 512], BF16)
        arenaB = sb("arenaB", [128, 4096], F32)
        aB_bf = arenaB[:].bitcast(BF16)
        rqT = aB_bf[:, 0:1024].rearrange("p (c t) -> p c t", c=2)
        rkT = aB_bf[:, 1024:2048].rearrange("p (c t) -> p c t", c=2)
        qxT = aB_bf[:, 2048:3072].rearrange("p (c t) -> p c t", c=2)
        kz = aB_bf[:, 3072:4096].rearrange("p (c n) -> p c n", c=NSUB)
        inT = aB_bf[:, 4096:5120].rearrange("p (a b c n) -> p a b c n", a=2, b=2, c=2)
        Rb = aB_bf[:, 5120:6144].rearrange("p (c h e) -> p c h e", c=4, h=2)
        a0 = arenaB[:, 3072:3584].rearrange("p (b e) -> p b e", b=4)
        ad = arenaB[:, 3584:4096].rearrange("p (b e) -> p b e", b=4)
        hnB = arenaB[:].rearrange("p (s d) -> p s d", s=NSUB)
        PT = sgpt[:].bitcast(BF16).rearrange("p (b t) -> p b t", b=4)
        dqz1 = sb("dqz1", [128, 4, TT], BF16)
        Rm = sb("Rm", [128, 2, 128], F32)
        junk = sb("junk", [128, 4, 128], BF16)
        cs_sb = sb("cs_sb", [128, 2, TT], F32)
        tokb_flat = tokb[:].rearrange("p s d -> p (s d)")
        xn = [tokb_flat[:, 0:2048].bitcast(F32), tokb_flat[:, 2048:4096].bitcast(F32),
              rv_sb[:].rearrange("p s d -> p (s d)").bitcast(F32), cs_sb[:].rearrange("p a t -> p (a t)")]
        identF = sb("identF_sb", [128, 128], F32)
        identB = sb("identB_sb", [128, 128], BF16)
        decT = sb("decT_sb", [128, 2, 2, 128], F32)
        xit = sb("xit_sb", [128, 2, 128], F32)
        gdec = sb("gdec_sb", [128, 2], F32)
        zt = sb("zt_sb", [128, 256], F32)
        mask = sb("mask_sb", [128, 128], BF16)
        gret = sb("gret", [128, 512], F32)
        gdiff = sb("gdiff", [128, 512], F32)
        gfin = sb("gfin", [128, D], F32)
        gcol = sb("gcol", [128, 5, 8], F32)
        memKT = sb("memKT", [128, 8, NMEM], BF16)
        memV = sb("memV", [128, 2, 4, 258], BF16)
        lqk = sb("lqk", [128, 4, 64], F32)
        st = sb("st", [128, 64], F32)
        junkR = lqk[:].rearrange("p a b -> p (a b)").bitcast(BF16).rearrange("p (h e) -> p h e", h=4)
        eps_t = sb("eps_t", [128, 1], F32)
        neglam = sb("neglam", [128, 1], F32)
        ps = [es.enter_context(nc.psum_tensor(f"ps{i}", [128, 512], F32)) for i in range(4)]
        pp = [es.enter_context(nc.psum_tensor(f"pp{i}", [128, 1024], F32)) for i in range(2)]
        ps = ps + [pp[0][:, 0:512], pp[0][:, 512:1024], pp[1][:, 0:512], pp[1][:, 512:1024]]

        ss = st[:, 0:4]
        rstd = st[:, 4:8]
        sso = st[:, 8:12]
        rso = st[:, 12:16]
        rcp = st[:, 16:24]
        lam_s = st[:, 24:28]
        ssoR = st[:, 28:32]
        rsoR = st[:, 32:36]

        hn_keys = [("hn", s) for s in range(NSUB)]
        act_keys = [("actT", f) for f in range(NF)]
        mixA_keys = ["srg", ("rt1", 0), ("rt1", 1), ("rt2", 0), ("rt2", 1), "dqT"]
        xatA_keys = ["qT", "PTx", "PTx2"]
        fams = [hn_keys, act_keys, mixA_keys, xatA_keys]
        for fam in fams:
            others = [k2 for f2 in fams if f2 is not fam for k2 in f2]
            for k in fam:
                S.alias[k] = others
        for h_ in range(4):
            S.alias[("junkR", h_)] = ["lqk"]
        hnB_keys = [("hnB", s_) for s_ in range(NSUB)]
        tmpB_keys = ([("rqT", i) for i in range(2)] + [("rkT", i) for i in range(2)] + [("qxT", i) for i in range(2)]
                     + [("kz", i) for i in range(4)] + [("inT", i, j) for i in range(2) for j in range(2)]
                     + [("a0", i) for i in range(4)] + [("ad", i) for i in range(4)] + [("Rb", i) for i in range(4)])
        for k in hnB_keys:
            S.alias[k] = tmpB_keys
        for k in tmpB_keys:
            S.alias[k] = hnB_keys
        XN = [("xn", s_) for s_ in range(NSUB)]
        xn_al = {0: [("tokb", 0, 0), ("tokb", 0, 1), ("tokb", 1, 0), ("tokb", 1, 1)],
                 1: [("tokb", 2, 0), ("tokb", 2, 1), ("tokb", 3, 0), ("tokb", 3, 1)],
                 2: [("rv", i) for i in range(4)], 3: ["cs"]}
        for i_, ks in xn_al.items():
            S.alias[XN[i_]] = ks
            for k in ks:
                S.alias[k] = [XN[i_]]
        sg_keys = [("sg", i) for i in range(2)]
        pt_keys = [("PT", i) for i in range(4)]
        for k in sg_keys:
            S.alias[k] = pt_keys
        for k in pt_keys:
            S.alias[k] = sg_keys

        PSK = [("ps", i) for i in range(8)]

        def dma(eng, out, in_, reads, writes, grp, slow=False):
            if slow:
                S.add(eng, lambda e: e.dma_start(out=out, in_=in_, allow_slow_non_contiguous=True),
                      reads=reads, writes=writes, dma=grp)
            else:
                S.add(eng, lambda e: e.dma_start(out=out, in_=in_), reads=reads, writes=writes, dma=grp)

        def mm(out, lhsT, rhs, start, stop, reads, writes, skip=False):
            S.add("pe", lambda e: e.matmul(out, lhsT=lhsT, rhs=rhs, start=start, stop=stop,
                                           skip_group_check=skip), reads=reads, writes=writes)

        def tr(out, in_, ident, reads, writes):
            S.add("pe", lambda e: e.transpose(out=out, in_=in_, identity=ident), reads=reads, writes=writes)

        def act(out, in_, func, reads, writes, scale=1.0, bias=None, accum=None):
            def fn(e):
                kw = {}
                if bias is not None:
                    kw["bias"] = bias
                if accum is not None:
                    kw["accum_out"] = accum
                return e.activation(out=out, in_=in_, func=func, scale=scale, **kw)
            S.add("act", fn, reads=reads, writes=writes)

        def ts(eng, out, in0, s1, s2, op0, op1, reads, writes):
            if s2 is None:
                S.add(eng, lambda e: e.tensor_scalar(out=out, in0=in0, scalar1=s1, scalar2=None, op0=op0),
                      reads=reads, writes=writes)
            else:
                S.add(eng, lambda e: e.tensor_scalar(out=out, in0=in0, scalar1=s1, scalar2=s2, op0=op0, op1=op1),
                      reads=reads, writes=writes)

        def tt(eng, out, in0, in1, op, reads, writes):
            S.add(eng, lambda e: e.tensor_tensor(out=out, in0=in0, in1=in1, op=op), reads=reads, writes=writes)

        def stt(eng, out, in0, scalar, in1, op0, op1, reads, writes):
            S.add(eng, lambda e: e.scalar_tensor_tensor(out=out, in0=in0, scalar=scalar, in1=in1, op0=op0, op1=op1),
                  reads=reads, writes=writes)

        def cp(eng, out, in_, reads, writes):
            if eng == "act":
                S.add("act", lambda e: e.copy(out=out, in_=in_), reads=reads, writes=writes)
            else:
                S.add(eng, lambda e: e.tensor_copy(out=out, in_=in_), reads=reads, writes=writes)

        evac_rr = [0]

        def evac_eng():
            evac_rr[0] += 1
            return "dve" if evac_rr[0] % 2 else "act"

        def unit(sc, ui, n):
            return (sc[ui, :, 0:n], n, ("S", sc.tensor.name, ui))

        U_kv = [unit(S_kv, i, 4096) for i in range(4)]

        def units_tile():
            seq = []
            for i in range(11):
                seq.append(("gu1", i, unit(S_gu[0], i, 4096)))
            dn_nf = [8, 8, 6, 8, 8, 6]
            for i in range(6):
                seq.append(("d1", i, unit(S_dn[0], i, dn_nf[i] * 512)))
            for i in (0, 1, 4, 5, 2, 3, 6):
                seq.append(("in", i, unit(S_in, i, 4096)))
            for i in range(2):
                seq.append(("out", i, unit(S_out, i, 4096)))
            for i in range(2):
                seq.append(("wq", i, unit(S_wq, i, 4096)))
            for i in range(2):
                seq.append(("wo", i, unit(S_wo, i, 4096)))
            for i in range(11):
                seq.append(("gu2", i, unit(S_gu[1], i, 4096)))
            for i in range(6):
                seq.append(("d2", i, unit(S_dn[1], i, dn_nf[i] * 512)))
            return seq

        useq = []
        for t in range(NT):
            ut = units_tile()
            if t == 0:
                ut = ut[:17] + [("kv", i, U_kv[i]) for i in range(4)] + ut[17:]
            useq += ut
        ustate = {"next_load": 0, "next_use": 0}

        def unit_get():
            v = ustate["next_use"]
            while ustate["next_load"] < len(useq) and ustate["next_load"] <= v + RING - 2:
                u = ustate["next_load"]
                sl = u % RING
                src, n, skey = useq[u][2]
                dma("sp", wring[:, sl, 0:n], src, [skey], [("w", sl)], f"w{sl}")
                ustate["next_load"] += 1
            ustate["next_use"] += 1
            sl = v % RING
            n = useq[v][2][1]
            return wring[:, sl, 0:n], ("w", sl)

        dma("sp", identF[:], C_identF, [], ["identF"], "c0")
        dma("sp", identB[:], C_identB, [], ["identB"], "c0")
        dma("sp", decT[:], C_decT, [], ["decT"], "c0")
        dma("sp", xit[:], C_xit, [], ["xit"], "c0")
        dma("sp", gdec[:], C_gdec, [], ["gdec"], "c0")
        dma("sp", zt[:], C_zt, [], ["zt"], "c0")
        dma("sp", mask[:], C_mask, [], ["mask"], "c0")
        dma("sp", gret[:], Vd["ret_out_norm"].partition_broadcast(128), [], ["gret"], "c0")
        dma("sp", gdiff[:], Vd["diff_out_norm"].partition_broadcast(128), [], ["gdiff"], "c0")
        dma("sp", gfin[:], Vd["final_norm"].partition_broadcast(128), [], ["gfin"], "c0")
        for i, nm in enumerate(["diff_lq1", "diff_lk1", "diff_lq2", "diff_lk2"]):
            dma("sp", lqk[:, i, :], Vd[nm].partition_broadcast(128), [], ["lqk"], "c0")
        for i, nm in enumerate(["ffn1_norm", "mix_norm", "xattn_norm", "ffn2_norm", "mem_norm"]):
            src = Vd[nm].rearrange("o (kc p) -> p (o kc)", p=128)
            dma("sp", gcol[:, i, :], src, [], ["gcol"], "c0", slow=True)
        S.seal("c0")
        S.add("pool", lambda e: e.memset(eps_t[:], EPS), writes=["eps_t"])
        S.add("pool", lambda e: e.memset(Vc[:, :, :, 128:130], 1.0), writes=["Vc_ones"])
        S.add("pool", lambda e: e.memset(memV[:, :, :, 256:258], 1.0), writes=["memV_ones"])
        S.add("pool", lambda e: e.memset(Rm[:], 0.0), writes=["Rm"])
        S.add("pool", lambda e: e.memset(dqz1[0:64, :, :], 0.0), writes=["dqz1"])
        ts("dve", gdiff[:], gdiff[:], 0.8, None, ALU.mult, None, ["gdiff"], ["gdiff"])
        tt("dve", lqk[:, 0, :], lqk[:, 0, :], lqk[:, 1, :], ALU.mult, ["lqk"], ["lqk"])
        tt("dve", lqk[:, 2, :], lqk[:, 2, :], lqk[:, 3, :], ALU.mult, ["lqk"], ["lqk"])
        act(lqk[:, 1, :], lqk[:, 0, :], AF.Identity, ["lqk"], ["lqk", "lam"], accum=lam_s[:, 0:1])
        act(lqk[:, 3, :], lqk[:, 2, :], AF.Identity, ["lqk"], ["lqk", "lam"], accum=lam_s[:, 1:2])
        act(lam_s[:, 2:4], lam_s[:, 0:2], AF.Exp, ["lam"], ["lam"])
        tt("dve", neglam[:], lam_s[:, 3:4], lam_s[:, 2:3], ALU.subtract, ["lam"], ["neglam"])
        ts("dve", neglam[:], neglam[:], -0.2, None, ALU.add, None, ["neglam"], ["neglam"])

        pre_rr = [0]

        def wview(name):
            return Wd_[name].rearrange("(kc p) n -> p kc n", p=128)

        def prepass_cols(dst_unit, name, c0, ncols, off=0):
            dst, n, key = dst_unit
            d = dst[:, off:off + 8 * ncols].rearrange("p (kc n) -> p kc n", kc=8)
            g = pre_rr[0] % NPRE
            pre_rr[0] += 1
            dma("pool", d, wview(name)[:, :, c0:c0 + ncols], [], [key, ("preg", g)], f"pre{g}")

        def prepass_ffn(idx):
            g, u, dn = [("ffn1_w_gate", "ffn1_w_up", "ffn1_w_down"), ("ffn2_w_gate", "ffn2_w_up", "ffn2_w_down")][idx]
            for i in range(11):
                uu = unit(S_gu[idx], i, 4096)
                prepass_cols(uu, g, i * 256, 256, 0)
                prepass_cols(uu, u, i * 256, 256, 2048)
            wd = Wd_[dn].rearrange("(fc p) n -> p fc n", p=128)
            f0s = [0, 8, 16, 0, 8, 16]
            nfs = [8, 8, 6, 8, 8, 6]
            for i in range(6):
                h = i // 3
                dst, n, key = unit(S_dn[idx], i, nfs[i] * 512)
                d = dst.rearrange("p (f n) -> p f n", f=nfs[i])
                g = pre_rr[0] % NPRE
                pre_rr[0] += 1
                dma("pool", d, wd[:, f0s[i]:f0s[i] + nfs[i], h * 512:(h + 1) * 512], [], [key, ("preg", g)], f"pre{g}")

        PRO = int(os.environ.get("PRO", "255"))
        wtmp = arenaA[:, 0:4096].rearrange("p (g two j) -> p g two j", two=2, j=32)
        wtmp_b = hT[:].rearrange("p c t -> p (c t)").rearrange("p (g two j) -> p g two j", two=2, j=32)
        dma("sp", arenaA[:, 0:4096].rearrange("p (kc n) -> p kc n", kc=8), wview("w_in")[:, :, 0:512],
            [], hn_keys, "c1")
        cp("dve", wtmp_b[:, :, 0, :], wtmp[:, :, 1, :], hn_keys, ["hT_all"])
        cp("act", wtmp_b[:, :, 1, :], wtmp[:, :, 0, :], hn_keys, ["hT_all2"])
        urot = unit(S_in, 1, 4096)
        dma("sp", urot[0], hT[:].rearrange("p c t -> p (c t)"), ["hT_all", "hT_all2"], [urot[2]], "c2")
        if PRO & 32:
            prepass_ffn(0)
        if PRO & 4:
            for i in range(4):
                prepass_cols(unit(S_kv, i, 4096), "xattn_wkv", i * 512, 512)
        in_cols = {0: 0, 2: 1536, 3: 2048, 4: 512, 5: 1024, 6: 2560}
        for ui in ([0, 4, 5, 2, 3, 6] if PRO & 8 else []):
            prepass_cols(unit(S_in, ui, 4096), "w_in", in_cols[ui], 512)
        if PRO & 16:
            for i in range(2):
                prepass_cols(unit(S_out, i, 4096), "w_out", i * 512, 512)
            for i in range(2):
                prepass_cols(unit(S_wq, i, 4096), "xattn_wq", i * 512, 512)
            for i in range(2):
                prepass_cols(unit(S_wo, i, 4096), "xattn_wo", i * 512, 512)
            prepass_ffn(1)

        HT_KEYS = [("hT", kc) for kc in range(8)]
        for k in HT_KEYS:
            S.state[k] = [None, {"dma:c2": ("dma", "c2", S.dma_cnt["c2"])}]

        bank_rr = [0]

        def nb(pool=(0, 1, 2, 3, 4, 5, 6, 7)):
            bank_rr[0] += 1
            return pool[bank_rr[0] % len(pool)]

        def norm_stats(src, src_keys, nsub, hb, hk):
            def sq(s):
                act(hb[:, s, :], src[s], AF.Square, [src_keys[s]], [(hk, s), ("ss", s)],
                    scale=1.0 / 32.0, accum=ss[:, s:s + 1])

            def rs(s):
                act(rstd[:, s:s + 1], ss[:, s:s + 1], AF.Ln, [("ss", s), "eps_t"], [("rstd", s)], bias=eps_t[:])
                act(rstd[:, s:s + 1], rstd[:, s:s + 1], AF.Exp, [("rstd", s)], [("rstd", s)], scale=-0.5)
                ts("dve", hb[:, s, :], src[s], rstd[:, s:s + 1], None, ALU.mult, None,
                   [src_keys[s], ("rstd", s)], [(hk, s)])

            sq(0)
            for s in range(nsub):
                if s + 1 < nsub:
                    sq(s + 1)
                rs(s)

        def norm_transposes(nsub, gi, hb, hk, pool=(0, 1, 2, 3, 4, 5, 6, 7)):
            for kc in range(8):
                b = nb(pool)
                for s in range(nsub):
                    tr(ps[b][:, s * 128:(s + 1) * 128], hb[:, s, kc * 128:(kc + 1) * 128], identF[:],
                       [(hk, s), "identF"], [PSK[b]])
                eng = evac_eng()
                n = nsub * 128
                if eng == "dve":
                    ts("dve", hT[:, kc, 0:n], ps[b][:, 0:n], gcol[:, gi, kc:kc + 1], None, ALU.mult, None,
                       [PSK[b], "gcol"], [("hT", kc)])
                else:
                    act(hT[:, kc, 0:n], ps[b][:, 0:n], AF.Identity, [PSK[b], "gcol"], [("hT", kc)],
                        scale=gcol[:, gi, kc:kc + 1])

        def norm_to_hT(src, src_keys, nsub, gi, ncols_tok):
            norm_stats([src[:, s, :] for s in range(nsub)], src_keys, nsub, hn, "hn")
            norm_transposes(nsub, gi, hn, "hn")

        XK = [("x", s) for s in range(NSUB)]

        def ffn(gi, pre_normed=False, resid=None, resid_keys=None, hook_start=None, hook_mid=None):
            if not pre_normed:
                norm_to_hT(x_sb, XK, NSUB, gi, TT)
            if resid is None:
                resid = [x_sb[:, s, :] for s in range(NSUB)]
                resid_keys = XK
            gub = (0, 1, 2, 3)
            for u in range(11):
                slot, wk = unit_get()
                wg = slot[:, 0:2048].rearrange("p (kc n) -> p kc n", kc=8)
                wu = slot[:, 2048:4096].rearrange("p (kc n) -> p kc n", kc=8)
                for fc in range(2):
                    f = 2 * u + fc
                    bg = gub[(2 * f) % 4]
                    bu = gub[(2 * f + 1) % 4]
                    for kc in range(8):
                        mm(ps[bg][:], wg[:, kc, fc * 128:(fc + 1) * 128], hT[:, kc, :], kc == 0, kc == 7,
                           [wk, ("hT", kc)], [PSK[bg]])
                    for kc in range(8):
                        mm(ps[bu][:], wu[:, kc, fc * 128:(fc + 1) * 128], hT[:, kc, :], kc == 0, kc == 7,
                           [wk, ("hT", kc)], [PSK[bu]])
                    sgb = sg[:, f % 2, :]
                    act(sgb, ps[bg][:], AF.Silu, [PSK[bg]], [("sg", f % 2)])
                    tt("dve", actT[:, f, :], sgb, ps[bu][:], ALU.mult, [("sg", f % 2), PSK[bu]], [("actT", f)])
            f0s = [0, 8, 16]
            nfs = [8, 8, 6]
            if hook_start is not None:
                hook_start()
            for h in range(2):
                banks = [(4, 5, 6, 7), (0, 1, 2, 3)][h]
                for g in range(3):
                    slot, wk = unit_get()
                    wd = slot.rearrange("p (f n) -> p f n", f=nfs[g])
                    for s in range(NSUB):
                        b = banks[s]
                        for fi in range(nfs[g]):
                            f = f0s[g] + fi
                            mm(ps[b][:], actT[:, f, s * 128:(s + 1) * 128], wd[:, fi, :],
                               f == 0, f == NF - 1, [wk, ("actT", f)], [PSK[b]])
                    if h == 1 and g == 0 and hook_mid is not None:
                        hook_mid()
                for s in range(NSUB):
                    b = banks[s]
                    xs = x_sb[:, s, h * 512:(h + 1) * 512]
                    stt("dve", xs, ps[b][:], 0.5, resid[s][:, h * 512:(h + 1) * 512], ALU.mult, ALU.add,
                        [PSK[b], resid_keys[s], ("x", s)], [("x", s)])

        def tok_to_hT(src_keys, pool=(0, 1, 2, 3, 4, 5, 6, 7)):
            for kc in range(8):
                b = nb(pool)
                pbv = ps[b][:].bitcast(BF16)
                for s in range(NSUB):
                    tr(pbv[:, s * 128:(s + 1) * 128], tokb[:, s, kc * 128:(kc + 1) * 128], identB[:],
                       [src_keys[s][kc // 4], "identB"], [PSK[b]])
                cp(evac_eng(), hT[:, kc, :], pbv[:, 0:512], [PSK[b]], [("hT", kc)])

        def proj_to_x(src_tag):
            for half in range(2):
                slot, wk = unit_get()
                w = slot.rearrange("p (kc n) -> p kc n", kc=8)
                for s in range(NSUB):
                    b = nb()
                    for kc in range(8):
                        mm(ps[b][:], hT[:, kc, s * 128:(s + 1) * 128], w[:, kc, :], kc == 0, kc == 7,
                           [wk, ("hT", kc)], [PSK[b]])
                    xs = x_sb[:, s, half * 512:(half + 1) * 512]
                    tt("dve", xs, ps[b][:], xs, ALU.add, [PSK[b], ("x", s)], [("x", s)])

        TOKB = [("tokb", s) for s in range(NSUB)]
        TOKB2 = [[("tokb", s, 0), ("tokb", s, 1)] for s in range(NSUB)]

        def mix(t):
            norm_to_hT(x_sb, XK, NSUB, 1, TT)
            dma("sp", cs_sb[:], C_cs[:, :, t * TT:(t + 1) * TT], [], ["cs"], "cs")

            def fm_chunk(w, wk, c, b):
                for kc in range(8):
                    mm(ps[b][:], w[:, kc, c * 128:(c + 1) * 128], hT[:, kc, :], kc == 0, kc == 7,
                       [wk, ("hT", kc)], [PSK[b]])

            s0, k0 = unit_get()
            w0 = s0.rearrange("p (kc n) -> p kc n", kc=8)
            s1, k1 = unit_get()
            w1 = s1.rearrange("p (kc n) -> p kc n", kc=8)
            for c in range(4):
                ba = nb()
                bb = nb()
                fm_chunk(w0, k0, c, ba)
                fm_chunk(w1, k1, c, bb)
                tt("dve", rt1[:, c % 2, :], ps[ba][:], cs_sb[:, 0, :], ALU.mult, [PSK[ba], "cs"], [("rt1", c % 2)])
                tt("dve", rt2[:, c % 2, :], ps[bb][:], cs_sb[:, 1, :], ALU.mult, [PSK[bb], "cs"], [("rt2", c % 2)])
                dst = rqT[:, c, :] if c < 2 else rkT[:, c - 2, :]
                dk_ = ("rqT", c) if c < 2 else ("rkT", c - 2)
                tt(POOL_ELT, dst, rt1[:, c % 2, :], rt2[:, c % 2, :], ALU.add,
                   [("rt1", c % 2), ("rt2", c % 2)], [dk_])
                if c < 2:
                    for cc in range(NSUB):
                        tt(POOL_ELT, qxT[:, c, cc * 128:(cc + 1) * 128], rqT[:, c, cc * 128:(cc + 1) * 128],
                           xit[:, c, :], ALU.mult, [dk_, "xit"], [("qxT", c)])
            def ret_gen():
                for c in range(NSUB):
                    bK = nb()
                    pbv = ps[bK][:].bitcast(BF16)
                    for hp in range(2):
                        tr(pbv[:, hp * 128:(hp + 1) * 128], rkT[:, hp, c * 128:(c + 1) * 128], identB[:],
                           [("rkT", hp), "identB"], [PSK[bK]])
                    tt("dve", kz[:, c, :], pbv[:, 0:256], zt[:], ALU.mult, [PSK[bK], "zt"], [("kz", c)])

                    def inner(hh, RB):
                        psI = ps[RB][:, 0:256].rearrange("p (hp n) -> p hp n", hp=2)
                        pr = slice(hh * 64, (hh + 1) * 64)
                        for hp in range(2):
                            mm(psI[:, hp, :], rkT[pr, hp, c * 128:(c + 1) * 128], rqT[pr, hp, c * 128:(c + 1) * 128],
                               True, True, [("rkT", hp), ("rqT", hp)], [PSK[RB]])
                        tt("dve", inT[:, c % 2, hh, :, :], psI, decT[:, hh, :, :], ALU.mult,
                           [PSK[RB], "decT"], [("inT", c % 2, hh)])

                    inner(0, nb())
                    yield
                    inner(1, nb())
                    bU = nb()
                    psU = ps[bU][:, 0:256].rearrange("p (hp e) -> p hp e", hp=2)
                    for h in range(4):
                        hp, hh = h // 2, h % 2
                        mm(psU[hh * 64:(hh + 1) * 64, hp, :], kz[:, c, h * 64:(h + 1) * 64],
                           rv_sb[:, c, h * 128:(h + 1) * 128], True, True, [("kz", c), ("rv", c)], [PSK[bU]])
                    cp("act", Rb[:, c, :, :], Rm[:], ["Rm"], [("Rb", c)])
                    for hp in range(2):
                        stt("dve", Rm[:, hp, :], Rm[:, hp, :], gdec[:, hp:hp + 1], psU[:, hp, :], ALU.mult, ALU.add,
                            ["Rm", "gdec", PSK[bU]], ["Rm"])
                    yield
                    for hh in range(2):
                        RB = nb()
                        psO = ps[RB][:, 0:256].rearrange("p (hp e) -> p hp e", hp=2)
                        pr = slice(hh * 64, (hh + 1) * 64)
                        for hp in range(2):
                            h = 2 * hp + hh
                            mm(psO[:, hp, :], inT[:, c % 2, hh, hp, :], rv_sb[:, c, h * 128:(h + 1) * 128], True, False,
                               [("inT", c % 2, hh), ("rv", c)], [PSK[RB]])
                            mm(psO[:, hp, :], qxT[pr, hp, c * 128:(c + 1) * 128], Rb[pr, c, hp, :], False, True,
                               [("qxT", hp), ("Rb", c)], [PSK[RB]])
                        for hp in range(2):
                            col = hh * 2 + hp
                            act(junkR[:, col, :], psO[:, hp, :], AF.Square, [PSK[RB]], [("junkR", col), ("ssoR", col)],
                                scale=1.0 / math.sqrt(128.0), accum=ssoR[:, col:col + 1])
                        cs2 = slice(hh * 2, hh * 2 + 2)
                        act(rsoR[:, cs2], ssoR[:, cs2], AF.Ln, [("ssoR", hh * 2), ("ssoR", hh * 2 + 1), "eps_t"],
                            [("rsoR", hh)], bias=eps_t[:])
                        act(rsoR[:, cs2], rsoR[:, cs2], AF.Exp, [("rsoR", hh)], [("rsoR", hh)], scale=-0.5)
                        for hp in range(2):
                            h = 2 * hp + hh
                            col = hh * 2 + hp
                            stt("dve", tokb[:, c, h * 128:(h + 1) * 128], psO[:, hp, :], rsoR[:, col:col + 1],
                                srg[:, c, h * 128:(h + 1) * 128], ALU.mult, ALU.mult,
                                [PSK[RB], ("rsoR", hh), "srg"], [("tokb", c, 0)])
                    yield

            def tm_group(evac, after=None):
                slot, wk = unit_get()
                w = slot.rearrange("p (kc n) -> p kc n", kc=8)
                for s in range(NSUB):
                    b = nb()
                    for kc in range(8):
                        mm(ps[b][:], hT[:, kc, s * 128:(s + 1) * 128], w[:, kc, :], kc == 0, kc == 7,
                           [wk, ("hT", kc)], [PSK[b]])
                    evac(s, b)
                    if after is not None:
                        after()

            tm_group(lambda s, b: cp(evac_eng(), rv_sb[:, s, :], ps[b][:], [PSK[b]], [("rv", s)]))

            def ev_rg(s, b):
                act(srg[:, s, :], ps[b][:], AF.Silu, [PSK[b]], ["srg"])
                tt(POOL_ELT, srg[:, s, :], srg[:, s, :], gret[:], ALU.mult, ["srg", "gret"], ["srg"])
            tm_group(ev_rg)

            rgen = ret_gen()
            ret_left = [3 * NSUB]

            def ret_step():
                if ret_left[0] > 0:
                    next(rgen)
                    ret_left[0] -= 1

            S.add("dve", lambda e: e.memset(dqT[64:128, :, :], 0.0), writes=["dqT"])
            s2, k2 = unit_get()
            w2 = s2.rearrange("p (kc n) -> p kc n", kc=8)
            for h in range(4):
                b = nb()
                fm_chunk(w2, k2, h, b)
                cp(evac_eng(), dqT[0:64, h, :], ps[b][0:64, :], [PSK[b]], ["dqT"])
                cp(evac_eng(), dqz1[64:128, h, :], ps[b][64:128, :], [PSK[b]], ["dqz1"])
                ret_step()
            s3, k3 = unit_get()
            w3 = s3.rearrange("p (kc n) -> p kc n", kc=8)
            for h in range(4):
                b = nb()
                fm_chunk(w3, k3, h, b)
                cp(evac_eng(), Kc[:, h, t * TT:(t + 1) * TT], ps[b][:], [PSK[b]], [("Kc", h)])
                ret_step()

            def ev_dv(s, b):
                j = 4 * t + s
                cp(evac_eng(), Vc[:, j, :, 0:128], ps[b][:].rearrange("p (h e) -> p h e", h=4), [PSK[b]], [("Vc", j)])
            tm_group(ev_dv, after=ret_step)
            while ret_left[0] > 0:
                ret_step()

            SB4 = (4, 5, 6)

            accsets = [(0, 1), (2, 3)]
            pending = []
            flat = []
            rnd = 0
            nk = 4 * t + 4
            n_iter_total = 8 * (2 * t + 4)
            pair_rr = [0]
            stride = max(1, n_iter_total // (3 * NSUB + 2))
            it_count = [0]
            for h in range(4):
                for cc in range(2):
                    bA, bB = accsets[rnd % 2]
                    rnd += 1
                    accA = ps[bA][:, 0:387].rearrange("p (b e) -> p b e", e=129)
                    accB = ps[bB][:, 0:129]
                    firstAB = [True, True]
                    dqz = dqT if cc == 0 else dqz1

                    def qk(item, h=h, cc=cc, dqz=dqz):
                        kjs, pj = item
                        pr_ = pair_rr[0] % 2
                        pair_rr[0] += 1
                        item.append(pr_)
                        for i_, kj in enumerate(kjs):
                            a = max(0, kj - 4 * t)
                            nq = TT - 128 * a
                            bs = 4 + 2 * pr_ + i_
                            mm(ps[bs][:, 0:nq], Kc[:, h, kj * 128:(kj + 1) * 128], dqz[:, h, a * 128:TT], True, True,
                               [("Kc", h), "dqT", "dqz1"], [PSK[bs]])
                        if len(kjs) == 2:
                            act(PT[:, 2 * pr_:2 * pr_ + 2, :], pp[pr_][:].rearrange("p (b t) -> p b t", b=2), AF.Exp,
                                [PSK[4 + 2 * pr_], PSK[5 + 2 * pr_]], [("PT", 2 * pr_), ("PT", 2 * pr_ + 1)], scale=0.125)
                        else:
                            kj = kjs[0]
                            a = max(0, kj - 4 * t)
                            nq = TT - 128 * a
                            bs = 4 + 2 * pr_
                            pi = 2 * pr_
                            act(PT[:, pi, 0:nq], ps[bs][:, 0:nq], AF.Exp, [PSK[bs]], [("PT", pi)], scale=0.125)
                            if kj >= 4 * t:
                                tt(POOL_ELT, PT[:, pi, 0:128], PT[:, pi, 0:128], mask[:], ALU.mult,
                                   [("PT", pi), "mask"], [("PT", pi)])

                    def pv(item, h=h, accA=accA, accB=accB, bA=bA, bB=bB, firstAB=firstAB):
                        kjs, pj, pr_ = item
                        for i_, kj in enumerate(kjs):
                            a = max(0, kj - 4 * t)
                            pi = 2 * pr_ + i_
                            for bq in range(a, 4):
                                lhs = PT[:, pi, (bq - a) * 128:(bq - a + 1) * 128]
                                rhs = Vc[:, kj, h, 0:129]
                                if bq < 3:
                                    mm(accA[:, bq, :], lhs, rhs, firstAB[0], kj == 4 * t + bq,
                                       [("PT", pi), ("Vc", kj), "Vc_ones"], [PSK[bA]], skip=True)
                                    firstAB[0] = False
                                else:
                                    mm(accB, lhs, rhs, firstAB[1], kj == 4 * t + bq,
                                       [("PT", pi), ("Vc", kj), "Vc_ones"], [PSK[bB]], skip=True)
                                    firstAB[1] = False

                    def fin1(h=h, cc=cc, accA=accA, accB=accB, bA=bA, bB=bB):
                        for bq in range(4):
                            acc = accA[:, bq, :] if bq < 3 else accB
                            bk = PSK[bA] if bq < 3 else PSK[bB]
                            rc = rcp[:, bq:bq + 1]
                            S.add("dve", lambda e, acc=acc, rc=rc: e.reciprocal(out=rc, in_=acc[:, 128:129]),
                                  reads=[bk], writes=[("rcp", bq)])
                            if cc == 0:
                                ts("dve", a0[:, bq, :], acc[:, 0:128], rc, None, ALU.mult, None,
                                   [bk, ("rcp", bq)], [("a0", bq)])
                            else:
                                tt("dve", rc, rc, neglam[:], ALU.mult, [("rcp", bq), "neglam"], [("rcp", bq)])
                                stt("dve", ad[:, bq, :], acc[:, 0:128], rc, a0[:, bq, :], ALU.mult, ALU.add,
                                    [bk, ("rcp", bq), ("a0", bq)], [("ad", bq)])

                    def fin2(h=h):
                        for bq in range(4):
                            act(junk[:, bq, :], ad[:, bq, :], AF.Square, [("ad", bq)], [("junk", bq), ("sso", bq)],
                                scale=1.0 / math.sqrt(128.0), accum=sso[:, bq:bq + 1])
                        act(rso[:], sso[:], AF.Ln, [("sso", bq) for bq in range(4)] + ["eps_t"], ["rso"], bias=eps_t[:])
                        act(rso[:], rso[:], AF.Exp, ["rso"], ["rso"], scale=-0.5)

                    def fin3(h=h):
                        for bq in range(4):
                            stt("dve", tokb[:, bq, 512 + h * 128:512 + (h + 1) * 128], ad[:, bq, :], rso[:, bq:bq + 1],
                                gdiff[:, h * 128:(h + 1) * 128], ALU.mult, ALU.mult,
                                [("ad", bq), "rso", "gdiff"], [("tokb", bq, 1)])

                    items = [[[kj, kj + 1], 0] for kj in range(0, 4 * t, 2)] + [[[kj], 0] for kj in range(4 * t, nk)]
                    for ii, it_ in enumerate(items):
                        last = ii == len(items) - 1
                        flat.append((qk, pv, it_, (fin1, fin2, fin3) if (last and cc == 1) else
                                     ((fin1,) if last else ())))
            LA = 1
            for i in range(min(LA, len(flat))):
                flat[i][0](flat[i][2])
            for i in range(len(flat)):
                if i + LA < len(flat):
                    flat[i + LA][0](flat[i + LA][2])
                flat[i][1](flat[i][2])
                if pending:
                    pending.pop(0)()
                for f_ in flat[i][3]:
                    pending.append(f_)
            while pending:
                pending.pop(0)()
            if dbg and t == 0:
                dma("pool", DBG[4].rearrange("(s p) d -> p s d", p=128), tokb[:], [k for kk in TOKB2 for k in kk], ["DBG4"], "dbg")
            tok_to_hT(TOKB2, pool=(4, 5, 6, 7))
            proj_to_x("out")

        def xattn():
            norm_to_hT(x_sb, XK, NSUB, 2, TT)
            for half in range(2):
                slot, wk = unit_get()
                w = slot.rearrange("p (kc n) -> p kc n", kc=8)
                for c4 in range(4):
                    dc = half * 4 + c4
                    b = nb()
                    for kc in range(8):
                        mm(ps[b][:], w[:, kc, c4 * 128:(c4 + 1) * 128], hT[:, kc, :], kc == 0, kc == 7,
                           [wk, ("hT", kc)], [PSK[b]])
                    cp(evac_eng(), qT[:, dc, :], ps[b][:], [PSK[b]], ["qT"])
            def xs_scores(h):
                P = PTx if h % 2 == 0 else PTx2
                pk = "PTx" if h % 2 == 0 else "PTx2"
                for mt in range(2):
                    b = nb()
                    for i in range(2):
                        mm(ps[b][:], memKT[:, 2 * h + i, mt * 128:(mt + 1) * 128], qT[:, 2 * h + i, :], i == 0, i == 1,
                           ["memKT", "qT"], [PSK[b]])
                    act(P[:, mt, :], ps[b][:], AF.Exp, [PSK[b]], [pk], scale=1.0 / 16.0)

            def xs_pv(h):
                P = PTx if h % 2 == 0 else PTx2
                pk = "PTx" if h % 2 == 0 else "PTx2"
                for s in range(NSUB):
                    b = nb()
                    for mt in range(2):
                        mm(ps[b][:, 0:257], P[:, mt, s * 128:(s + 1) * 128], memV[:, mt, h, 0:257], mt == 0, mt == 1,
                           [pk, "memV", "memV_ones"], [PSK[b]])
                    rc = rcp[:, 4 + s:5 + s]
                    S.add("dve", lambda e, b=b, rc=rc: e.reciprocal(out=rc, in_=ps[b][:, 256:257]),
                          reads=[PSK[b]], writes=[("rcpx", s)])
                    ts("dve", tokb[:, s, h * 256:(h + 1) * 256], ps[b][:, 0:256], rc, None, ALU.mult, None,
                       [PSK[b], ("rcpx", s)], [("tokb", s, h // 2)])

            xs_scores(0)
            for h in range(4):
                if h + 1 < 4:
                    xs_scores(h + 1)
                xs_pv(h)
            tok_to_hT(TOKB2)
            proj_to_x("wo")

        def mem_kv():
            for s_ in range(2):
                dma("sp", xn[s_], MEM[s_ * 128:(s_ + 1) * 128, :], [], [XN[s_]], f"x{s_}")
            norm_stats([xn[0], xn[1]], XN[0:2], 2, hn, "hn")
            norm_transposes(2, 4, hn, "hn")
            for u in range(2):
                slot, wk = unit_get()
                w = slot.rearrange("p (kc n) -> p kc n", kc=8)
                for c4 in range(4):
                    dc = u * 4 + c4
                    b = nb()
                    for kc in range(8):
                        mm(ps[b][:, 0:NMEM], w[:, kc, c4 * 128:(c4 + 1) * 128], hT[:, kc, 0:NMEM], kc == 0, kc == 7,
                           [wk, ("hT", kc)], [PSK[b]])
                    cp(evac_eng(), memKT[:, dc, :], ps[b][:, 0:NMEM], [PSK[b]], ["memKT"])
            for u in range(2):
                slot, wk = unit_get()
                w = slot.rearrange("p (kc n) -> p kc n", kc=8)
                for mt in range(2):
                    b = nb()
                    for kc in range(8):
                        mm(ps[b][:], hT[:, kc, mt * 128:(mt + 1) * 128], w[:, kc, :], kc == 0, kc == 7,
                           [wk, ("hT", kc)], [PSK[b]])
                    cp(evac_eng(), memV[:, mt, 2 * u:2 * u + 2, 0:256], ps[b][:].rearrange("p (h e) -> p h e", h=2),
                       [PSK[b]], ["memV"])

        def final_norm_store(t):
            for s in range(NSUB):
                act(hn[:, s, :], x_sb[:, s, :], AF.Square, [("x", s)], [("hn", s), ("ss", s)],
                    scale=1.0 / 32.0, accum=ss[:, s:s + 1])
                act(rstd[:, s:s + 1], ss[:, s:s + 1], AF.Ln, [("ss", s), "eps_t"], [("rstd", s)], bias=eps_t[:])
                act(rstd[:, s:s + 1], rstd[:, s:s + 1], AF.Exp, [("rstd", s)], [("rstd", s)], scale=-0.5)
                stt("dve", x_sb[:, s, :], x_sb[:, s, :], rstd[:, s:s + 1], gfin[:], ALU.mult, ALU.mult,
                    [("x", s), ("rstd", s), "gfin"], [("x", s)])
            for s in range(NSUB):
                dma("sp", OUT[t * TT + s * 128:t * TT + (s + 1) * 128, :], x_sb[:, s, :], [XK[s]], [("OUT", s)], f"out{s}")

        def dump(i):
            if dbg:
                dma("sp", DBG[i].rearrange("(s p) d -> p s d", p=128), x_sb[:], XK, ["DBG"], "dbg")

        def load_xn(t):
            for s_ in range(NSUB):
                dma("sp", xn[s_], X[t * TT + s_ * 128:t * TT + (s_ + 1) * 128, :], [], [XN[s_]], f"x{s_}")

        load_xn(0)
        norm_stats(xn, XN, NSUB, hnB, "hnB")
        norm_transposes(NSUB, 0, hnB, "hnB")
        for t in range(NT):
            S.epoch = t + 1
            if stage >= 3:
                ffn(0, pre_normed=True, resid=xn, resid_keys=XN)
            if t == 0:
                mem_kv()
            if t == 0:
                dump(0)
            if stage >= 4:
                mix(t)
            if t == 0:
                dump(1)
            if stage >= 5:
                xattn()
            if t == 0:
                dump(2)
            nxt = t + 1 < NT
            if nxt:
                load_xn(t + 1)
            if stage >= 6:
                if nxt:
                    ffn(3, hook_start=lambda: norm_stats(xn, XN, NSUB, hnB, "hnB"),
                        hook_mid=lambda: norm_transposes(NSUB, 0, hnB, "hnB", pool=(4, 5, 6, 7)))
                else:
                    ffn(3)
            if t == 0:
                dump(3)
            final_norm_store(t)
        for s_ in range(NSUB):
            S.wait_final(f"out{s_}")
        if dbg:
            S.wait_final("dbg")
        S.emit(nc)
    return nc


_CACHE = {}


def kernel(**inputs):
    n = 8
    consts = _const_tables()
    if "nc" not in _CACHE:
        _CACHE["nc"] = build(8, False)
    nc = _CACHE["nc"]
    x = np.asarray(inputs["x"], dtype=np.float32)
    mem = np.asarray(inputs["mem"], dtype=np.float32)
    shared = {}
    for nm in W_NAMES:
        shared[nm] = np.ascontiguousarray(np.asarray(inputs[nm], dtype=np.float32)[0])
    for nm in ["ffn1_norm", "mix_norm", "xattn_norm", "ffn2_norm", "mem_norm"]:
        shared[nm] = np.ascontiguousarray(np.asarray(inputs[nm], dtype=np.float32).reshape(1, D))
    shared["final_norm"] = np.ascontiguousarray(np.asarray(inputs["final_norm"], dtype=np.float32).reshape(1, D))
    shared["ret_out_norm"] = np.ascontiguousarray(np.asarray(inputs["ret_out_norm"], dtype=np.float32).reshape(1, 512))
    shared["diff_out_norm"] = np.ascontiguousarray(np.asarray(inputs["diff_out_norm"], dtype=np.float32).reshape(1, 512))
    for nm in ["diff_lq1", "diff_lk1", "diff_lq2", "diff_lk2"]:
        shared[nm] = np.ascontiguousarray(np.asarray(inputs[nm], dtype=np.float32).reshape(1, 64))
    shared.update(consts)
    in_maps = []
    for b in range(n):
        m = dict(shared)
        m["x"] = np.ascontiguousarray(x[b])
        m["mem"] = np.ascontiguousarray(mem[b])
        in_maps.append(m)
    res = run_bass_kernel_spmd(nc, in_maps, core_ids=list(range(n)))
    out = np.stack([np.asarray(r["out"], dtype=np.float32) for r in res.results], axis=0)
    return out
```
